# Optimizing a Trainium2 kernel written in Bass

```python
import math
import jax, jax.numpy as jnp
from jax import lax
import numpy as np

D_MODEL = 1024
BATCH = 8
SEQ = 2048
DEPTH = 1
DEC_BATCH = 128
DEC_SEQ = 4
PAST_LEN = 16384
PAGE_SIZE = 128

D_MIX = D_MODEL
D_GLA = D_MIX // 2
D_CONV = D_MIX - D_GLA
GLA_HEADS = 4
GLA_DV = D_GLA // GLA_HEADS
GLA_DK = GLA_DV // 2
D_GLA_K = GLA_HEADS * GLA_DK
GATE_RANK = 16
GATE_NORMALIZER = 16.0
GLA_CHUNK = 64
CONV_WIDTH = 3
NORM_EPS = 1e-6

SPLIT_SIZES = [D_GLA_K, D_GLA_K, D_GLA, D_GLA, GATE_RANK, D_CONV, D_CONV, D_CONV, D_CONV]
N_IN = sum(SPLIT_SIZES)
SPLIT_IDX = np.cumsum(SPLIT_SIZES)[:-1].tolist()

kernel_name = "hymba_gla_shortconv_decode_step"


def rmsnorm(x, g):
    xf = x.astype(jnp.float32)
    y = xf * lax.rsqrt(jnp.mean(xf * xf, axis=-1, keepdims=True) + NORM_EPS)
    return (y * g.astype(jnp.float32)).astype(x.dtype)


def gla_chunked(q, k, v, gk, S0):
    B, L, H, DK = q.shape
    DV = v.shape[-1]
    C = math.gcd(L, GLA_CHUNK)
    N = L // C

    def to_chunks(t):
        return t.astype(jnp.float32).reshape(B, N, C, H, t.shape[-1]).transpose(1, 0, 3, 2, 4)

    mask = jnp.tril(jnp.ones((C, C), dtype=bool))

    def step(S, inp):
        qc, kc, vc, gc = inp
        b = jnp.cumsum(gc, axis=2)
        inter = jnp.einsum('bhtk,bhkv->bhtv', qc * jnp.exp(b), S)
        rel = b[:, :, :, None, :] - b[:, :, None, :, :]
        decay = jnp.exp(jnp.where(mask[:, :, None], rel, -jnp.inf))
        A = jnp.einsum('bhtk,bhtsk,bhsk->bhts', qc, decay, kc)
        intra = jnp.einsum('bhts,bhsv->bhtv', A, vc)
        b_last = b[:, :, -1:, :]
        S_new = jnp.exp(b_last[:, :, 0, :])[..., None] * S + jnp.einsum(
            'bhsk,bhsv->bhkv', kc * jnp.exp(b_last - b), vc)
        return S_new, inter + intra

    S_fin, o = lax.scan(step, S0.astype(jnp.float32),
                        (to_chunks(q), to_chunks(k), to_chunks(v), to_chunks(gk)))
    o = o.transpose(1, 0, 3, 2, 4).reshape(B, L, H, DV)
    return o, S_fin


def mixer_layer(x, S0, conv0, norm_gain, w_in, w_gk_up, b_gk, gla_norm_gain, conv_w, w_out):
    B, L, _ = x.shape
    h = rmsnorm(x, norm_gain)
    proj = jnp.einsum('bld,dn->bln', h, w_in)
    q, k, v, g_gla, gk_lr, u, b_gate, c_gate, g_conv = jnp.split(proj, SPLIT_IDX, axis=-1)

    q = q.reshape(B, L, GLA_HEADS, GLA_DK) * (GLA_DK ** -0.5)
    k = k.reshape(B, L, GLA_HEADS, GLA_DK)
    v = v.reshape(B, L, GLA_HEADS, GLA_DV)
    gk = jax.nn.log_sigmoid(
        (jnp.einsum('blr,rk->blk', gk_lr, w_gk_up) + b_gk).astype(jnp.float32)) / GATE_NORMALIZER
    gk = gk.reshape(B, L, GLA_HEADS, GLA_DK)
    o, S_new = gla_chunked(q, k, v, gk, S0)
    o = rmsnorm(o, gla_norm_gain).reshape(B, L, D_GLA).astype(x.dtype)
    o = o * jax.nn.silu(g_gla)

    hc = c_gate * u
    hp = jnp.concatenate([conv0.astype(hc.dtype), hc], axis=1)
    yc = sum(conv_w[j] * hp[:, j:j + L] for j in range(CONV_WIDTH))
    conv_new = hp[:, -(CONV_WIDTH - 1):]
    yc = b_gate * yc * jax.nn.silu(g_conv)

    mix = jnp.concatenate([o, yc], axis=-1)
    out = x + jnp.einsum('blm,md->bld', mix, w_out)
    return out, S_new.astype(S0.dtype), conv_new.astype(conv0.dtype)


def setup_inputs(seed: int = 0) -> dict:
    key = jax.random.key(seed)
    ks = jax.random.split(key, 12)
    f32 = jnp.float32
    return {
        "x_prompt": jax.random.normal(ks[0], (BATCH, SEQ, D_MODEL), f32),
        "x_sample": jax.random.normal(ks[1], (DEC_BATCH, DEC_SEQ, D_MODEL), f32),
        "state_gla": 0.3 * jax.random.normal(ks[2], (DEPTH, DEC_BATCH, GLA_HEADS, GLA_DK, GLA_DV), f32),
        "state_conv": jax.random.normal(ks[3], (DEPTH, DEC_BATCH, CONV_WIDTH - 1, D_CONV), f32),
        "norm_gain": 1.0 + 0.01 * jax.random.normal(ks[4], (DEPTH, D_MODEL), f32),
        "w_in": jax.random.normal(ks[5], (DEPTH, D_MODEL, N_IN), f32) * D_MODEL ** -0.5,
        "w_gk_up": jax.random.normal(ks[6], (DEPTH, GATE_RANK, D_GLA_K), f32) * GATE_RANK ** -0.5,
        "b_gk": 0.1 * jax.random.normal(ks[7], (DEPTH, D_GLA_K), f32),
        "gla_norm_gain": 1.0 + 0.01 * jax.random.normal(ks[8], (DEPTH, GLA_DV), f32),
        "conv_w": jax.random.normal(ks[9], (DEPTH, CONV_WIDTH, D_CONV), f32) * CONV_WIDTH ** -0.5,
        "w_out": jax.random.normal(ks[10], (DEPTH, D_MIX, D_MODEL), f32) * D_MIX ** -0.5,
        "final_norm_gain": 1.0 + 0.01 * jax.random.normal(ks[11], (D_MODEL,), f32),
    }


def reference(x_prompt, x_sample, state_gla, state_conv, norm_gain, w_in, w_gk_up, b_gk,
              gla_norm_gain, conv_w, w_out, final_norm_gain):
    hp, hs = x_prompt, x_sample
    Bp = x_prompt.shape[0]
    gla_p, conv_p, gla_s, conv_s = [], [], [], []
    for l in range(DEPTH):
        params = (norm_gain[l], w_in[l], w_gk_up[l], b_gk[l], gla_norm_gain[l], conv_w[l], w_out[l])
        S0p = jnp.zeros((Bp, GLA_HEADS, GLA_DK, GLA_DV), state_gla.dtype)
        c0p = jnp.zeros((Bp, CONV_WIDTH - 1, D_CONV), state_conv.dtype)
        hp, Sp, cp = mixer_layer(hp, S0p, c0p, *params)
        hs, Ss, cs = mixer_layer(hs, state_gla[l], state_conv[l], *params)
        gla_p.append(Sp)
        conv_p.append(cp)
        gla_s.append(Ss)
        conv_s.append(cs)
    y_prompt = rmsnorm(hp, final_norm_gain)
    y_sample = rmsnorm(hs, final_norm_gain)
    return (y_prompt, y_sample, jnp.stack(gla_p), jnp.stack(conv_p), jnp.stack(gla_s), jnp.stack(conv_s))
```

```python
import math
from contextlib import ExitStack

import numpy as np
import concourse.bass as bass
import concourse.mybir as mybir
from concourse.bass_utils import run_bass_kernel_spmd

F32 = mybir.dt.float32
BF16 = mybir.dt.bfloat16
AF = mybir.ActivationFunctionType
ALU = mybir.AluOpType

NCORES = 8
D = 1024
KC = 8
SEQ = 2048
NSEQ_S = 16
LS = 4
NIN = 3600
NCH = 29
EPS = 1e-6
LN_QS = math.log(0.125)

C_LR = 0
C_Q = 1
C_K = 3
C_V = 5
C_G = 9
C_CONV = 13

CO_ID = 0
CO_RMP = 128
CO_RMS = 640
CO_SEL = 768
CW = 800
C2_ONES = 0
C2_MP = 128
C2_MS = 256
C2_WGK = 384
CW2 = 640
PO_G = 0
PO_BGK = 8
PO_GG = 10
PO_CW = 11
PW = 23


class _Res:
    __slots__ = ("name", "w", "r", "dsem", "dcnt", "excl")

    def __init__(self, name, excl=False):
        self.name = name
        self.w = None
        self.r = {}
        self.dsem = None
        self.dcnt = 0
        self.excl = excl


class _Eng:
    def __init__(self, name):
        self.name = name
        self.ops = []
        self.cnt = 0
        self.sem = None
        self.seen = {}


class _Plan:
    def __init__(self, nc, es):
        self.nc = nc
        self.es = es
        self.eng = {n: _Eng(n) for n in ("pe", "act", "dve", "pool", "sp")}
        for n, e in self.eng.items():
            e.sem = es.enter_context(nc.semaphore("s_" + n))
        self.nsem = 0
        self.out_events = []

    def res(self, name, excl=False):
        return _Res(name, excl)

    def _deps(self, eng, reads, writes):
        E = self.eng[eng]
        deps = []
        for r in reads:
            if r.w is not None:
                deps.append(r.w)
            if r.excl:
                deps.extend(ev for ev in r.r.values() if ev[2] != eng)
        for w in writes:
            if w.w is not None:
                deps.append(w.w)
            deps.extend(w.r.values())
        waits = []
        for (sem, val, src) in deps:
            if src == "pe" and eng == "pe":
                continue
            k = id(sem)
            if E.seen.get(k, 0) >= val:
                continue
            E.seen[k] = val
            waits.append((sem, val))
        return waits

    def _commit(self, ev, reads, writes):
        for r in reads:
            k = id(ev[0])
            old = r.r.get(k)
            if old is None or old[1] < ev[1]:
                r.r[k] = ev
        for w in writes:
            w.w = ev
            w.r = {}

    def op(self, eng, fn, reads=(), writes=()):
        E = self.eng[eng]
        waits = self._deps(eng, reads, writes)
        E.cnt += 1
        ev = (E.sem, E.cnt, eng)
        E.ops.append((waits, fn, (E.sem, 1), False))
        self._commit(ev, reads, writes)
        return ev

    def dma(self, fn, buf, reads=(), writes=(), n=1, is_output=False, queue="sp"):
        E = self.eng[queue]
        if buf.dsem is None:
            self.nsem += 1
            buf.dsem = self.es.enter_context(self.nc.semaphore("d%d_%s" % (self.nsem, buf.name)))
        waits = self._deps(queue, reads, writes)
        buf.dcnt += 16 * n
        ev = (buf.dsem, buf.dcnt, "dma")
        E.ops.append((waits, fn, (buf.dsem, 16), True))
        self._commit(ev, reads, writes)
        if is_output:
            self.out_events.append(ev)
        return ev

    def replay(self, eng, e):
        E = self.eng[eng]
        for waits, fn, inc, inc_all in E.ops:
            for sem, val in waits:
                e.wait_ge(sem, val)
            ins = fn(e)
            if not isinstance(ins, (list, tuple)):
                ins = [ins]
            if inc_all:
                for i in ins:
                    i.then_inc(inc[0], inc[1])
            else:
                ins[-1].then_inc(inc[0], inc[1])
        if eng == "sp":
            last = {}
            for (sem, val, _) in self.out_events:
                k = id(sem)
                if k not in last or last[k][1] < val:
                    last[k] = (sem, val)
            for sem, val in last.values():
                e.wait_ge(sem, val)


def build_program():
    nc = bass.Bass("TRN2", target_bir_lowering=False)
    NTOK_IN = SEQ + 128
    NTOK_OUT = SEQ + NSEQ_S * LS
    x_all = nc.dram_tensor("x_all", [NTOK_IN, D], F32, kind="ExternalInput").ap()
    gla_in = nc.dram_tensor("gla_in", [NSEQ_S, 4, 64, 128], F32, kind="ExternalInput").ap()
    conv_in = nc.dram_tensor("conv_in", [NSEQ_S * 2, 512], F32, kind="ExternalInput").ap()
    w_in_d = nc.dram_tensor("w_in_l", [128, NCH, KC * 128], F32, kind="ExternalInput").ap()
    w_v_d = nc.dram_tensor("w_v_l", [128, KC * 512], F32, kind="ExternalInput").ap()
    w_out_d = nc.dram_tensor("w_out_l", [128, 2, KC * 512], F32, kind="ExternalInput").ap()
    cst_d = nc.dram_tensor("cst", [128, CW], F32, kind="ExternalInput").ap()
    cst2_d = nc.dram_tensor("cst2", [128, CW2], F32, kind="ExternalInput").ap()
    prm_d = nc.dram_tensor("prm", [128, PW], F32, kind="ExternalInput").ap()
    fg_d = nc.dram_tensor("fg", [128, D], F32, kind="ExternalInput").ap()
    y_d = nc.dram_tensor("y", [NTOK_OUT, D], F32, kind="ExternalOutput").ap()
    gla_p_d = nc.dram_tensor("gla_p", [4, 64, 128], F32, kind="ExternalOutput").ap()
    conv_p_d = nc.dram_tensor("conv_p", [2, 512], F32, kind="ExternalOutput").ap()
    gla_s_d = nc.dram_tensor("gla_s", [NSEQ_S, 4, 64, 128], F32, kind="ExternalOutput").ap()
    conv_s_d = nc.dram_tensor("conv_s", [NSEQ_S * 2, 512], F32, kind="ExternalOutput").ap()

    with ExitStack() as es:
        es.enter_context(nc.allow_low_precision("bf16 matmul operands, fp32 accumulation"))
        P = _Plan(nc, es)

        def sb(name, shape, dt):
            return es.enter_context(nc.sbuf_tensor(name, shape, dt))

        def ps(name, shape, dt):
            return es.enter_context(nc.psum_tensor(name, shape, dt))

        NSLOT = NCH - 4
        w_in_bf = sb("w_in_bf", [128, NSLOT, KC, 128], BF16)
        w_v_bf = sb("w_v_bf", [128, KC, 512], BF16)
        R_win = [P.res("win%d" % c) for c in range(NCH)]
        w_out_bf = sb("w_out_bf", [128, 2, KC, 512], BF16)
        R_wout = [P.res("wout%d" % c) for c in range(2)]

        def wslot(c):
            return c if c < C_V else c - 4
        class _Pool:
            def __init__(self, name, n, dt=F32):
                self.t = [sb("%s%d" % (name, i), [128, 1024], dt) for i in range(n)]
                self.R = [P.res("%s%d" % (name, i)) for i in range(n)]
                self.busy = [False] * n
                self.i = 0
                self.flush = {}

            def get(self):
                n = len(self.t)
                for k in range(n):
                    j = (self.i + k) % n
                    if not self.busy[j]:
                        break
                else:
                    assert self.flush, "pool exhausted"
                    j = next(iter(self.flush))
                    self.flush.pop(j)()
                self.i = j + 1
                self.busy[j] = True
                return self.t[j], self.R[j], j

            def release(self, j):
                self.busy[j] = False

            def nfree(self):
                return sum(1 for x in self.busy if not x)

            def defer(self, j, fn):
                def run():
                    fn()
                    self.busy[j] = False
                self.flush[j] = run

            def flush_all(self):
                for j in list(self.flush):
                    self.flush.pop(j)()

        poolA = _Pool("pA", 3)
        poolB = _Pool("pB", 3)

        poolX = _Pool("xn", 5, BF16)

        cst_f = sb("cst_f", [128, CW], F32); R_cst = P.res("cst")
        prm_f = sb("prm_f", [128, PW], F32); R_prm = P.res("prm")
        fg_t = sb("fg_t", [128, D], F32); R_fg = P.res("fg")
        R_id = P.res("ident")
        cbf = sb("cbf", [128, 512 + 256], BF16); R_cbf = P.res("cbf"); R_cbfi = P.res("cbfi")
        nbgk = sb("nbgk", [128, 2], F32); R_nbgk = P.res("nbgk")
        st1 = sb("st1", [128, 4, 4], F32); R_st1 = [P.res("st1_%d" % i) for i in range(4)]
        st1_i = [0]
        st2 = sb("st2", [128, 8], F32); R_st2 = P.res("st2")
        hTb = [sb("hT%d" % i, [128, KC, 512], BF16) for i in range(2)]
        R_hTb = [[P.res("hT%d_%d" % (i, t)) for t in range(4)] for i in range(2)]
        o_sb = sb("o_sb", [128, 4, 128], F32); R_osb = P.res("o_sb")
        qp = [[[sb("qp%d%d%d" % (b, p, h), [128, 512], BF16) for h in range(2)] for p in range(2)] for b in range(1)]
        R_qp = [[[P.res("qp%d%d%d" % (b, p, h)) for h in range(2)] for p in range(2)] for b in range(1)]
        kT = [sb("kT%d" % b, [128, 2, 512], BF16) for b in range(1)]
        R_kT = [[P.res("kT%d%d" % (b, p)) for p in range(2)] for b in range(1)]
        vt = [sb("vt%d" % b, [128, 4, 512], BF16) for b in range(1)]
        R_vt = [[P.res("vt%d%d" % (b, t)) for t in range(4)] for b in range(1)]
        sg = [sb("sg%d" % b, [128, 4, 512], BF16) for b in range(1)]
        R_sg = [[P.res("sg%d%d" % (b, h)) for h in range(4)] for b in range(1)]
        el_p = [sb("el_p%d" % b, [128, 2, 4], F32) for b in range(1)]
        R_elp = [P.res("elp%d" % b) for b in range(1)]
        mixC = [sb("mixC%d" % b, [128, 4, 512], BF16) for b in range(1)]
        R_mixC = [[P.res("mixC%d%d" % (b, j)) for j in range(4)] for b in range(1)]
        mixG = sb("mixG", [128, 4, 512], BF16)
        R_mixG = [P.res("mixG%d" % t) for t in range(4)]
        kt = sb("kt", [128, 4, 256], BF16); R_kt = P.res("kt")
        el_s = sb("el_s", [128, 32, 2], F32); R_els = P.res("els")
        lr_bf = sb("lr_bf", [128, 512], BF16); R_lr = P.res("lr")
        e1b = [sb("e1_%d" % i, [128, 512], F32) for i in range(2)]; R_e1b = [P.res("e1_0"), P.res("e1_1")]
        Bcb = [sb("Bc%d" % i, [128, 512], F32) for i in range(2)]; R_Bcb = [P.res("Bc0"), P.res("Bc1")]
        ebq = [sb("ebq%d" % p, [128, 512], F32) for p in range(2)]; R_ebq = [P.res("ebq0"), P.res("ebq1")]
        enb = [sb("enb%d" % p, [128, 512], F32) for p in range(2)]; R_enb = [P.res("enb0"), P.res("enb1")]
        u_sb = sb("u_sb", [128, 512], F32); R_u = P.res("u")
        hp_p = sb("hp_p", [128, 4, 514], F32); R_hpp = [P.res("hpp%d" % j) for j in range(4)]
        hp_s = sb("hp_s", [128, 4, 32, 6], F32); R_hps = [P.res("hps%d" % j) for j in range(4)]
        tA = sb("tA", [128, 512], F32); R_tA = P.res("tA")
        tB = sb("tB", [128, 512], F32); R_tB = P.res("tB")
        sgt = sb("sgt", [128, 512], F32); R_sgt = P.res("sgt")
        ATm = [sb("ATm%d" % i, [128, 4, 128], BF16) for i in range(2)]; R_ATm = [P.res("ATm0"), P.res("ATm1")]
        S = sb("S", [128, 2, 128], F32); R_S = P.res("S")
        SUe = [sb("SUe%d" % i, [128, 2, 128], F32) for i in range(2)]; R_SUe = [P.res("SUe0"), P.res("SUe1")]
        S_bf = [sb("S_bf%d" % i, [128, 2, 128], BF16) for i in range(2)]; R_Sbf = [P.res("Sbf0"), P.res("Sbf1")]
        sq = sb("sq", [128, 512], BF16); R_sq = P.res("sq")
        rstd = sb("rstd", [128, 4, 128], F32); R_rstd = P.res("rstd")

        NBIG = 3
        big = [ps("big%d" % i, [128, 512], F32) for i in range(NBIG)]
        R_big = [P.res("big%d" % i, True) for i in range(NBIG)]
        big_i = [0]

        def big_next():
            i = big_i[0] % NBIG
            big_i[0] += 1
            return big[i], R_big[i]

        NSM = 2
        sml = [ps("sml%d" % i, [128, 4, 128], F32) for i in range(NSM)]
        R_sml = [P.res("sml%d" % i, True) for i in range(NSM)]
        sml_i = [0]

        def sml_next():
            i = sml_i[0] % NSM
            sml_i[0] += 1
            return sml[i], R_sml[i]

        TRb = [ps("TR%d" % i, [128, 8, 128], BF16) for i in range(2)]; R_TR = [P.res("TR0", True), P.res("TR1", True)]
        tr_i = [0]
        OTb = [ps("OT0", [128, 4, 128], F32)] * 2; R_OTb = [P.res("OT0", True)] * 2
        ot_i = [0]

        ident_bf = cbf[:, 0:128]
        ones_bf = cbf[:, 128:256]
        maskP_bf = cbf[:, 256:384]
        maskS_bf = cbf[:, 384:512]
        wgk_bf = cbf[:, 512:768]
        ident_f = cst_f[:, CO_ID:CO_ID + 128]

        g_b = prm_f[:, PO_G:PO_G + 8].unsqueeze(2).broadcast_to([128, 8, 128])
        gg = prm_f[:, PO_GG:PO_GG + 1]
        fg = fg_t[:]

        use_order = [C_LR, "v"] + list(range(C_G, C_G + 4)) + [C_Q, C_Q + 1, C_K, C_K + 1] + list(range(C_CONV, NCH)) + ["o0", "o1"]
        w_next = [0]

        def issue_weights(n):
            while n > 0 and w_next[0] < len(use_order):
                c = use_order[w_next[0]]
                w_next[0] += 1
                n -= 1
                if c == "v":
                    P.dma(lambda e: e.dma_start(out=w_v_bf[:], in_=w_v_d.rearrange("p (k n) -> p k n", k=KC)), R_win[C_V],
                          writes=R_win[C_V:C_V + 4], queue="pool")
                elif c in ("o0", "o1"):
                    hf = int(c[1])
                    P.dma(lambda e, hf=hf: e.dma_start(out=w_out_bf[:, hf, :, :], in_=w_out_d[:, hf, :].rearrange("p (k n) -> p k n", k=KC)),
                          R_wout[hf], writes=[R_wout[hf]], queue="pool")
                else:
                    P.dma(lambda e, c=c: e.dma_start(out=w_in_bf[:, wslot(c), :, :].rearrange("p k n -> p (k n)"), in_=w_in_d[:, c, :]),
                          R_win[c], writes=[R_win[c]], queue="pool")

        def need_w(kind, c):
            pass

        xready = {}

        xdma = {}

        def x_dma(tok0, t):
            xb, Rx, xj = poolA.get()
            r0 = tok0 + t * 128
            P.dma(lambda e: e.dma_start(out=xb[:], in_=x_all[r0:r0 + 128, :]), Rx, writes=[Rx])
            xdma[(tok0, t)] = (xb, Rx, xj)

        def x_load(tok0, t):
            if (tok0, t) not in xdma:
                x_dma(tok0, t)
            xb, Rx, xj = xdma.pop((tok0, t))
            xs, Rxs, sj = poolX.get()
            k = st1_i[0] % 4
            st1_i[0] += 1
            sk, Rk = st1[:, k, :], R_st1[k]
            P.op("act", lambda e: e.activation(out=xs[:], in_=xb[:], func=AF.Square, accum_out=sk[:, 0:1]),
                 reads=[Rx], writes=[Rxs, Rk])
            P.op("act", lambda e: e.activation(out=sk[:, 1:2], in_=sk[:, 0:1], func=AF.Ln, scale=1.0 / D, bias=EPS),
                 reads=[Rk], writes=[Rk])
            P.op("act", lambda e: e.activation(out=sk[:, 2:3], in_=sk[:, 1:2], func=AF.Exp, scale=-0.5),
                 reads=[Rk], writes=[Rk])
            P.op("pool", lambda e: e.tensor_scalar(out=xs[:], in0=xb[:], scalar1=sk[:, 2:3], scalar2=0.0,
                                                   op0=ALU.mult, op1=ALU.add),
                 reads=[Rx, Rk], writes=[Rxs])
            poolA.release(xj)
            xready[(tok0, t)] = (xs, Rxs, sj)

        xdone = set()

        def x_tr(tok0, t, hb):
            if (tok0, t) in xdone:
                return
            xdone.add((tok0, t))
            if (tok0, t) not in xready:
                x_load(tok0, t)
            xs, Rxs, sj = xready.pop((tok0, t))
            ti = tr_i[0] % 2
            tr_i[0] += 1
            P.op("pe", lambda e: [e.transpose(out=TRb[ti][:, kc, :], in_=xs[:, kc * 128:(kc + 1) * 128], identity=ident_bf)
                                  for kc in range(KC)],
                 reads=[Rxs, R_cbfi], writes=[R_TR[ti]])
            P.op("dve", lambda e: e.tensor_tensor(out=hTb[hb][:, :, t * 128:(t + 1) * 128], in0=TRb[ti][:], in1=g_b, op=ALU.mult),
                 reads=[R_TR[ti], R_prm], writes=[R_hTb[hb][t]])
            poolX.release(sj)

        def x_prep(tok0, NT, hb):
            for t in range(NT):
                if (tok0, t) not in xdone:
                    x_tr(tok0, t, hb)
                    yield

        cur_hb = [0]

        def inproj_fm(c, T, NT):
            hT, R_hT = hTb[cur_hb[0]], R_hTb[cur_hb[0]]
            bank, Rb = big_next()
            P.op("pe", lambda e: [e.matmul(out=bank[:, 0:T], lhsT=w_in_bf[:, wslot(c), kc, :], rhs=hT[:, kc, 0:T],
                                            start=(kc == 0), stop=(kc == KC - 1)) for kc in range(KC)],
                 reads=[R_win[c]] + R_hT[:NT], writes=[Rb])
            return bank, Rb

        def silu_from_psum(bank, Rb, T, out_ap, R_out):
            P.op("act", lambda e: e.activation(out=sgt[:, 0:T], in_=bank[:, 0:T], func=AF.Exp, scale=-1.0), reads=[Rb], writes=[R_sgt])
            P.op("act", lambda e: e.activation(out=sgt[:, 0:T], in_=sgt[:, 0:T], func=AF.Ln, bias=1.0), reads=[R_sgt], writes=[R_sgt])
            P.op("act", lambda e: e.activation(out=sgt[:, 0:T], in_=sgt[:, 0:T], func=AF.Exp, scale=-1.0), reads=[R_sgt], writes=[R_sgt])
            P.op("dve", lambda e: e.tensor_tensor(out=out_ap, in0=bank[:, 0:T], in1=sgt[:, 0:T], op=ALU.mult),
                 reads=[Rb, R_sgt], writes=[R_out])

        def kt_transposes(b, NT):
            ti = tr_i[0] % 2
            tr_i[0] += 1
            tr_i[0] += 1
            P.op("pe", lambda e: [e.transpose(out=TRb[ti][:, t * 2 + p, :], in_=kT[b][:, p, t * 128:(t + 1) * 128], identity=ident_bf)
                                  for t in range(NT) for p in range(2)],
                 reads=R_kT[b] + [R_cbfi], writes=[R_TR[ti]])
            P.op("act", lambda e: e.copy(out=kt[:, 0:NT, :], in_=TRb[ti][:, 0:2 * NT, :].rearrange("p (t q) n -> p t (q n)", q=2)),
                 reads=[R_TR[ti]], writes=[R_kt])

        def stage1(b, tok0, NT, sample, last_prompt):
            T = NT * 128
            rm = cst_f[:, CO_RMS:CO_RMS + 128] if sample else cst_f[:, CO_RMP:CO_RMP + 512]
            bank, Rb = inproj_fm(C_LR, T, NT)
            P.op("act", lambda e, bank=bank: e.copy(out=lr_bf[:, 0:T], in_=bank[:, 0:T]), reads=[Rb], writes=[R_lr])
            yield
            hT, R_hT = hTb[cur_hb[0]], R_hTb[cur_hb[0]]

            def v_step(t):
                bank, Rb = big_next()
                P.op("pe", lambda e: [e.matmul(out=bank[:, :], lhsT=hT[:, kc, t * 128:(t + 1) * 128],
                                               rhs=w_v_bf[:, kc, :],
                                               start=(kc == 0), stop=(kc == KC - 1)) for kc in range(KC)],
                     reads=R_win[C_V:C_V + 4] + [R_hT[t]], writes=[Rb])
                P.op("act", lambda e: e.copy(out=vt[b][:, t, :], in_=bank[:, :]), reads=[Rb], writes=[R_vt[b][t]])

            for t in range(min(NT, 2)):
                v_step(t)
                yield
            gbanks = []
            for p in range(2):
                bank, Rb = big_next()
                gbanks.append((bank, Rb))
                P.op("pe", lambda e, bank=bank, p=p: e.matmul(out=bank[:, 0:T], lhsT=wgk_bf[:, p * 128:(p + 1) * 128],
                                                               rhs=lr_bf[:, 0:T], start=True, stop=True),
                     reads=[R_cbf, R_lr], writes=[Rb])
            for p in range(2):
                bank, Rb = gbanks[p]
                e1, R_e1 = e1b[p], R_e1b[p]
                P.op("act", lambda e, bank=bank, p=p, e1=e1: e.activation(out=e1[:, 0:T], in_=bank[:, 0:T], func=AF.Exp,
                                                                           scale=-1.0, bias=nbgk[:, p:p + 1]),
                     reads=[Rb, R_nbgk], writes=[R_e1])
                P.op("act", lambda e, e1=e1: e.activation(out=e1[:, 0:T], in_=e1[:, 0:T], func=AF.Ln, bias=1.0),
                     reads=[R_e1], writes=[R_e1])
                P.op("dve", lambda e, p=p, e1=e1: e.tensor_tensor_scan(out=Bcb[p][:, 0:T], data0=rm[:, 0:T], data1=e1[:, 0:T],
                                                                        initial=0.0, op0=ALU.mult, op1=ALU.add),
                     reads=[R_e1, R_cst], writes=[R_Bcb[p]])
            yield
            for p in range(2):
                Bc, R_Bc = Bcb[p], R_Bcb[p]
                P.op("act", lambda e, p=p, Bc=Bc: e.activation(out=ebq[p][:, 0:T], in_=Bc[:, 0:T], func=AF.Exp,
                                                                scale=-1.0 / 16.0, bias=LN_QS),
                     reads=[R_Bc], writes=[R_ebq[p]])
                P.op("act", lambda e, p=p, Bc=Bc: e.activation(out=enb[p][:, 0:T], in_=Bc[:, 0:T], func=AF.Exp, scale=1.0 / 16.0),
                     reads=[R_Bc], writes=[R_enb[p]])
                if sample:
                    P.op("act", lambda e, p=p, Bc=Bc: e.activation(out=el_s[:, :, p], in_=Bc[:, LS - 1:128:LS], func=AF.Exp,
                                                                    scale=-1.0 / 16.0),
                         reads=[R_Bc], writes=[R_els])
                else:
                    P.op("act", lambda e, p=p, Bc=Bc: e.activation(out=el_p[b][:, p, 0:NT], in_=Bc[:, 127:T:128], func=AF.Exp,
                                                                    scale=-1.0 / 16.0),
                         reads=[R_Bc], writes=[R_elp[b]])
            yield
            for h in range(4):
                bank, Rb = inproj_fm(C_G + h, T, NT)
                silu_from_psum(bank, Rb, T, sg[b][:, h, 0:T], R_sg[b][h])
                yield
                if h < 2 and 2 + h < NT:
                    v_step(2 + h)
                    yield
            for p in range(2):
                bank, Rb = inproj_fm(C_Q + p, T, NT)
                for hh in range(2):
                    rows = slice(hh * 64, (hh + 1) * 64)
                    P.op("dve", lambda e, bank=bank, p=p, hh=hh, rows=rows: e.tensor_tensor(
                        out=qp[b][p][hh][rows, 0:T], in0=bank[rows, 0:T], in1=ebq[p][rows, 0:T], op=ALU.mult),
                        reads=[Rb, R_ebq[p]], writes=[R_qp[b][p][hh]])
                yield
            for p in range(2):
                bank, Rb = inproj_fm(C_K + p, T, NT)
                P.op("dve", lambda e, bank=bank, p=p: e.tensor_tensor(out=kT[b][:, p, 0:T], in0=bank[:, 0:T], in1=enb[p][:, 0:T],
                                                                      op=ALU.mult),
                     reads=[Rb, R_enb[p]], writes=[R_kT[b][p]])
                yield
            kt_transposes(b, NT)
            yield "SPLIT"
            for j in range(4):
                if sample:
                    hv = hp_s[:, j, :, :]
                    Rh = R_hps[j]

                    def win(o, hv=hv):
                        return hv[:, :, o:o + LS]

                    def v3(ap):
                        return ap.rearrange("p (s l) -> p s l", l=LS)
                else:
                    hv = hp_p[:, j, :]
                    Rh = R_hpp[j]

                    def win(o, hv=hv):
                        return hv[:, o:o + T]

                    def v3(ap):
                        return ap
                cb = C_CONV + 4 * j
                cwj = prm_f[:, PO_CW + 3 * j:PO_CW + 3 * j + 3]
                bank, Rb = inproj_fm(cb + 0, T, NT)
                P.op("act", lambda e, bank=bank: e.copy(out=u_sb[:, 0:T], in_=bank[:, 0:T]), reads=[Rb], writes=[R_u])
                yield
                bank, Rb = inproj_fm(cb + 1, T, NT)
                P.op("dve", lambda e, bank=bank, win=win, v3=v3: e.tensor_tensor(out=win(2), in0=v3(bank[:, 0:T]), in1=v3(u_sb[:, 0:T]),
                                                                              op=ALU.mult),
                     reads=[Rb, R_u], writes=[Rh])
                P.op("pool", lambda e, win=win, v3=v3, cwj=cwj: e.tensor_scalar(out=v3(tA[:, 0:T]), in0=win(0), scalar1=cwj[:, 0:1],
                                                                             scalar2=0.0, op0=ALU.mult, op1=ALU.add),
                     reads=[Rh, R_prm], writes=[R_tA])
                yield
                bank, Rb = inproj_fm(cb + 2, T, NT)
                P.op("dve", lambda e, win=win, v3=v3, cwj=cwj: e.scalar_tensor_tensor(out=v3(tB[:, 0:T]), in0=win(1), scalar=cwj[:, 1:2],
                                                                                   in1=v3(tA[:, 0:T]), op0=ALU.mult, op1=ALU.add),
                     reads=[Rh, R_tA, R_prm], writes=[R_tB])
                P.op("dve", lambda e, win=win, v3=v3, cwj=cwj: e.scalar_tensor_tensor(out=v3(tA[:, 0:T]), in0=win(2), scalar=cwj[:, 2:3],
                                                                                   in1=v3(tB[:, 0:T]), op0=ALU.mult, op1=ALU.add),
                     reads=[Rh, R_tB, R_prm], writes=[R_tA])
                if not sample and not last_prompt:
                    P.op("pool", lambda e, hv=hv: e.tensor_copy(out=hv[:, 0:2], in_=hv[:, T:T + 2]), reads=[Rh], writes=[Rh])
                silu_from_psum(bank, Rb, T, sgt[:, 0:T], R_sgt)
                P.op("pool", lambda e: e.tensor_tensor(out=tA[:, 0:T], in0=tA[:, 0:T], in1=sgt[:, 0:T], op=ALU.mult),
                     reads=[R_tA, R_sgt], writes=[R_tA])
                yield
                bank, Rb = inproj_fm(cb + 3, T, NT)
                P.op("dve", lambda e, bank=bank, j=j: e.tensor_tensor(out=mixC[b][:, j, 0:T], in0=bank[:, 0:T], in1=tA[:, 0:T],
                                                                      op=ALU.mult),
                     reads=[Rb, R_tA], writes=[R_mixC[b][j]])
                yield
            if sample or last_prompt:
                bank, Rb = big_next()
                co_sb, R_co, cj = poolA.get()
                if sample:
                    cn_t, R_cn, cnj = poolA.get()
                    cn_sb = cn_t[:, 0:256].rearrange("p (j n) -> p j n", j=4)
                    for j in range(4):
                        P.op("act", lambda e, j=j: e.copy(out=cn_sb[:, j, :].rearrange("p (s r) -> p s r", r=2), in_=hp_s[:, j, :, LS:LS + 2]),
                             reads=[R_hps[j]], writes=[R_cn])
                    P.op("pe", lambda e, bank=bank: [e.transpose(out=bank[0:64, j * 128:(j + 1) * 128], in_=cn_sb[:, j, :],
                                                                  identity=ident_f) for j in range(4)],
                         reads=[R_cn, R_id], writes=[Rb])
                    poolA.release(cnj)
                    P.op("act", lambda e, bank=bank: e.copy(out=co_sb[0:64, 0:512], in_=bank[0:64, :]), reads=[Rb], writes=[R_co])
                    P.dma(lambda e: e.dma_start(out=conv_s_d, in_=co_sb[0:32, 0:512]), R_co, reads=[R_co], is_output=True)
                    poolA.release(cj)
                else:
                    P.op("pe", lambda e, bank=bank: [e.transpose(out=bank[0:32, j * 128:(j + 1) * 128], in_=hp_p[:, j, T - 30:T + 2],
                                                                  identity=ident_f) for j in range(4)],
                         reads=R_hpp + [R_id], writes=[Rb])
                    P.op("act", lambda e, bank=bank: e.copy(out=co_sb[0:32, 0:512], in_=bank[0:32, :]), reads=[Rb], writes=[R_co])
                    P.dma(lambda e: e.dma_start(out=conv_p_d, in_=co_sb[30:32, 0:512]), R_co, reads=[R_co], is_output=True)
                    poolA.release(cj)
                yield

        pend_mix = []

        def flush_mix():
            while pend_mix:
                pend_mix.pop(0)()

        def norm_a(OT, R_OT):
            P.op("act", lambda e: e.activation(out=sq[:], in_=OT[:].rearrange("p h n -> p (h n)"), func=AF.Square),
                 reads=[R_OT], writes=[R_sq])
            P.op("act", lambda e: e.copy(out=o_sb[:], in_=OT[:]), reads=[R_OT], writes=[R_osb])

        def norm_gate(b, c, cs, OT, R_OT):
            MS, R_MS = sml_next()
            P.op("pe", lambda e: e.matmul(out=MS[:].rearrange("p h n -> p (h n)"), lhsT=ones_bf, rhs=sq[:], start=True, stop=True),
                 reads=[R_sq, R_cbf], writes=[R_MS])
            P.op("act", lambda e: e.activation(out=rstd[:], in_=MS[:], func=AF.Ln, bias=EPS), reads=[R_MS], writes=[R_rstd])
            P.op("act", lambda e: e.activation(out=rstd[:], in_=rstd[:], func=AF.Exp, scale=-0.5), reads=[R_rstd], writes=[R_rstd])
            P.op("pool", lambda e: e.tensor_tensor(out=rstd[:], in0=rstd[:], in1=sg[b][:, :, cs], op=ALU.mult),
                 reads=[R_rstd] + R_sg[b], writes=[R_rstd])
            pend_mix.append(lambda: P.op("dve", lambda e: e.scalar_tensor_tensor(out=mixG[:, :, cs], in0=o_sb[:], scalar=gg, in1=rstd[:],
                                                                                op0=ALU.mult, op1=ALU.mult),
                                         reads=[R_osb, R_rstd, R_prm], writes=[R_mixG[c]]))

        def out_proj(b, tok0, t, sample):
            flush_mix()
            for c in range(8):
                need_w("out", c)
            xr, Rx, bi = poolB.get()
            r0 = tok0 + t * 128
            P.dma(lambda e: e.dma_start(out=xr[:], in_=x_all[r0:r0 + 128, :]), Rx, writes=[Rx])
            for hf in range(2):
                bank, Rb = big_next()

                def mm(e, bank=bank, hf=hf):
                    ins = []
                    for kc in range(KC):
                        lhs = mixG[:, kc, t * 128:(t + 1) * 128] if kc < 4 else mixC[b][:, kc - 4, t * 128:(t + 1) * 128]
                        ins.append(e.matmul(out=bank[:, :], lhsT=lhs, rhs=w_out_bf[:, hf, kc, :],
                                            start=(kc == 0), stop=(kc == KC - 1)))
                    return ins
                P.op("pe", mm, reads=[R_mixG[t]] + R_mixC[b] + [R_wout[hf]], writes=[Rb])
                P.op("dve", lambda e, bank=bank, hf=hf: e.tensor_tensor(out=xr[:, hf * 512:(hf + 1) * 512], in0=bank[:, :],
                                                                        in1=xr[:, hf * 512:(hf + 1) * 512], op=ALU.add),
                     reads=[Rb, Rx], writes=[Rx])
                yield
            jk, R_jk, jj = poolX.get()
            P.op("act", lambda e: e.activation(out=jk[:], in_=xr[:], func=AF.Square, accum_out=st2[:, 0:1]),
                 reads=[Rx], writes=[R_jk, R_st2])
            poolX.release(jj)
            P.op("act", lambda e: e.activation(out=st2[:, 1:2], in_=st2[:, 0:1], func=AF.Ln, scale=1.0 / D, bias=EPS),
                 reads=[R_st2], writes=[R_st2])
            P.op("act", lambda e: e.activation(out=st2[:, 2:3], in_=st2[:, 1:2], func=AF.Exp, scale=-0.5),
                 reads=[R_st2], writes=[R_st2])
            P.op("act", lambda e: e.activation(out=xr[:], in_=xr[:], func=AF.Copy, scale=st2[:, 2:3]),
                 reads=[Rx, R_st2], writes=[Rx])
            P.op("pool", lambda e: e.tensor_tensor(out=xr[:], in0=xr[:], in1=fg, op=ALU.mult),
                 reads=[Rx, R_fg], writes=[Rx])
            def store():
                if sample:
                    P.dma(lambda e: e.dma_start(out=y_d[SEQ:SEQ + NSEQ_S * LS, :], in_=xr[0:NSEQ_S * LS, :]), Rx, reads=[Rx],
                          is_output=True)
                else:
                    P.dma(lambda e: e.dma_start(out=y_d[r0:r0 + 128, :], in_=xr[:]), Rx, reads=[Rx], is_output=True)
            poolB.flush_all()
            poolB.defer(bi, store)
            yield

        def gla_prompt(b, tok0, NT, last_prompt):
            Rq = [R_qp[b][0][0], R_qp[b][0][1], R_qp[b][1][0], R_qp[b][1][1]]
            ots = {}

            def A(c):
                cs = slice(c * 128, (c + 1) * 128)
                AT, R_AT = sml_next()
                P.op("pe", lambda e: [e.matmul(out=AT[:, h, :], lhsT=kT[b][:, h // 2, cs], rhs=qp[b][h // 2][h % 2][:, cs],
                                               start=True, stop=True) for h in range(4)],
                     reads=R_kT[b] + Rq, writes=[R_AT])
                P.op("dve", lambda e: e.tensor_tensor(out=ATm[c % 2][:], in0=AT[:], in1=maskP_bf.unsqueeze(1).broadcast_to([128, 4, 128]),
                                                      op=ALU.mult),
                     reads=[R_AT, R_cbf], writes=[R_ATm[c % 2]])

            def U(c):
                SU, R_SU = sml_next()
                P.op("pe", lambda e: [e.matmul(out=SU[:, h, :], lhsT=kt[:, c, (h // 2) * 128:(h // 2 + 1) * 128],
                                               rhs=vt[b][:, c, h * 128:(h + 1) * 128], start=True, stop=True) for h in range(4)],
                     reads=[R_kt, R_vt[b][c]], writes=[R_SU])
                for hh in range(2):
                    rows = slice(hh * 64, (hh + 1) * 64)
                    P.op("dve", lambda e, hh=hh, rows=rows: e.tensor_tensor(
                        out=SUe[c % 2][rows, :, :], in0=SU[rows, hh::2, :], in1=el_p[b][rows, :, c:c + 1].broadcast_to([64, 2, 128]), op=ALU.mult),
                        reads=[R_SU, R_elp[b]], writes=[R_SUe[c % 2]])

            def O(c):
                cs = slice(c * 128, (c + 1) * 128)
                oi = ot_i[0] % 2
                ot_i[0] += 1
                OT, R_OT = OTb[oi], R_OTb[oi]
                ots[c] = (OT, R_OT)
                si = sbf_i[0]

                def o_mm(e):
                    ins = []
                    for h in range(4):
                        ins.append(e.matmul(out=OT[:, h, :], lhsT=vt[b][:, c, h * 128:(h + 1) * 128], rhs=ATm[c % 2][:, h, :],
                                            start=True, stop=False))
                        ins.append(e.matmul(out=OT[:, h, :], lhsT=S_bf[si][:, h // 2, :], rhs=qp[b][h // 2][h % 2][:, cs],
                                            start=False, stop=True))
                    return ins
                P.op("pe", o_mm, reads=[R_vt[b][c], R_ATm[c % 2], R_Sbf[si]] + Rq, writes=[R_OT])
                norm_a(OT, R_OT)
                for p in range(2):
                    P.op("dve", lambda e, p=p: e.scalar_tensor_tensor(out=S[:, p, :], in0=S[:, p, :], scalar=el_p[b][:, p, c:c + 1],
                                                                     in1=SUe[c % 2][:, p, :], op0=ALU.mult, op1=ALU.add),
                         reads=[R_S, R_SUe[c % 2], R_elp[b]], writes=[R_S])
                sn = 1 - si
                pend_mix.append(lambda: P.op("act", lambda e: e.copy(out=S_bf[sn][:], in_=S[:]), reads=[R_S], writes=[R_Sbf[sn]]))
                sbf_i[0] = sn
                if last_prompt and c == NT - 1:
                    P.dma(lambda e: [e.dma_start(out=gla_p_d.rearrange("(pr hh) k v -> hh k pr v", hh=2)[hh],
                                                 in_=S[hh * 64:(hh + 1) * 64, :, :]) for hh in range(2)],
                          R_S, reads=[R_S], n=2, is_output=True)

            def N(c):
                OT, R_OT = ots.pop(c)
                norm_gate(b, c, slice(c * 128, (c + 1) * 128), OT, R_OT)

            sched = []
            for c in range(NT):
                sched.append(("A", c))
                if c >= 1:
                    sched.append(("N", c - 1))
                sched.append(("U", c))
                sched.append(("O", c))
            deferred_last["f"] = lambda: N(NT - 1)
            for kind, c in sched:
                flush_mix()
                if kind == "A":
                    A(c)
                    yield
                elif kind == "U":
                    U(c)
                    yield
                elif kind == "O":
                    O(c)
                    yield
                elif kind == "N":
                    N(c)
                    yield

        def gla_sample(b, tok0):
            Rq = [R_qp[b][0][0], R_qp[b][0][1], R_qp[b][1][0], R_qp[b][1][1]]
            cs = slice(0, 128)
            AT, R_AT = sml_next()
            P.op("pe", lambda e: [e.matmul(out=AT[:, h, :], lhsT=kT[b][:, h // 2, cs], rhs=qp[b][h // 2][h % 2][:, cs],
                                           start=True, stop=True) for h in range(4)],
                 reads=R_kT[b] + Rq, writes=[R_AT])
            P.op("dve", lambda e: e.tensor_tensor(out=ATm[0][:], in0=AT[:], in1=maskS_bf.unsqueeze(1).broadcast_to([128, 4, 128]),
                                                  op=ALU.mult),
                 reads=[R_AT, R_cbf], writes=[R_ATm[0]])
            oi = ot_i[0] % 2
            ot_i[0] += 1
            OT, R_OT = OTb[oi], R_OTb[oi]
            yield
            first = [True]
            pre = {}
            prev_store = [None]

            def s0_issue(G):
                S0g_t, R_S0g, gj = poolA.get()
                S0g = S0g_t[:].rearrange("p (s a v) -> p s a v", s=4, a=2)
                P.dma(lambda e, G=G, S0g=S0g: [e.dma_start(out=S0g[hh * 64:(hh + 1) * 64, :, :, :],
                                                           in_=gla_in.rearrange("s (pr hh) k v -> hh k s pr v", hh=2)[hh][:, 4 * G:4 * G + 4])
                                               for hh in range(2)],
                      R_S0g, writes=[R_S0g], n=2)
                pre[G] = (S0g, R_S0g, gj)
            s0_issue(0)
            for G in range(4):
                if G + 1 < 4:
                    s0_issue(G + 1)
                S0g, R_S0g, gj = pre.pop(G)
                sbt, R_S0gbf, sbj = poolX.get()
                S0g_bf = sbt[:].rearrange("p (s a v) -> p s a v", s=4, a=2)
                P.op("act", lambda e, S0g=S0g, S0g_bf=S0g_bf: e.copy(out=S0g_bf, in_=S0g), reads=[R_S0g], writes=[R_S0gbf])

                def inter(e, G=G, S0g_bf=S0g_bf):
                    ins = []
                    for h in range(4):
                        for i in range(4):
                            s = 4 * G + i
                            ins.append(e.matmul(out=OT[:, h, s * LS:(s + 1) * LS], lhsT=S0g_bf[:, i, h // 2, :],
                                                rhs=qp[b][h // 2][h % 2][:, s * LS:(s + 1) * LS],
                                                start=first[0], stop=False, skip_group_check=True))
                            first[0] = False
                    return ins
                P.op("pe", inter, reads=[R_S0gbf] + Rq, writes=[R_OT])
                poolX.release(sbj)
                yield
                kmt, R_kmask, kmj = poolX.get()
                kmask = kmt[:].rearrange("p (s n) -> p s n", s=4)
                sel = cst_f[:, CO_SEL + 4 * G:CO_SEL + 4 * G + 4]
                P.op("dve", lambda e, sel=sel, kmask=kmask: e.tensor_tensor(out=kmask, in0=kt[:, 0:1, :].broadcast_to([128, 4, 256]),
                                                                            in1=sel.unsqueeze(2).broadcast_to([128, 4, 256]), op=ALU.mult),
                     reads=[R_kt, R_cst], writes=[R_kmask])
                for i in range(4):
                    SU, R_SU = sml_next()
                    P.op("pe", lambda e, i=i, SU=SU, kmask=kmask: [e.matmul(out=SU[:, h, :], lhsT=kmask[:, i, (h // 2) * 128:(h // 2 + 1) * 128],
                                                                            rhs=vt[b][:, 0, h * 128:(h + 1) * 128], start=True, stop=True)
                                                                   for h in range(4)],
                         reads=[R_kmask, R_vt[b][0]], writes=[R_SU])
                    for hh in range(2):
                        rows = slice(hh * 64, (hh + 1) * 64)
                        P.op("dve", lambda e, i=i, hh=hh, rows=rows, SU=SU, S0g=S0g: e.tensor_tensor(
                            out=S0g[rows, i, :, :], in0=SU[rows, hh::2, :], in1=S0g[rows, i, :, :], op=ALU.add),
                            reads=[R_SU, R_S0g], writes=[R_S0g])
                    yield
                poolX.release(kmj)
                P.op("pool", lambda e, G=G, S0g=S0g: e.tensor_tensor(out=S0g, in0=S0g,
                                                                     in1=el_s[:, 4 * G:4 * G + 4, :].unsqueeze(3).broadcast_to([128, 4, 2, 128]),
                                                                     op=ALU.mult),
                     reads=[R_S0g, R_els], writes=[R_S0g])
                def st(G=G, S0g=S0g, R_S0g=R_S0g):
                    P.dma(lambda e: [e.dma_start(out=gla_s_d.rearrange("s (pr hh) k v -> hh k s pr v", hh=2)[hh][:, 4 * G:4 * G + 4],
                                                 in_=S0g[hh * 64:(hh + 1) * 64, :, :, :]) for hh in range(2)],
                          R_S0g, reads=[R_S0g], n=2, is_output=True)
                poolA.defer(gj, st)
                yield
                poolA.flush_all()
            P.op("pe", lambda e: [e.matmul(out=OT[:, h, :], lhsT=vt[b][:, 0, h * 128:(h + 1) * 128], rhs=ATm[0][:, h, :],
                                           start=False, stop=(h == 3), skip_group_check=True) for h in range(4)],
                 reads=[R_vt[b][0], R_ATm[0]], writes=[R_OT])
            norm_a(OT, R_OT)
            yield
            deferred_last["f"] = lambda: norm_gate(b, 0, cs, OT, R_OT)

        deferred_last = {}

        def outproj_all(b, tok0, NT, sample):
            deferred_last.pop("f")()
            yield
            for t in range(NT):
                yield from out_proj(b, tok0, t, sample)
            poolB.flush_all()

        def conv_state_in():
            cs_t, R_csin, cj = poolA.get()
            cs_in = cs_t[0:32, 0:512]
            P.dma(lambda e: e.dma_start(out=cs_in, in_=conv_in), R_csin, writes=[R_csin])
            bank, Rb = big_next()
            P.op("pe", lambda e: [e.transpose(out=bank[:, j * 32:(j + 1) * 32], in_=cs_t[0:32, j * 128:(j + 1) * 128],
                                              identity=ident_f[0:32, 0:32]) for j in range(4)],
                 reads=[R_csin, R_id], writes=[Rb])
            poolA.release(cj)
            for j in range(4):
                P.op("dve", lambda e, j=j: e.tensor_copy(out=hp_s[:, j, 0:NSEQ_S, 0:2],
                                                         in_=bank[:, j * 32:(j + 1) * 32].rearrange("p (s r) -> p s r", r=2)),
                     reads=[Rb], writes=[R_hps[j]])
            yield


        def chain(*gens):
            for g in gens:
                yield from g

        def interleave(a, bgen, na=1, nb=1, extras=()):
            a_done = a is None
            b_done = bgen is None
            acc = 0
            extras = list(extras)
            every = max(1, (na - 2) // (len(extras) + 1)) if extras else 0
            ka = 0
            while not (a_done and b_done):
                if not a_done:
                    try:
                        next(a)
                    except StopIteration:
                        a_done = True
                    ka += 1
                    if extras and ka >= 1 and (ka - 1) % every == 0:
                        extras.pop(0)()
                acc += nb
                while (acc >= na or a_done) and not b_done:
                    acc -= na
                    try:
                        next(bgen)
                    except StopIteration:
                        b_done = True

        def n_steps1(NT, sample, lastp):
            return NT + (1 if sample else 0) + 1 + NT + 2 + 4 + 2 + 2 + 16 + (1 if (sample or lastp) else 0)

        def n_steps2(NT, sample):
            return (1 + 1 + 4 * 5 + 3 + 3) if sample else (1 + 4 * NT + 3 * NT)

        items = [(0, 4, False, False), (512, 4, False, False), (1024, 4, False, False), (1536, 4, False, True), (SEQ, 1, True, False)]
        P.dma(lambda e: e.dma_start(out=cst_f[:, 0:128], in_=cst_d[:, 0:128]), R_id, writes=[R_id])
        P.op("act", lambda e: e.copy(out=cbf[:, 0:128], in_=cst_f[:, CO_ID:CO_ID + 128]), reads=[R_id], writes=[R_cbfi])
        issue_weights(1)
        for t in range(items[0][1]):
            x_load(items[0][0], t)
        P.dma(lambda e: e.dma_start(out=prm_f[:], in_=prm_d), R_prm, writes=[R_prm])
        P.dma(lambda e: e.dma_start(out=cst_f[:, 128:CW], in_=cst_d[:, 128:CW]), R_cst, writes=[R_cst])
        c2, R_c2, c2j = poolA.get()
        P.dma(lambda e: e.dma_start(out=c2[:, 0:CW2], in_=cst2_d), R_c2, writes=[R_c2])
        P.op("act", lambda e: e.copy(out=cbf[:, 128:768], in_=c2[:, 0:CW2]), reads=[R_c2], writes=[R_cbf])
        poolA.release(c2j)
        issue_weights(100)
        P.dma(lambda e: e.dma_start(out=fg_t[:], in_=fg_d), R_fg, writes=[R_fg])
        P.op("act", lambda e: e.mul(out=nbgk[:], in_=prm_f[:, PO_BGK:PO_BGK + 2], mul=-1.0),
             reads=[R_prm], writes=[R_nbgk])
        for b in range(1):
            for p in range(2):
                for h in range(2):
                    P.op("dve", lambda e, t=qp[b][p][h]: e.memset(t[:], 0.0), writes=[R_qp[b][p][h]])
        P.op("dve", lambda e: e.memset(hp_p[:], 0.0), writes=R_hpp)
        P.op("dve", lambda e: e.memset(hp_s[:], 0.0), writes=R_hps)
        P.op("dve", lambda e: e.memset(S[:], 0.0), writes=[R_S])
        P.op("dve", lambda e: e.memset(S_bf[0][:], 0.0), writes=[R_Sbf[0]])
        sbf_i = [0]

        def run_until_split(g):
            for v in g:
                if v == "SPLIT":
                    return
                yield

        prev_out = None
        prev_nout = 1
        for i, (tok0, NT, sample, lastp) in enumerate(items):
            b = 0
            hb = i % 2
            cur_hb[0] = hb
            gens = [x_prep(tok0, NT, hb)]
            if sample:
                gens.append(conv_state_in())
            g1 = chain(*gens, stage1(b, tok0, NT, sample, lastp))
            nA = (1 if sample else 0) + 1 + NT + 2 + 4 + 2 + 2
            interleave(run_until_split(g1), prev_out, nA, prev_nout)
            extras = []
            if i + 1 < len(items):
                ntok0, nNT = items[i + 1][0], items[i + 1][1]

                def mk(t, ntok0=ntok0, nNT=nNT):
                    def f():
                        if t < nNT and poolA.nfree() >= 2:
                            x_dma(ntok0, t)
                        if t >= 1 and (ntok0, t - 1) in xdma and poolX.nfree() >= 2:
                            x_load(ntok0, t - 1)
                    return f

                def mk2(t, ntok0=ntok0, nhb=1 - hb):
                    def f():
                        if (ntok0, t) in xready:
                            x_tr(ntok0, t, nhb)
                    return f
                extras = [mk(t) for t in range(nNT + 1)] + [mk2(t) for t in range(nNT)]
            gla = gla_sample(b, tok0) if sample else gla_prompt(b, tok0, NT, lastp)
            nB = 16 + (1 if (sample or lastp) else 0)
            nG = (1 + 4 * 6 + 1) if sample else (4 * NT - 1)
            interleave(g1, gla, nB, nG, extras)
            prev_out = outproj_all(b, tok0, NT, sample)
            prev_nout = 3 * NT + 1
        interleave(None, prev_out)

        with nc.Block() as block:
            @block.sync
            def _(e):
                P.replay("sp", e)

            @block.scalar
            def _(e):
                P.replay("act", e)

            @block.vector
            def _(e):
                P.replay("dve", e)

            @block.gpsimd
            def _(e):
                P.replay("pool", e)

            @block.tensor
            def _(e):
                P.replay("pe", e)
    return nc


def _colmap():
    cols = []
    cols += list(range(1536, 1552)) + [-1] * 112
    cols += list(range(0, 256))
    cols += list(range(256, 512))
    cols += list(range(512, 1024))
    cols += list(range(1024, 1536))
    for j in range(4):
        cols += list(range(1552 + 128 * j, 1552 + 128 * (j + 1)))
        cols += list(range(2576 + 128 * j, 2576 + 128 * (j + 1)))
        cols += list(range(3088 + 128 * j, 3088 + 128 * (j + 1)))
        cols += list(range(2064 + 128 * j, 2064 + 128 * (j + 1)))
    assert len(cols) == NCH * 128
    return np.array(cols)


def _constants():
    cst = np.zeros((128, CW), np.float32)
    i = np.arange(128)
    cst[:, CO_ID:CO_ID + 128] = np.eye(128, dtype=np.float32)
    rmp = np.ones(512, np.float32); rmp[0::128] = 0.0
    rms = np.ones(128, np.float32); rms[0::LS] = 0.0
    cst[:, CO_RMP:CO_RMP + 512] = rmp[None, :]
    cst[:, CO_RMS:CO_RMS + 128] = rms[None, :]
    cst[:, CO_SEL:CO_SEL + 32] = (i[:, None] // LS == np.arange(32)[None, :]).astype(np.float32)
    return cst


def _constants2(w_gk_up):
    c2 = np.zeros((128, CW2), np.float32)
    i = np.arange(128)
    c2[:, C2_ONES:C2_ONES + 128] = 1.0 / 128.0
    c2[:, C2_MP:C2_MP + 128] = (i[None, :] >= i[:, None]).astype(np.float32)
    c2[:, C2_MS:C2_MS + 128] = ((i[None, :] >= i[:, None]) & (i[None, :] // LS == i[:, None] // LS)).astype(np.float32)
    c2[0:16, C2_WGK:C2_WGK + 256] = w_gk_up
    return c2


_CACHE = {}


def kernel(x_prompt, x_sample, state_gla, state_conv, norm_gain, w_in, w_gk_up, b_gk,
           gla_norm_gain, conv_w, w_out, final_norm_gain):
    f = lambda a: np.ascontiguousarray(np.asarray(a), dtype=np.float32)
    x_prompt, x_sample, state_gla, state_conv = f(x_prompt), f(x_sample), f(state_gla), f(state_conv)
    norm_gain, w_in, w_gk_up, b_gk = f(norm_gain), f(w_in), f(w_gk_up), f(b_gk)
    gla_norm_gain, conv_w, w_out, final_norm_gain = f(gla_norm_gain), f(conv_w), f(w_out), f(final_norm_gain)

    if "nc" not in _CACHE:
        _CACHE["nc"] = build_program()
    nc = _CACHE["nc"]

    cm = _colmap()
    w_in_p = np.concatenate([w_in[0], np.zeros((D, 1), np.float32)], axis=1)[:, cm]
    w_in_l = np.ascontiguousarray(w_in_p.reshape(KC, 128, NCH, 128).transpose(1, 2, 0, 3)).reshape(128, NCH, KC * 128)
    w_v_l = np.ascontiguousarray(w_in[0][:, 512:1024].reshape(KC, 128, 512).transpose(1, 0, 2)).reshape(128, KC * 512)
    w_out_l = np.ascontiguousarray(w_out[0].reshape(KC, 128, 2, 512).transpose(1, 2, 0, 3)).reshape(128, 2, KC * 512)
    prm = np.zeros((128, PW), np.float32)
    prm[:, PO_G:PO_G + 8] = norm_gain[0].reshape(KC, 128).T
    prm[:, PO_BGK:PO_BGK + 2] = b_gk[0].reshape(2, 128).T
    prm[:, PO_GG] = gla_norm_gain[0]
    prm[:, PO_CW:PO_CW + 12] = conv_w[0].reshape(3, 4, 128).transpose(2, 1, 0).reshape(128, 12)
    fg = np.ascontiguousarray(np.broadcast_to(final_norm_gain[None, :], (128, D)))
    cst = _constants()
    cst2 = _constants2(w_gk_up[0])

    in_maps = []
    for c in range(NCORES):
        xs = x_sample[c * NSEQ_S:(c + 1) * NSEQ_S].reshape(NSEQ_S * LS, D)
        x_all = np.concatenate([x_prompt[c], xs, np.zeros((128 - NSEQ_S * LS, D), np.float32)], axis=0)
        in_maps.append({
            "x_all": np.ascontiguousarray(x_all),
            "gla_in": np.ascontiguousarray(state_gla[0, c * NSEQ_S:(c + 1) * NSEQ_S]),
            "conv_in": np.ascontiguousarray(state_conv[0, c * NSEQ_S:(c + 1) * NSEQ_S].reshape(NSEQ_S * 2, 512)),
            "w_in_l": w_in_l, "w_v_l": w_v_l, "w_out_l": w_out_l, "cst": cst, "cst2": cst2, "prm": prm, "fg": fg,
        })
    res = run_bass_kernel_spmd(nc, in_maps, core_ids=list(range(NCORES)))
    R = res.results
    y_prompt = np.stack([R[c]["y"][:SEQ] for c in range(NCORES)], axis=0)
    y_sample = np.concatenate([R[c]["y"][SEQ:].reshape(NSEQ_S, LS, D) for c in range(NCORES)], axis=0)
    gla_p = np.stack([R[c]["gla_p"] for c in range(NCORES)], axis=0)[None]
    conv_p = np.stack([R[c]["conv_p"] for c in range(NCORES)], axis=0)[None]
    gla_s = np.concatenate([R[c]["gla_s"] for c in range(NCORES)], axis=0)[None]
    conv_s = np.concatenate([R[c]["conv_s"].reshape(NSEQ_S, 2, 512) for c in range(NCORES)], axis=0)[None]
    return (y_prompt.astype(np.float32), y_sample.astype(np.float32), gla_p.astype(np.float32),
            conv_p.astype(np.float32), gla_s.astype(np.float32), conv_s.astype(np.float32))
```

```python
import math
from contextlib import ExitStack

import numpy as np
import concourse.bass as bass
import concourse.mybir as mybir
from concourse.bass_utils import run_bass_kernel_spmd

F32 = mybir.dt.float32
BF16 = mybir.dt.bfloat16
AF = mybir.ActivationFunctionType
ALU = mybir.AluOpType

NCORES = 8
D = 1024
KC = 8
SEQ = 2048
NSEQ_S = 16
LS = 4
NIN = 3600
NCH = 29
EPS = 1e-6
LN_QS = math.log(0.125)

C_LR = 0
C_Q = 1
C_K = 3
C_V = 5
C_G = 9
C_CONV = 13

CO_ID = 0
CO_RMP = 128
CO_RMS = 640
CO_SEL = 768
CW = 800
C2_ONES = 0
C2_MP = 128
C2_MS = 256
C2_WGK = 384
CW2 = 640
PO_G = 0
PO_BGK = 8
PO_GG = 10
PO_CW = 11
PW = 23


class _Res:
    __slots__ = ("name", "w", "r", "dsem", "dcnt", "excl")

    def __init__(self, name, excl=False):
        self.name = name
        self.w = None
        self.r = {}
        self.dsem = None
        self.dcnt = 0
        self.excl = excl


class _Eng:
    def __init__(self, name):
        self.name = name
        self.ops = []
        self.cnt = 0
        self.sem = None
        self.seen = {}


class _Plan:
    def __init__(self, nc, es):
        self.nc = nc
        self.es = es
        self.eng = {n: _Eng(n) for n in ("pe", "act", "dve", "pool", "sp")}
        for n, e in self.eng.items():
            e.sem = es.enter_context(nc.semaphore("s_" + n))
        self.nsem = 0
        self.out_events = []

    def res(self, name, excl=False):
        return _Res(name, excl)

    def _deps(self, eng, reads, writes):
        E = self.eng[eng]
        deps = []
        for r in reads:
            if r.w is not None:
                deps.append(r.w)
            if r.excl:
                deps.extend(ev for ev in r.r.values() if ev[2] != eng)
        for w in writes:
            if w.w is not None:
                deps.append(w.w)
            deps.extend(w.r.values())
        waits = []
        for (sem, val, src) in deps:
            if src == "pe" and eng == "pe":
                continue
            k = id(sem)
            if E.seen.get(k, 0) >= val:
                continue
            E.seen[k] = val
            waits.append((sem, val))
        return waits

    def _commit(self, ev, reads, writes):
        for r in reads:
            k = id(ev[0])
            old = r.r.get(k)
            if old is None or old[1] < ev[1]:
                r.r[k] = ev
        for w in writes:
            w.w = ev
            w.r = {}

    def op(self, eng, fn, reads=(), writes=()):
        E = self.eng[eng]
        waits = self._deps(eng, reads, writes)
        E.cnt += 1
        ev = (E.sem, E.cnt, eng)
        E.ops.append((waits, fn, (E.sem, 1), False))
        self._commit(ev, reads, writes)
        return ev

    def dma(self, fn, buf, reads=(), writes=(), n=1, is_output=False, queue="sp"):
        E = self.eng[queue]
        if buf.dsem is None:
            self.nsem += 1
            buf.dsem = self.es.enter_context(self.nc.semaphore("d%d_%s" % (self.nsem, buf.name)))
        waits = self._deps(queue, reads, writes)
        buf.dcnt += 16 * n
        ev = (buf.dsem, buf.dcnt, "dma")
        E.ops.append((waits, fn, (buf.dsem, 16), True))
        self._commit(ev, reads, writes)
        if is_output:
            self.out_events.append(ev)
        return ev

    def replay(self, eng, e):
        E = self.eng[eng]
        for waits, fn, inc, inc_all in E.ops:
            for sem, val in waits:
                e.wait_ge(sem, val)
            ins = fn(e)
            if not isinstance(ins, (list, tuple)):
                ins = [ins]
            if inc_all:
                for i in ins:
                    i.then_inc(inc[0], inc[1])
            else:
                ins[-1].then_inc(inc[0], inc[1])
        if eng == "sp":
            last = {}
            for (sem, val, _) in self.out_events:
                k = id(sem)
                if k not in last or last[k][1] < val:
                    last[k] = (sem, val)
            for sem, val in last.values():
                e.wait_ge(sem, val)


def build_program():
    nc = bass.Bass("TRN2", target_bir_lowering=False)
    NTOK_IN = SEQ + 128
    NTOK_OUT = SEQ + NSEQ_S * LS
    x_all = nc.dram_tensor("x_all", [NTOK_IN, D], F32, kind="ExternalInput").ap()
    gla_in = nc.dram_tensor("gla_in", [NSEQ_S, 4, 64, 128], F32, kind="ExternalInput").ap()
    conv_in = nc.dram_tensor("conv_in", [NSEQ_S * 2, 512], F32, kind="ExternalInput").ap()
    w_in_d = nc.dram_tensor("w_in_l", [128, NCH, KC * 128], F32, kind="ExternalInput").ap()
    w_v_d = nc.dram_tensor("w_v_l", [128, KC * 512], F32, kind="ExternalInput").ap()
    w_out_d = nc.dram_tensor("w_out_l", [128, 2, KC * 512], F32, kind="ExternalInput").ap()
    cst_d = nc.dram_tensor("cst", [128, CW], F32, kind="ExternalInput").ap()
    cst2_d = nc.dram_tensor("cst2", [128, CW2], F32, kind="ExternalInput").ap()
    prm_d = nc.dram_tensor("prm", [128, PW], F32, kind="ExternalInput").ap()
    fg_d = nc.dram_tensor("fg", [128, D], F32, kind="ExternalInput").ap()
    y_d = nc.dram_tensor("y", [NTOK_OUT, D], F32, kind="ExternalOutput").ap()
    gla_p_d = nc.dram_tensor("gla_p", [4, 64, 128], F32, kind="ExternalOutput").ap()
    conv_p_d = nc.dram_tensor("conv_p", [2, 512], F32, kind="ExternalOutput").ap()
    gla_s_d = nc.dram_tensor("gla_s", [NSEQ_S, 4, 64, 128], F32, kind="ExternalOutput").ap()
    conv_s_d = nc.dram_tensor("conv_s", [NSEQ_S * 2, 512], F32, kind="ExternalOutput").ap()

    with ExitStack() as es:
        es.enter_context(nc.allow_low_precision("bf16 matmul operands, fp32 accumulation"))
        P = _Plan(nc, es)

        def sb(name, shape, dt):
            return es.enter_context(nc.sbuf_tensor(name, shape, dt))

        def ps(name, shape, dt):
            return es.enter_context(nc.psum_tensor(name, shape, dt))

        NSLOT = NCH - 4
        w_in_bf = sb("w_in_bf", [128, NSLOT, KC, 128], BF16)
        w_v_bf = sb("w_v_bf", [128, KC, 512], BF16)
        R_win = [P.res("win%d" % c) for c in range(NCH)]
        w_out_bf = sb("w_out_bf", [128, 2, KC, 512], BF16)
        R_wout = [P.res("wout%d" % c) for c in range(2)]

        def wslot(c):
            return c if c < C_V else c - 4
        class _Pool:
            def __init__(self, name, n, dt=F32):
                self.t = [sb("%s%d" % (name, i), [128, 1024], dt) for i in range(n)]
                self.R = [P.res("%s%d" % (name, i)) for i in range(n)]
                self.busy = [False] * n
                self.i = 0
                self.flush = {}

            def get(self):
                n = len(self.t)
                for k in range(n):
                    j = (self.i + k) % n
                    if not self.busy[j]:
                        break
                else:
                    assert self.flush, "pool exhausted"
                    j = next(iter(self.flush))
                    self.flush.pop(j)()
                self.i = j + 1
                self.busy[j] = True
                return self.t[j], self.R[j], j

            def release(self, j):
                self.busy[j] = False

            def nfree(self):
                return sum(1 for x in self.busy if not x)

            def defer(self, j, fn):
                def run():
                    fn()
                    self.busy[j] = False
                self.flush[j] = run

            def flush_all(self):
                for j in list(self.flush):
                    self.flush.pop(j)()

        poolA = _Pool("pA", 3)
        poolB = _Pool("pB", 3)

        poolX = _Pool("xn", 5, BF16)

        cst_f = sb("cst_f", [128, CW], F32); R_cst = P.res("cst")
        prm_f = sb("prm_f", [128, PW], F32); R_prm = P.res("prm")
        fg_t = sb("fg_t", [128, D], F32); R_fg = P.res("fg")
        R_id = P.res("ident")
        cbf = sb("cbf", [128, 512 + 256], BF16); R_cbf = P.res("cbf"); R_cbfi = P.res("cbfi")
        nbgk = sb("nbgk", [128, 2], F32); R_nbgk = P.res("nbgk")
        st1 = sb("st1", [128, 4, 4], F32); R_st1 = [P.res("st1_%d" % i) for i in range(4)]
        st1_i = [0]
        st2 = sb("st2", [128, 8], F32); R_st2 = P.res("st2")
        hTb = [sb("hT%d" % i, [128, KC, 512], BF16) for i in range(2)]
        R_hTb = [[P.res("hT%d_%d" % (i, t)) for t in range(4)] for i in range(2)]
        o_sb = sb("o_sb", [128, 4, 128], F32); R_osb = P.res("o_sb")
        qp = [[[sb("qp%d%d%d" % (b, p, h), [128, 512], BF16) for h in range(2)] for p in range(2)] for b in range(1)]
        R_qp = [[[P.res("qp%d%d%d" % (b, p, h)) for h in range(2)] for p in range(2)] for b in range(1)]
        kT = [sb("kT%d" % b, [128, 2, 512], BF16) for b in range(1)]
        R_kT = [[P.res("kT%d%d" % (b, p)) for p in range(2)] for b in range(1)]
        vt = [sb("vt%d" % b, [128, 4, 512], BF16) for b in range(1)]
        R_vt = [[P.res("vt%d%d" % (b, t)) for t in range(4)] for b in range(1)]
        sg = [sb("sg%d" % b, [128, 4, 512], BF16) for b in range(1)]
        R_sg = [[P.res("sg%d%d" % (b, h)) for h in range(4)] for b in range(1)]
        el_p = [sb("el_p%d" % b, [128, 2, 4], F32) for b in range(1)]
        R_elp = [P.res("elp%d" % b) for b in range(1)]
        mixC = [sb("mixC%d" % b, [128, 4, 512], BF16) for b in range(1)]
        R_mixC = [[P.res("mixC%d%d" % (b, j)) for j in range(4)] for b in range(1)]
        mixG = sb("mixG", [128, 4, 512], BF16)
        R_mixG = [P.res("mixG%d" % t) for t in range(4)]
        kt = sb("kt", [128, 4, 256], BF16); R_kt = P.res("kt")
        el_s = sb("el_s", [128, 32, 2], F32); R_els = P.res("els")
        lr_bf = sb("lr_bf", [128, 512], BF16); R_lr = P.res("lr")
        e1b = [sb("e1_%d" % i, [128, 512], F32) for i in range(2)]; R_e1b = [P.res("e1_0"), P.res("e1_1")]
        Bcb = [sb("Bc%d" % i, [128, 512], F32) for i in range(2)]; R_Bcb = [P.res("Bc0"), P.res("Bc1")]
        ebq = [sb("ebq%d" % p, [128, 512], F32) for p in range(2)]; R_ebq = [P.res("ebq0"), P.res("ebq1")]
        enb = [sb("enb%d" % p, [128, 512], F32) for p in range(2)]; R_enb = [P.res("enb0"), P.res("enb1")]
        u_sb = sb("u_sb", [128, 512], F32); R_u = P.res("u")
        hp_p = sb("hp_p", [128, 4, 514], F32); R_hpp = [P.res("hpp%d" % j) for j in range(4)]
        hp_s = sb("hp_s", [128, 4, 32, 6], F32); R_hps = [P.res("hps%d" % j) for j in range(4)]
        tA = sb("tA", [128, 512], F32); R_tA = P.res("tA")
        tB = sb("tB", [128, 512], F32); R_tB = P.res("tB")
        sgt = sb("sgt", [128, 512], F32); R_sgt = P.res("sgt")
        ATm = [sb("ATm%d" % i, [128, 4, 128], BF16) for i in range(2)]; R_ATm = [P.res("ATm0"), P.res("ATm1")]
        S = sb("S", [128, 2, 128], F32); R_S = P.res("S")
        SUe = [sb("SUe%d" % i, [128, 2, 128], F32) for i in range(2)]; R_SUe = [P.res("SUe0"), P.res("SUe1")]
        S_bf = [sb("S_bf%d" % i, [128, 2, 128], BF16) for i in range(2)]; R_Sbf = [P.res("Sbf0"), P.res("Sbf1")]
        sq = sb("sq", [128, 512], BF16); R_sq = P.res("sq")
        rstd = sb("rstd", [128, 4, 128], F32); R_rstd = P.res("rstd")

        NBIG = 3
        big = [ps("big%d" % i, [128, 512], F32) for i in range(NBIG)]
        R_big = [P.res("big%d" % i, True) for i in range(NBIG)]
        big_i = [0]

        def big_next():
            i = big_i[0] % NBIG
            big_i[0] += 1
            return big[i], R_big[i]

        NSM = 2
        sml = [ps("sml%d" % i, [128, 4, 128], F32) for i in range(NSM)]
        R_sml = [P.res("sml%d" % i, True) for i in range(NSM)]
        sml_i = [0]

        def sml_next():
            i = sml_i[0] % NSM
            sml_i[0] += 1
            return sml[i], R_sml[i]

        TRb = [ps("TR%d" % i, [128, 8, 128], BF16) for i in range(2)]; R_TR = [P.res("TR0", True), P.res("TR1", True)]
        tr_i = [0]
        OTb = [ps("OT0", [128, 4, 128], F32)] * 2; R_OTb = [P.res("OT0", True)] * 2
        ot_i = [0]

        ident_bf = cbf[:, 0:128]
        ones_bf = cbf[:, 128:256]
        maskP_bf = cbf[:, 256:384]
        maskS_bf = cbf[:, 384:512]
        wgk_bf = cbf[:, 512:768]
        ident_f = cst_f[:, CO_ID:CO_ID + 128]

        g_b = prm_f[:, PO_G:PO_G + 8].unsqueeze(2).broadcast_to([128, 8, 128])
        gg = prm_f[:, PO_GG:PO_GG + 1]
        fg = fg_t[:]

        use_order = [C_LR, "v"] + list(range(C_G, C_G + 4)) + [C_Q, C_Q + 1, C_K, C_K + 1] + [C_CONV + k for k in (0, 1, 2, 4, 3, 5, 6, 8, 7, 9, 10, 12, 11, 13, 14, 15)] + ["o0", "o1"]
        w_next = [0]

        def issue_weights(n):
            while n > 0 and w_next[0] < len(use_order):
                c = use_order[w_next[0]]
                w_next[0] += 1
                n -= 1
                if c == "v":
                    P.dma(lambda e: e.dma_start(out=w_v_bf[:], in_=w_v_d.rearrange("p (k n) -> p k n", k=KC)), R_win[C_V],
                          writes=R_win[C_V:C_V + 4], queue="pool")
                elif c in ("o0", "o1"):
                    hf = int(c[1])
                    P.dma(lambda e, hf=hf: e.dma_start(out=w_out_bf[:, hf, :, :], in_=w_out_d[:, hf, :].rearrange("p (k n) -> p k n", k=KC)),
                          R_wout[hf], writes=[R_wout[hf]], queue="pool")
                else:
                    P.dma(lambda e, c=c: e.dma_start(out=w_in_bf[:, wslot(c), :, :].rearrange("p k n -> p (k n)"), in_=w_in_d[:, c, :]),
                          R_win[c], writes=[R_win[c]], queue="pool")

        def need_w(kind, c):
            pass

        xready = {}

        xdma = {}

        def x_dma(tok0, t):
            xb, Rx, xj = poolA.get()
            r0 = tok0 + t * 128
            P.dma(lambda e: e.dma_start(out=xb[:], in_=x_all[r0:r0 + 128, :]), Rx, writes=[Rx])
            xdma[(tok0, t)] = (xb, Rx, xj)

        def x_load(tok0, t):
            if (tok0, t) not in xdma:
                x_dma(tok0, t)
            xb, Rx, xj = xdma.pop((tok0, t))
            xs, Rxs, sj = poolX.get()
            k = st1_i[0] % 4
            st1_i[0] += 1
            sk, Rk = st1[:, k, :], R_st1[k]
            P.op("act", lambda e: e.activation(out=xs[:], in_=xb[:], func=AF.Square, accum_out=sk[:, 0:1]),
                 reads=[Rx], writes=[Rxs, Rk])
            P.op("act", lambda e: e.activation(out=sk[:, 1:2], in_=sk[:, 0:1], func=AF.Ln, scale=1.0 / D, bias=EPS),
                 reads=[Rk], writes=[Rk])
            P.op("act", lambda e: e.activation(out=sk[:, 2:3], in_=sk[:, 1:2], func=AF.Exp, scale=-0.5),
                 reads=[Rk], writes=[Rk])
            P.op("pool", lambda e: e.tensor_scalar(out=xs[:], in0=xb[:], scalar1=sk[:, 2:3], scalar2=0.0,
                                                   op0=ALU.mult, op1=ALU.add),
                 reads=[Rx, Rk], writes=[Rxs])
            poolA.release(xj)
            xready[(tok0, t)] = (xs, Rxs, sj)

        xdone = set()

        def x_tr(tok0, t, hb):
            if (tok0, t) in xdone:
                return
            xdone.add((tok0, t))
            if (tok0, t) not in xready:
                x_load(tok0, t)
            xs, Rxs, sj = xready.pop((tok0, t))
            ti = tr_i[0] % 2
            tr_i[0] += 1
            P.op("pe", lambda e: [e.transpose(out=TRb[ti][:, kc, :], in_=xs[:, kc * 128:(kc + 1) * 128], identity=ident_bf)
                                  for kc in range(KC)],
                 reads=[Rxs, R_cbfi], writes=[R_TR[ti]])
            P.op("dve", lambda e: e.tensor_tensor(out=hTb[hb][:, :, t * 128:(t + 1) * 128], in0=TRb[ti][:], in1=g_b, op=ALU.mult),
                 reads=[R_TR[ti], R_prm], writes=[R_hTb[hb][t]])
            poolX.release(sj)

        def x_prep(tok0, NT, hb):
            for t in range(NT):
                if (tok0, t) not in xdone:
                    x_tr(tok0, t, hb)
                    yield

        cur_hb = [0]

        def inproj_fm(c, T, NT):
            hT, R_hT = hTb[cur_hb[0]], R_hTb[cur_hb[0]]
            bank, Rb = big_next()
            P.op("pe", lambda e: [e.matmul(out=bank[:, 0:T], lhsT=w_in_bf[:, wslot(c), kc, :], rhs=hT[:, kc, 0:T],
                                            start=(kc == 0), stop=(kc == KC - 1)) for kc in range(KC)],
                 reads=[R_win[c]] + R_hT[:NT], writes=[Rb])
            return bank, Rb

        def silu_from_psum(bank, Rb, T, out_ap, R_out):
            P.op("act", lambda e: e.activation(out=sgt[:, 0:T], in_=bank[:, 0:T], func=AF.Exp, scale=-1.0), reads=[Rb], writes=[R_sgt])
            P.op("act", lambda e: e.activation(out=sgt[:, 0:T], in_=sgt[:, 0:T], func=AF.Ln, bias=1.0), reads=[R_sgt], writes=[R_sgt])
            P.op("act", lambda e: e.activation(out=sgt[:, 0:T], in_=sgt[:, 0:T], func=AF.Exp, scale=-1.0), reads=[R_sgt], writes=[R_sgt])
            P.op("dve", lambda e: e.tensor_tensor(out=out_ap, in0=bank[:, 0:T], in1=sgt[:, 0:T], op=ALU.mult),
                 reads=[Rb, R_sgt], writes=[R_out])

        def kt_transposes(b, NT):
            ti = tr_i[0] % 2
            tr_i[0] += 1
            tr_i[0] += 1
            P.op("pe", lambda e: [e.transpose(out=TRb[ti][:, t * 2 + p, :], in_=kT[b][:, p, t * 128:(t + 1) * 128], identity=ident_bf)
                                  for t in range(NT) for p in range(2)],
                 reads=R_kT[b] + [R_cbfi], writes=[R_TR[ti]])
            P.op("act", lambda e: e.copy(out=kt[:, 0:NT, :], in_=TRb[ti][:, 0:2 * NT, :].rearrange("p (t q) n -> p t (q n)", q=2)),
                 reads=[R_TR[ti]], writes=[R_kt])

        def stage1(b, tok0, NT, sample, last_prompt):
            T = NT * 128
            rm = cst_f[:, CO_RMS:CO_RMS + 128] if sample else cst_f[:, CO_RMP:CO_RMP + 512]
            bank, Rb = inproj_fm(C_LR, T, NT)
            P.op("act", lambda e, bank=bank: e.copy(out=lr_bf[:, 0:T], in_=bank[:, 0:T]), reads=[Rb], writes=[R_lr])
            yield
            hT, R_hT = hTb[cur_hb[0]], R_hTb[cur_hb[0]]

            def v_step(t):
                bank, Rb = big_next()
                P.op("pe", lambda e: [e.matmul(out=bank[:, :], lhsT=hT[:, kc, t * 128:(t + 1) * 128],
                                               rhs=w_v_bf[:, kc, :],
                                               start=(kc == 0), stop=(kc == KC - 1)) for kc in range(KC)],
                     reads=R_win[C_V:C_V + 4] + [R_hT[t]], writes=[Rb])
                P.op("act", lambda e: e.copy(out=vt[b][:, t, :], in_=bank[:, :]), reads=[Rb], writes=[R_vt[b][t]])

            for t in range(min(NT, 2)):
                v_step(t)
                yield
            gbanks = []
            for p in range(2):
                bank, Rb = big_next()
                gbanks.append((bank, Rb))
                P.op("pe", lambda e, bank=bank, p=p: e.matmul(out=bank[:, 0:T], lhsT=wgk_bf[:, p * 128:(p + 1) * 128],
                                                               rhs=lr_bf[:, 0:T], start=True, stop=True),
                     reads=[R_cbf, R_lr], writes=[Rb])
            for p in range(2):
                bank, Rb = gbanks[p]
                e1, R_e1 = e1b[p], R_e1b[p]
                P.op("act", lambda e, bank=bank, p=p, e1=e1: e.activation(out=e1[:, 0:T], in_=bank[:, 0:T], func=AF.Exp,
                                                                           scale=-1.0, bias=nbgk[:, p:p + 1]),
                     reads=[Rb, R_nbgk], writes=[R_e1])
                P.op("act", lambda e, e1=e1: e.activation(out=e1[:, 0:T], in_=e1[:, 0:T], func=AF.Ln, bias=1.0),
                     reads=[R_e1], writes=[R_e1])
                P.op("dve", lambda e, p=p, e1=e1: e.tensor_tensor_scan(out=Bcb[p][:, 0:T], data0=rm[:, 0:T], data1=e1[:, 0:T],
                                                                        initial=0.0, op0=ALU.mult, op1=ALU.add),
                     reads=[R_e1, R_cst], writes=[R_Bcb[p]])
            yield
            for p in range(2):
                Bc, R_Bc = Bcb[p], R_Bcb[p]
                P.op("act", lambda e, p=p, Bc=Bc: e.activation(out=ebq[p][:, 0:T], in_=Bc[:, 0:T], func=AF.Exp,
                                                                scale=-1.0 / 16.0, bias=LN_QS),
                     reads=[R_Bc], writes=[R_ebq[p]])
                P.op("act", lambda e, p=p, Bc=Bc: e.activation(out=enb[p][:, 0:T], in_=Bc[:, 0:T], func=AF.Exp, scale=1.0 / 16.0),
                     reads=[R_Bc], writes=[R_enb[p]])
                if sample:
                    P.op("act", lambda e, p=p, Bc=Bc: e.activation(out=el_s[:, :, p], in_=Bc[:, LS - 1:128:LS], func=AF.Exp,
                                                                    scale=-1.0 / 16.0),
                         reads=[R_Bc], writes=[R_els])
                else:
                    P.op("act", lambda e, p=p, Bc=Bc: e.activation(out=el_p[b][:, p, 0:NT], in_=Bc[:, 127:T:128], func=AF.Exp,
                                                                    scale=-1.0 / 16.0),
                         reads=[R_Bc], writes=[R_elp[b]])
            yield
            for h in range(4):
                bank, Rb = inproj_fm(C_G + h, T, NT)
                silu_from_psum(bank, Rb, T, sg[b][:, h, 0:T], R_sg[b][h])
                yield
                if h < 2 and 2 + h < NT:
                    v_step(2 + h)
                    yield
            for p in range(2):
                bank, Rb = inproj_fm(C_Q + p, T, NT)
                for hh in range(2):
                    rows = slice(hh * 64, (hh + 1) * 64)
                    P.op("dve", lambda e, bank=bank, p=p, hh=hh, rows=rows: e.tensor_tensor(
                        out=qp[b][p][hh][rows, 0:T], in0=bank[rows, 0:T], in1=ebq[p][rows, 0:T], op=ALU.mult),
                        reads=[Rb, R_ebq[p]], writes=[R_qp[b][p][hh]])
                yield
            for p in range(2):
                bank, Rb = inproj_fm(C_K + p, T, NT)
                P.op("dve", lambda e, bank=bank, p=p: e.tensor_tensor(out=kT[b][:, p, 0:T], in0=bank[:, 0:T], in1=enb[p][:, 0:T],
                                                                      op=ALU.mult),
                     reads=[Rb, R_enb[p]], writes=[R_kT[b][p]])
                yield
            kt_transposes(b, NT)
            yield "SPLIT"
            def ctx(j):
                if sample:
                    hv = hp_s[:, j, :, :]
                    Rh = R_hps[j]

                    def win(o):
                        return hv[:, :, o:o + LS]

                    def v3(ap):
                        return ap.rearrange("p (s l) -> p s l", l=LS)
                else:
                    hv = hp_p[:, j, :]
                    Rh = R_hpp[j]

                    def win(o):
                        return hv[:, o:o + T]

                    def v3(ap):
                        return ap
                return hv, Rh, win, v3, C_CONV + 4 * j, prm_f[:, PO_CW + 3 * j:PO_CW + 3 * j + 3]

            def u_step(j):
                hv, Rh, win, v3, cb, cwj = ctx(j)
                bank, Rb = inproj_fm(cb + 0, T, NT)
                P.op("act", lambda e: e.copy(out=u_sb[:, 0:T], in_=bank[:, 0:T]), reads=[Rb], writes=[R_u])

            def C_step(j):
                hv, Rh, win, v3, cb, cwj = ctx(j)
                bank, Rb = inproj_fm(cb + 1, T, NT)
                P.op("dve", lambda e: e.tensor_tensor(out=win(2), in0=v3(bank[:, 0:T]), in1=v3(u_sb[:, 0:T]), op=ALU.mult),
                     reads=[Rb, R_u], writes=[Rh])
                P.op("pool", lambda e: e.tensor_scalar(out=v3(tA[:, 0:T]), in0=win(0), scalar1=cwj[:, 0:1],
                                                       scalar2=0.0, op0=ALU.mult, op1=ALU.add),
                     reads=[Rh, R_prm], writes=[R_tA])

            def g_step(j):
                hv, Rh, win, v3, cb, cwj = ctx(j)
                bank, Rb = inproj_fm(cb + 2, T, NT)
                P.op("dve", lambda e: e.scalar_tensor_tensor(out=v3(tB[:, 0:T]), in0=win(1), scalar=cwj[:, 1:2],
                                                             in1=v3(tA[:, 0:T]), op0=ALU.mult, op1=ALU.add),
                     reads=[Rh, R_tA, R_prm], writes=[R_tB])
                P.op("dve", lambda e: e.scalar_tensor_tensor(out=v3(tA[:, 0:T]), in0=win(2), scalar=cwj[:, 2:3],
                                                             in1=v3(tB[:, 0:T]), op0=ALU.mult, op1=ALU.add),
                     reads=[Rh, R_tB, R_prm], writes=[R_tA])
                if not sample and not last_prompt:
                    P.op("pool", lambda e: e.tensor_copy(out=hv[:, 0:2], in_=hv[:, T:T + 2]), reads=[Rh], writes=[Rh])
                silu_from_psum(bank, Rb, T, sgt[:, 0:T], R_sgt)
                P.op("pool", lambda e: e.tensor_tensor(out=tA[:, 0:T], in0=tA[:, 0:T], in1=sgt[:, 0:T], op=ALU.mult),
                     reads=[R_tA, R_sgt], writes=[R_tA])

            def B_step(j):
                hv, Rh, win, v3, cb, cwj = ctx(j)
                bank, Rb = inproj_fm(cb + 3, T, NT)
                P.op("dve", lambda e: e.tensor_tensor(out=mixC[b][:, j, 0:T], in0=bank[:, 0:T], in1=tA[:, 0:T], op=ALU.mult),
                     reads=[Rb, R_tA], writes=[R_mixC[b][j]])

            cseq = [(u_step, 0), (C_step, 0), (g_step, 0)]
            for j in range(1, 4):
                cseq += [(u_step, j), (B_step, j - 1), (C_step, j), (g_step, j)]
            cseq.append((B_step, 3))
            for fn, j in cseq:
                fn(j)
                yield
            if sample or last_prompt:
                bank, Rb = big_next()
                co_sb, R_co, cj = poolA.get()
                if sample:
                    cn_t, R_cn, cnj = poolA.get()
                    cn_sb = cn_t[:, 0:256].rearrange("p (j n) -> p j n", j=4)
                    for j in range(4):
                        P.op("act", lambda e, j=j: e.copy(out=cn_sb[:, j, :].rearrange("p (s r) -> p s r", r=2), in_=hp_s[:, j, :, LS:LS + 2]),
                             reads=[R_hps[j]], writes=[R_cn])
                    P.op("pe", lambda e, bank=bank: [e.transpose(out=bank[0:64, j * 128:(j + 1) * 128], in_=cn_sb[:, j, :],
                                                                  identity=ident_f) for j in range(4)],
                         reads=[R_cn, R_id], writes=[Rb])
                    poolA.release(cnj)
                    P.op("act", lambda e, bank=bank: e.copy(out=co_sb[0:64, 0:512], in_=bank[0:64, :]), reads=[Rb], writes=[R_co])
                    P.dma(lambda e: e.dma_start(out=conv_s_d, in_=co_sb[0:32, 0:512]), R_co, reads=[R_co], is_output=True)
                    poolA.release(cj)
                else:
                    P.op("pe", lambda e, bank=bank: [e.transpose(out=bank[0:32, j * 128:(j + 1) * 128], in_=hp_p[:, j, T - 30:T + 2],
                                                                  identity=ident_f) for j in range(4)],
                         reads=R_hpp + [R_id], writes=[Rb])
                    P.op("act", lambda e, bank=bank: e.copy(out=co_sb[0:32, 0:512], in_=bank[0:32, :]), reads=[Rb], writes=[R_co])
                    P.dma(lambda e: e.dma_start(out=conv_p_d, in_=co_sb[30:32, 0:512]), R_co, reads=[R_co], is_output=True)
                    poolA.release(cj)
                yield

        pend_mix = []

        def flush_mix():
            while pend_mix:
                pend_mix.pop(0)()

        def norm_a(OT, R_OT):
            P.op("act", lambda e: e.activation(out=sq[:], in_=OT[:].rearrange("p h n -> p (h n)"), func=AF.Square),
                 reads=[R_OT], writes=[R_sq])
            P.op("act", lambda e: e.copy(out=o_sb[:], in_=OT[:]), reads=[R_OT], writes=[R_osb])

        def norm_gate(b, c, cs, OT, R_OT):
            MS, R_MS = sml_next()
            P.op("pe", lambda e: e.matmul(out=MS[:].rearrange("p h n -> p (h n)"), lhsT=ones_bf, rhs=sq[:], start=True, stop=True),
                 reads=[R_sq, R_cbf], writes=[R_MS])
            P.op("act", lambda e: e.activation(out=rstd[:], in_=MS[:], func=AF.Ln, bias=EPS), reads=[R_MS], writes=[R_rstd])
            P.op("act", lambda e: e.activation(out=rstd[:], in_=rstd[:], func=AF.Exp, scale=-0.5), reads=[R_rstd], writes=[R_rstd])
            P.op("pool", lambda e: e.tensor_tensor(out=rstd[:], in0=rstd[:], in1=sg[b][:, :, cs], op=ALU.mult),
                 reads=[R_rstd] + R_sg[b], writes=[R_rstd])
            pend_mix.append(lambda: P.op("dve", lambda e: e.scalar_tensor_tensor(out=mixG[:, :, cs], in0=o_sb[:], scalar=gg, in1=rstd[:],
                                                                                op0=ALU.mult, op1=ALU.mult),
                                         reads=[R_osb, R_rstd, R_prm], writes=[R_mixG[c]]))

        def out_proj(b, tok0, t, sample):
            flush_mix()
            for c in range(8):
                need_w("out", c)
            xr, Rx, bi = poolB.get()
            r0 = tok0 + t * 128
            P.dma(lambda e: e.dma_start(out=xr[:], in_=x_all[r0:r0 + 128, :]), Rx, writes=[Rx])
            for hf in range(2):
                bank, Rb = big_next()

                def mm(e, bank=bank, hf=hf):
                    ins = []
                    for kc in range(KC):
                        lhs = mixG[:, kc, t * 128:(t + 1) * 128] if kc < 4 else mixC[b][:, kc - 4, t * 128:(t + 1) * 128]
                        ins.append(e.matmul(out=bank[:, :], lhsT=lhs, rhs=w_out_bf[:, hf, kc, :],
                                            start=(kc == 0), stop=(kc == KC - 1)))
                    return ins
                P.op("pe", mm, reads=[R_mixG[t]] + R_mixC[b] + [R_wout[hf]], writes=[Rb])
                P.op("dve", lambda e, bank=bank, hf=hf: e.tensor_tensor(out=xr[:, hf * 512:(hf + 1) * 512], in0=bank[:, :],
                                                                        in1=xr[:, hf * 512:(hf + 1) * 512], op=ALU.add),
                     reads=[Rb, Rx], writes=[Rx])
                yield
            jk, R_jk, jj = poolX.get()
            P.op("act", lambda e: e.activation(out=jk[:], in_=xr[:], func=AF.Square, accum_out=st2[:, 0:1]),
                 reads=[Rx], writes=[R_jk, R_st2])
            poolX.release(jj)
            P.op("act", lambda e: e.activation(out=st2[:, 1:2], in_=st2[:, 0:1], func=AF.Ln, scale=1.0 / D, bias=EPS),
                 reads=[R_st2], writes=[R_st2])
            P.op("act", lambda e: e.activation(out=st2[:, 2:3], in_=st2[:, 1:2], func=AF.Exp, scale=-0.5),
                 reads=[R_st2], writes=[R_st2])
            P.op("act", lambda e: e.activation(out=xr[:], in_=xr[:], func=AF.Copy, scale=st2[:, 2:3]),
                 reads=[Rx, R_st2], writes=[Rx])
            P.op("pool", lambda e: e.tensor_tensor(out=xr[:], in0=xr[:], in1=fg, op=ALU.mult),
                 reads=[Rx, R_fg], writes=[Rx])
            def store():
                if sample:
                    P.dma(lambda e: e.dma_start(out=y_d[SEQ:SEQ + NSEQ_S * LS, :], in_=xr[0:NSEQ_S * LS, :]), Rx, reads=[Rx],
                          is_output=True)
                else:
                    P.dma(lambda e: e.dma_start(out=y_d[r0:r0 + 128, :], in_=xr[:]), Rx, reads=[Rx], is_output=True)
            poolB.flush_all()
            poolB.defer(bi, store)
            yield

        def gla_prompt(b, tok0, NT, last_prompt):
            Rq = [R_qp[b][0][0], R_qp[b][0][1], R_qp[b][1][0], R_qp[b][1][1]]
            ots = {}

            def A(c):
                cs = slice(c * 128, (c + 1) * 128)
                AT, R_AT = sml_next()
                P.op("pe", lambda e: [e.matmul(out=AT[:, h, :], lhsT=kT[b][:, h // 2, cs], rhs=qp[b][h // 2][h % 2][:, cs],
                                               start=True, stop=True) for h in range(4)],
                     reads=R_kT[b] + Rq, writes=[R_AT])
                P.op("dve", lambda e: e.tensor_tensor(out=ATm[c % 2][:], in0=AT[:], in1=maskP_bf.unsqueeze(1).broadcast_to([128, 4, 128]),
                                                      op=ALU.mult),
                     reads=[R_AT, R_cbf], writes=[R_ATm[c % 2]])

            def U(c):
                SU, R_SU = sml_next()
                P.op("pe", lambda e: [e.matmul(out=SU[:, h, :], lhsT=kt[:, c, (h // 2) * 128:(h // 2 + 1) * 128],
                                               rhs=vt[b][:, c, h * 128:(h + 1) * 128], start=True, stop=True) for h in range(4)],
                     reads=[R_kt, R_vt[b][c]], writes=[R_SU])
                for hh in range(2):
                    rows = slice(hh * 64, (hh + 1) * 64)
                    P.op("dve", lambda e, hh=hh, rows=rows: e.tensor_tensor(
                        out=SUe[c % 2][rows, :, :], in0=SU[rows, hh::2, :], in1=el_p[b][rows, :, c:c + 1].broadcast_to([64, 2, 128]), op=ALU.mult),
                        reads=[R_SU, R_elp[b]], writes=[R_SUe[c % 2]])

            def O(c):
                cs = slice(c * 128, (c + 1) * 128)
                oi = ot_i[0] % 2
                ot_i[0] += 1
                OT, R_OT = OTb[oi], R_OTb[oi]
                ots[c] = (OT, R_OT)
                si = sbf_i[0]

                def o_mm(e):
                    ins = []
                    for h in range(4):
                        ins.append(e.matmul(out=OT[:, h, :], lhsT=vt[b][:, c, h * 128:(h + 1) * 128], rhs=ATm[c % 2][:, h, :],
                                            start=True, stop=False))
                        ins.append(e.matmul(out=OT[:, h, :], lhsT=S_bf[si][:, h // 2, :], rhs=qp[b][h // 2][h % 2][:, cs],
                                            start=False, stop=True))
                    return ins
                P.op("pe", o_mm, reads=[R_vt[b][c], R_ATm[c % 2], R_Sbf[si]] + Rq, writes=[R_OT])
                norm_a(OT, R_OT)
                for p in range(2):
                    P.op("dve", lambda e, p=p: e.scalar_tensor_tensor(out=S[:, p, :], in0=S[:, p, :], scalar=el_p[b][:, p, c:c + 1],
                                                                     in1=SUe[c % 2][:, p, :], op0=ALU.mult, op1=ALU.add),
                         reads=[R_S, R_SUe[c % 2], R_elp[b]], writes=[R_S])
                sn = 1 - si
                pend_mix.append(lambda: P.op("act", lambda e: e.copy(out=S_bf[sn][:], in_=S[:]), reads=[R_S], writes=[R_Sbf[sn]]))
                sbf_i[0] = sn
                if last_prompt and c == NT - 1:
                    P.dma(lambda e: [e.dma_start(out=gla_p_d.rearrange("(pr hh) k v -> hh k pr v", hh=2)[hh],
                                                 in_=S[hh * 64:(hh + 1) * 64, :, :]) for hh in range(2)],
                          R_S, reads=[R_S], n=2, is_output=True)

            def N(c):
                OT, R_OT = ots.pop(c)
                norm_gate(b, c, slice(c * 128, (c + 1) * 128), OT, R_OT)

            sched = []
            for c in range(NT):
                sched.append(("A", c))
                if c >= 1:
                    sched.append(("N", c - 1))
                sched.append(("U", c))
                sched.append(("O", c))
            deferred_last["f"] = lambda: N(NT - 1)
            for kind, c in sched:
                flush_mix()
                if kind == "A":
                    A(c)
                    yield
                elif kind == "U":
                    U(c)
                    yield
                elif kind == "O":
                    O(c)
                    yield
                elif kind == "N":
                    N(c)
                    yield

        def gla_sample(b, tok0):
            Rq = [R_qp[b][0][0], R_qp[b][0][1], R_qp[b][1][0], R_qp[b][1][1]]
            cs = slice(0, 128)
            AT, R_AT = sml_next()
            P.op("pe", lambda e: [e.matmul(out=AT[:, h, :], lhsT=kT[b][:, h // 2, cs], rhs=qp[b][h // 2][h % 2][:, cs],
                                           start=True, stop=True) for h in range(4)],
                 reads=R_kT[b] + Rq, writes=[R_AT])
            P.op("dve", lambda e: e.tensor_tensor(out=ATm[0][:], in0=AT[:], in1=maskS_bf.unsqueeze(1).broadcast_to([128, 4, 128]),
                                                  op=ALU.mult),
                 reads=[R_AT, R_cbf], writes=[R_ATm[0]])
            oi = ot_i[0] % 2
            ot_i[0] += 1
            OT, R_OT = OTb[oi], R_OTb[oi]
            yield
            first = [True]
            pre = {}
            prev_store = [None]

            def s0_issue(G):
                S0g_t, R_S0g, gj = poolA.get()
                S0g = S0g_t[:].rearrange("p (s a v) -> p s a v", s=4, a=2)
                P.dma(lambda e, G=G, S0g=S0g: [e.dma_start(out=S0g[hh * 64:(hh + 1) * 64, :, :, :],
                                                           in_=gla_in.rearrange("s (pr hh) k v -> hh k s pr v", hh=2)[hh][:, 4 * G:4 * G + 4])
                                               for hh in range(2)],
                      R_S0g, writes=[R_S0g], n=2)
                pre[G] = (S0g, R_S0g, gj)
            s0_issue(0)
            for G in range(4):
                if G + 1 < 4:
                    s0_issue(G + 1)
                S0g, R_S0g, gj = pre.pop(G)
                sbt, R_S0gbf, sbj = poolX.get()
                S0g_bf = sbt[:].rearrange("p (s a v) -> p s a v", s=4, a=2)
                P.op("act", lambda e, S0g=S0g, S0g_bf=S0g_bf: e.copy(out=S0g_bf, in_=S0g), reads=[R_S0g], writes=[R_S0gbf])

                def inter(e, G=G, S0g_bf=S0g_bf):
                    ins = []
                    for h in range(4):
                        for i in range(4):
                            s = 4 * G + i
                            ins.append(e.matmul(out=OT[:, h, s * LS:(s + 1) * LS], lhsT=S0g_bf[:, i, h // 2, :],
                                                rhs=qp[b][h // 2][h % 2][:, s * LS:(s + 1) * LS],
                                                start=first[0], stop=False, skip_group_check=True))
                            first[0] = False
                    return ins
                P.op("pe", inter, reads=[R_S0gbf] + Rq, writes=[R_OT])
                poolX.release(sbj)
                yield
                kmt, R_kmask, kmj = poolX.get()
                kmask = kmt[:].rearrange("p (s n) -> p s n", s=4)
                sel = cst_f[:, CO_SEL + 4 * G:CO_SEL + 4 * G + 4]
                P.op("dve", lambda e, sel=sel, kmask=kmask: e.tensor_tensor(out=kmask, in0=kt[:, 0:1, :].broadcast_to([128, 4, 256]),
                                                                            in1=sel.unsqueeze(2).broadcast_to([128, 4, 256]), op=ALU.mult),
                     reads=[R_kt, R_cst], writes=[R_kmask])
                for i in range(4):
                    SU, R_SU = sml_next()
                    P.op("pe", lambda e, i=i, SU=SU, kmask=kmask: [e.matmul(out=SU[:, h, :], lhsT=kmask[:, i, (h // 2) * 128:(h // 2 + 1) * 128],
                                                                            rhs=vt[b][:, 0, h * 128:(h + 1) * 128], start=True, stop=True)
                                                                   for h in range(4)],
                         reads=[R_kmask, R_vt[b][0]], writes=[R_SU])
                    for hh in range(2):
                        rows = slice(hh * 64, (hh + 1) * 64)
                        P.op("dve", lambda e, i=i, hh=hh, rows=rows, SU=SU, S0g=S0g: e.tensor_tensor(
                            out=S0g[rows, i, :, :], in0=SU[rows, hh::2, :], in1=S0g[rows, i, :, :], op=ALU.add),
                            reads=[R_SU, R_S0g], writes=[R_S0g])
                    yield
                poolX.release(kmj)
                P.op("pool", lambda e, G=G, S0g=S0g: e.tensor_tensor(out=S0g, in0=S0g,
                                                                     in1=el_s[:, 4 * G:4 * G + 4, :].unsqueeze(3).broadcast_to([128, 4, 2, 128]),
                                                                     op=ALU.mult),
                     reads=[R_S0g, R_els], writes=[R_S0g])
                def st(G=G, S0g=S0g, R_S0g=R_S0g):
                    P.dma(lambda e: [e.dma_start(out=gla_s_d.rearrange("s (pr hh) k v -> hh k s pr v", hh=2)[hh][:, 4 * G:4 * G + 4],
                                                 in_=S0g[hh * 64:(hh + 1) * 64, :, :, :]) for hh in range(2)],
                          R_S0g, reads=[R_S0g], n=2, is_output=True)
                poolA.defer(gj, st)
                yield
                poolA.flush_all()
            P.op("pe", lambda e: [e.matmul(out=OT[:, h, :], lhsT=vt[b][:, 0, h * 128:(h + 1) * 128], rhs=ATm[0][:, h, :],
                                           start=False, stop=(h == 3), skip_group_check=True) for h in range(4)],
                 reads=[R_vt[b][0], R_ATm[0]], writes=[R_OT])
            norm_a(OT, R_OT)
            yield
            deferred_last["f"] = lambda: norm_gate(b, 0, cs, OT, R_OT)

        deferred_last = {}

        def outproj_all(b, tok0, NT, sample):
            deferred_last.pop("f")()
            yield
            for t in range(NT):
                yield from out_proj(b, tok0, t, sample)
            poolB.flush_all()

        def conv_state_in():
            cs_t, R_csin, cj = poolA.get()
            cs_in = cs_t[0:32, 0:512]
            P.dma(lambda e: e.dma_start(out=cs_in, in_=conv_in), R_csin, writes=[R_csin])
            bank, Rb = big_next()
            P.op("pe", lambda e: [e.transpose(out=bank[:, j * 32:(j + 1) * 32], in_=cs_t[0:32, j * 128:(j + 1) * 128],
                                              identity=ident_f[0:32, 0:32]) for j in range(4)],
                 reads=[R_csin, R_id], writes=[Rb])
            poolA.release(cj)
            for j in range(4):
                P.op("dve", lambda e, j=j: e.tensor_copy(out=hp_s[:, j, 0:NSEQ_S, 0:2],
                                                         in_=bank[:, j * 32:(j + 1) * 32].rearrange("p (s r) -> p s r", r=2)),
                     reads=[Rb], writes=[R_hps[j]])
            yield


        def chain(*gens):
            for g in gens:
                yield from g

        def interleave(a, bgen, na=1, nb=1, extras=()):
            a_done = a is None
            b_done = bgen is None
            acc = 0
            extras = list(extras)
            every = max(1, (na - 2) // (len(extras) + 1)) if extras else 0
            ka = 0
            while not (a_done and b_done):
                if not a_done:
                    try:
                        next(a)
                    except StopIteration:
                        a_done = True
                    ka += 1
                    if extras and ka >= 1 and (ka - 1) % every == 0:
                        extras.pop(0)()
                acc += nb
                while (acc >= na or a_done) and not b_done:
                    acc -= na
                    try:
                        next(bgen)
                    except StopIteration:
                        b_done = True

        def n_steps1(NT, sample, lastp):
            return NT + (1 if sample else 0) + 1 + NT + 2 + 4 + 2 + 2 + 16 + (1 if (sample or lastp) else 0)

        def n_steps2(NT, sample):
            return (1 + 1 + 4 * 5 + 3 + 3) if sample else (1 + 4 * NT + 3 * NT)

        items = [(0, 4, False, False), (512, 4, False, False), (1024, 4, False, False), (1536, 4, False, True), (SEQ, 1, True, False)]
        P.dma(lambda e: e.dma_start(out=cst_f[:, 0:128], in_=cst_d[:, 0:128]), R_id, writes=[R_id])
        P.op("act", lambda e: e.copy(out=cbf[:, 0:128], in_=cst_f[:, CO_ID:CO_ID + 128]), reads=[R_id], writes=[R_cbfi])
        issue_weights(1)
        for t in range(items[0][1]):
            x_load(items[0][0], t)
        P.dma(lambda e: e.dma_start(out=prm_f[:], in_=prm_d), R_prm, writes=[R_prm])
        P.dma(lambda e: e.dma_start(out=cst_f[:, 128:CW], in_=cst_d[:, 128:CW]), R_cst, writes=[R_cst])
        c2, R_c2, c2j = poolA.get()
        P.dma(lambda e: e.dma_start(out=c2[:, 0:CW2], in_=cst2_d), R_c2, writes=[R_c2])
        P.op("act", lambda e: e.copy(out=cbf[:, 128:768], in_=c2[:, 0:CW2]), reads=[R_c2], writes=[R_cbf])
        poolA.release(c2j)
        issue_weights(100)
        P.dma(lambda e: e.dma_start(out=fg_t[:], in_=fg_d), R_fg, writes=[R_fg])
        P.op("act", lambda e: e.mul(out=nbgk[:], in_=prm_f[:, PO_BGK:PO_BGK + 2], mul=-1.0),
             reads=[R_prm], writes=[R_nbgk])
        for b in range(1):
            for p in range(2):
                for h in range(2):
                    P.op("dve", lambda e, t=qp[b][p][h]: e.memset(t[:], 0.0), writes=[R_qp[b][p][h]])
        P.op("dve", lambda e: e.memset(hp_p[:], 0.0), writes=R_hpp)
        P.op("dve", lambda e: e.memset(hp_s[:], 0.0), writes=R_hps)
        P.op("dve", lambda e: e.memset(S[:], 0.0), writes=[R_S])
        P.op("dve", lambda e: e.memset(S_bf[0][:], 0.0), writes=[R_Sbf[0]])
        sbf_i = [0]

        def run_until_split(g):
            for v in g:
                if v == "SPLIT":
                    return
                yield

        prev_out = None
        prev_nout = 1
        for i, (tok0, NT, sample, lastp) in enumerate(items):
            b = 0
            hb = i % 2
            cur_hb[0] = hb
            gens = [x_prep(tok0, NT, hb)]
            if sample:
                gens.append(conv_state_in())
            g1 = chain(*gens, stage1(b, tok0, NT, sample, lastp))
            nA = (1 if sample else 0) + 1 + NT + 2 + 4 + 2 + 2
            interleave(run_until_split(g1), prev_out, nA, prev_nout)
            extras = []
            if i + 1 < len(items):
                ntok0, nNT = items[i + 1][0], items[i + 1][1]

                def mk(t, ntok0=ntok0, nNT=nNT):
                    def f():
                        if t < nNT and poolA.nfree() >= 2:
                            x_dma(ntok0, t)
                        if t >= 1 and (ntok0, t - 1) in xdma and poolX.nfree() >= 2:
                            x_load(ntok0, t - 1)
                    return f

                def mk2(t, ntok0=ntok0, nhb=1 - hb):
                    def f():
                        if (ntok0, t) in xready:
                            x_tr(ntok0, t, nhb)
                    return f
                extras = [mk(t) for t in range(nNT + 1)] + [mk2(t) for t in range(nNT)]
            gla = gla_sample(b, tok0) if sample else gla_prompt(b, tok0, NT, lastp)
            nB = 16 + (1 if (sample or lastp) else 0)
            nG = (1 + 4 * 6 + 1) if sample else (4 * NT - 1)
            interleave(g1, gla, nB, nG, extras)
            prev_out = outproj_all(b, tok0, NT, sample)
            prev_nout = 3 * NT + 1
        interleave(None, prev_out)

        with nc.Block() as block:
            @block.sync
            def _(e):
                P.replay("sp", e)

            @block.scalar
            def _(e):
                P.replay("act", e)

            @block.vector
            def _(e):
                P.replay("dve", e)

            @block.gpsimd
            def _(e):
                P.replay("pool", e)

            @block.tensor
            def _(e):
                P.replay("pe", e)
    return nc


def _colmap():
    cols = []
    cols += list(range(1536, 1552)) + [-1] * 112
    cols += list(range(0, 256))
    cols += list(range(256, 512))
    cols += list(range(512, 1024))
    cols += list(range(1024, 1536))
    for j in range(4):
        cols += list(range(1552 + 128 * j, 1552 + 128 * (j + 1)))
        cols += list(range(2576 + 128 * j, 2576 + 128 * (j + 1)))
        cols += list(range(3088 + 128 * j, 3088 + 128 * (j + 1)))
        cols += list(range(2064 + 128 * j, 2064 + 128 * (j + 1)))
    assert len(cols) == NCH * 128
    return np.array(cols)


def _constants():
    cst = np.zeros((128, CW), np.float32)
    i = np.arange(128)
    cst[:, CO_ID:CO_ID + 128] = np.eye(128, dtype=np.float32)
    rmp = np.ones(512, np.float32); rmp[0::128] = 0.0
    rms = np.ones(128, np.float32); rms[0::LS] = 0.0
    cst[:, CO_RMP:CO_RMP + 512] = rmp[None, :]
    cst[:, CO_RMS:CO_RMS + 128] = rms[None, :]
    cst[:, CO_SEL:CO_SEL + 32] = (i[:, None] // LS == np.arange(32)[None, :]).astype(np.float32)
    return cst


def _constants2(w_gk_up):
    c2 = np.zeros((128, CW2), np.float32)
    i = np.arange(128)
    c2[:, C2_ONES:C2_ONES + 128] = 1.0 / 128.0
    c2[:, C2_MP:C2_MP + 128] = (i[None, :] >= i[:, None]).astype(np.float32)
    c2[:, C2_MS:C2_MS + 128] = ((i[None, :] >= i[:, None]) & (i[None, :] // LS == i[:, None] // LS)).astype(np.float32)
    c2[0:16, C2_WGK:C2_WGK + 256] = w_gk_up
    return c2


_CACHE = {}


def kernel(x_prompt, x_sample, state_gla, state_conv, norm_gain, w_in, w_gk_up, b_gk,
           gla_norm_gain, conv_w, w_out, final_norm_gain):
    f = lambda a: np.ascontiguousarray(np.asarray(a), dtype=np.float32)
    x_prompt, x_sample, state_gla, state_conv = f(x_prompt), f(x_sample), f(state_gla), f(state_conv)
    norm_gain, w_in, w_gk_up, b_gk = f(norm_gain), f(w_in), f(w_gk_up), f(b_gk)
    gla_norm_gain, conv_w, w_out, final_norm_gain = f(gla_norm_gain), f(conv_w), f(w_out), f(final_norm_gain)

    if "nc" not in _CACHE:
        _CACHE["nc"] = build_program()
    nc = _CACHE["nc"]

    cm = _colmap()
    w_in_p = np.concatenate([w_in[0], np.zeros((D, 1), np.float32)], axis=1)[:, cm]
    w_in_l = np.ascontiguousarray(w_in_p.reshape(KC, 128, NCH, 128).transpose(1, 2, 0, 3)).reshape(128, NCH, KC * 128)
    w_v_l = np.ascontiguousarray(w_in[0][:, 512:1024].reshape(KC, 128, 512).transpose(1, 0, 2)).reshape(128, KC * 512)
    w_out_l = np.ascontiguousarray(w_out[0].reshape(KC, 128, 2, 512).transpose(1, 2, 0, 3)).reshape(128, 2, KC * 512)
    prm = np.zeros((128, PW), np.float32)
    prm[:, PO_G:PO_G + 8] = norm_gain[0].reshape(KC, 128).T
    prm[:, PO_BGK:PO_BGK + 2] = b_gk[0].reshape(2, 128).T
    prm[:, PO_GG] = gla_norm_gain[0]
    prm[:, PO_CW:PO_CW + 12] = conv_w[0].reshape(3, 4, 128).transpose(2, 1, 0).reshape(128, 12)
    fg = np.ascontiguousarray(np.broadcast_to(final_norm_gain[None, :], (128, D)))
    cst = _constants()
    cst2 = _constants2(w_gk_up[0])

    in_maps = []
    for c in range(NCORES):
        xs = x_sample[c * NSEQ_S:(c + 1) * NSEQ_S].reshape(NSEQ_S * LS, D)
        x_all = np.concatenate([x_prompt[c], xs, np.zeros((128 - NSEQ_S * LS, D), np.float32)], axis=0)
        in_maps.append({
            "x_all": np.ascontiguousarray(x_all),
            "gla_in": np.ascontiguousarray(state_gla[0, c * NSEQ_S:(c + 1) * NSEQ_S]),
            "conv_in": np.ascontiguousarray(state_conv[0, c * NSEQ_S:(c + 1) * NSEQ_S].reshape(NSEQ_S * 2, 512)),
            "w_in_l": w_in_l, "w_v_l": w_v_l, "w_out_l": w_out_l, "cst": cst, "cst2": cst2, "prm": prm, "fg": fg,
        })
    res = run_bass_kernel_spmd(nc, in_maps, core_ids=list(range(NCORES)))
    R = res.results
    y_prompt = np.stack([R[c]["y"][:SEQ] for c in range(NCORES)], axis=0)
    y_sample = np.concatenate([R[c]["y"][SEQ:].reshape(NSEQ_S, LS, D) for c in range(NCORES)], axis=0)
    gla_p = np.stack([R[c]["gla_p"] for c in range(NCORES)], axis=0)[None]
    conv_p = np.stack([R[c]["conv_p"] for c in range(NCORES)], axis=0)[None]
    gla_s = np.concatenate([R[c]["gla_s"] for c in range(NCORES)], axis=0)[None]
    conv_s = np.concatenate([R[c]["conv_s"].reshape(NSEQ_S, 2, 512) for c in range(NCORES)], axis=0)[None]
    return (y_prompt.astype(np.float32), y_sample.astype(np.float32), gla_p.astype(np.float32),
            conv_p.astype(np.float32), gla_s.astype(np.float32), conv_s.astype(np.float32))
```

```python
import math
from contextlib import ExitStack

import numpy as np
import concourse.bass as bass
import concourse.mybir as mybir
from concourse.bass_utils import run_bass_kernel_spmd

F32 = mybir.dt.float32
BF16 = mybir.dt.bfloat16
AF = mybir.ActivationFunctionType
ALU = mybir.AluOpType

NCORES = 8
D = 1024
KC = 8
SEQ = 2048
NSEQ_S = 16
LS = 4
NIN = 3600
NCH = 29
EPS = 1e-6
LN_QS = math.log(0.125)

C_LR = 0
C_Q = 1
C_K = 3
C_V = 5
C_G = 9
C_CONV = 13

CO_ID = 0
CO_RMP = 128
CO_RMS = 640
CO_SEL = 768
CW = 800
C2_ONES = 0
C2_MP = 128
C2_MS = 256
C2_WGK = 384
CW2 = 640
PO_G = 0
PO_BGK = 8
PO_GG = 10
PO_CW = 11
PW = 23


class _Res:
    __slots__ = ("name", "w", "r", "dsem", "dcnt", "excl")

    def __init__(self, name, excl=False):
        self.name = name
        self.w = None
        self.r = {}
        self.dsem = None
        self.dcnt = 0
        self.excl = excl


class _Eng:
    def __init__(self, name):
        self.name = name
        self.ops = []
        self.cnt = 0
        self.sem = None
        self.seen = {}


class _Plan:
    def __init__(self, nc, es):
        self.nc = nc
        self.es = es
        self.eng = {n: _Eng(n) for n in ("pe", "act", "dve", "pool", "sp")}
        for n, e in self.eng.items():
            e.sem = es.enter_context(nc.semaphore("s_" + n))
        self.nsem = 0
        self.out_events = []

    def res(self, name, excl=False):
        return _Res(name, excl)

    def _deps(self, eng, reads, writes):
        E = self.eng[eng]
        deps = []
        for r in reads:
            if r.w is not None:
                deps.append(r.w)
            if r.excl:
                deps.extend(ev for ev in r.r.values() if ev[2] != eng)
        for w in writes:
            if w.w is not None:
                deps.append(w.w)
            deps.extend(w.r.values())
        waits = []
        for (sem, val, src) in deps:
            if src == "pe" and eng == "pe":
                continue
            k = id(sem)
            if E.seen.get(k, 0) >= val:
                continue
            E.seen[k] = val
            waits.append((sem, val))
        return waits

    def _commit(self, ev, reads, writes):
        for r in reads:
            k = id(ev[0])
            old = r.r.get(k)
            if old is None or old[1] < ev[1]:
                r.r[k] = ev
        for w in writes:
            w.w = ev
            w.r = {}

    def op(self, eng, fn, reads=(), writes=()):
        E = self.eng[eng]
        waits = self._deps(eng, reads, writes)
        E.cnt += 1
        ev = (E.sem, E.cnt, eng)
        E.ops.append((waits, fn, (E.sem, 1), False))
        self._commit(ev, reads, writes)
        return ev

    def dma(self, fn, buf, reads=(), writes=(), n=1, is_output=False, queue="sp"):
        E = self.eng[queue]
        if buf.dsem is None:
            self.nsem += 1
            buf.dsem = self.es.enter_context(self.nc.semaphore("d%d_%s" % (self.nsem, buf.name)))
        waits = self._deps(queue, reads, writes)
        buf.dcnt += 16 * n
        ev = (buf.dsem, buf.dcnt, "dma")
        E.ops.append((waits, fn, (buf.dsem, 16), True))
        self._commit(ev, reads, writes)
        if is_output:
            self.out_events.append(ev)
        return ev

    def replay(self, eng, e):
        E = self.eng[eng]
        for waits, fn, inc, inc_all in E.ops:
            for sem, val in waits:
                e.wait_ge(sem, val)
            ins = fn(e)
            if not isinstance(ins, (list, tuple)):
                ins = [ins]
            if inc_all:
                for i in ins:
                    i.then_inc(inc[0], inc[1])
            else:
                ins[-1].then_inc(inc[0], inc[1])
        if eng == "sp":
            last = {}
            for (sem, val, _) in self.out_events:
                k = id(sem)
                if k not in last or last[k][1] < val:
                    last[k] = (sem, val)
            for sem, val in last.values():
                e.wait_ge(sem, val)


def build_program():
    nc = bass.Bass("TRN2", target_bir_lowering=False)
    NTOK_IN = SEQ + 128
    NTOK_OUT = SEQ + NSEQ_S * LS
    x_all = nc.dram_tensor("x_all", [NTOK_IN, D], F32, kind="ExternalInput").ap()
    gla_in = nc.dram_tensor("gla_in", [NSEQ_S, 4, 64, 128], F32, kind="ExternalInput").ap()
    conv_in = nc.dram_tensor("conv_in", [NSEQ_S * 2, 512], F32, kind="ExternalInput").ap()
    w_in_d = nc.dram_tensor("w_in_l", [128, NCH, KC * 128], F32, kind="ExternalInput").ap()
    w_v_d = nc.dram_tensor("w_v_l", [128, KC * 512], F32, kind="ExternalInput").ap()
    w_out_d = nc.dram_tensor("w_out_l", [128, 2, KC * 512], F32, kind="ExternalInput").ap()
    cst_d = nc.dram_tensor("cst", [128, CW], F32, kind="ExternalInput").ap()
    cst2_d = nc.dram_tensor("cst2", [128, CW2], F32, kind="ExternalInput").ap()
    prm_d = nc.dram_tensor("prm", [128, PW], F32, kind="ExternalInput").ap()
    fg_d = nc.dram_tensor("fg", [128, D], F32, kind="ExternalInput").ap()
    y_d = nc.dram_tensor("y", [NTOK_OUT, D], F32, kind="ExternalOutput").ap()
    gla_p_d = nc.dram_tensor("gla_p", [4, 64, 128], F32, kind="ExternalOutput").ap()
    conv_p_d = nc.dram_tensor("conv_p", [2, 512], F32, kind="ExternalOutput").ap()
    gla_s_d = nc.dram_tensor("gla_s", [NSEQ_S, 4, 64, 128], F32, kind="ExternalOutput").ap()
    conv_s_d = nc.dram_tensor("conv_s", [NSEQ_S * 2, 512], F32, kind="ExternalOutput").ap()

    with ExitStack() as es:
        es.enter_context(nc.allow_low_precision("bf16 matmul operands, fp32 accumulation"))
        P = _Plan(nc, es)

        def sb(name, shape, dt):
            return es.enter_context(nc.sbuf_tensor(name, shape, dt))

        def ps(name, shape, dt):
            return es.enter_context(nc.psum_tensor(name, shape, dt))

        NSLOT = NCH - 4
        w_in_bf = sb("w_in_bf", [128, NSLOT, KC, 128], BF16)
        w_v_bf = sb("w_v_bf", [128, KC, 512], BF16)
        R_win = [P.res("win%d" % c) for c in range(NCH)]
        w_out_bf = sb("w_out_bf", [128, 2, KC, 512], BF16)
        R_wout = [P.res("wout%d" % c) for c in range(2)]

        def wslot(c):
            return c if c < C_V else c - 4
        class _Pool:
            def __init__(self, name, n, dt=F32):
                self.t = [sb("%s%d" % (name, i), [128, 1024], dt) for i in range(n)]
                self.R = [P.res("%s%d" % (name, i)) for i in range(n)]
                self.busy = [False] * n
                self.i = 0
                self.flush = {}

            def get(self):
                n = len(self.t)
                for k in range(n):
                    j = (self.i + k) % n
                    if not self.busy[j]:
                        break
                else:
                    assert self.flush, "pool exhausted"
                    j = next(iter(self.flush))
                    self.flush.pop(j)()
                self.i = j + 1
                self.busy[j] = True
                return self.t[j], self.R[j], j

            def release(self, j):
                self.busy[j] = False

            def nfree(self):
                return sum(1 for x in self.busy if not x)

            def defer(self, j, fn):
                def run():
                    fn()
                    self.busy[j] = False
                self.flush[j] = run

            def flush_all(self):
                for j in list(self.flush):
                    self.flush.pop(j)()

        poolA = _Pool("pA", 3)
        poolB = _Pool("pB", 3)

        poolX = _Pool("xn", 5, BF16)

        cst_f = sb("cst_f", [128, CW], F32); R_cst = P.res("cst")
        prm_f = sb("prm_f", [128, PW], F32); R_prm = P.res("prm")
        fg_t = sb("fg_t", [128, D], F32); R_fg = P.res("fg")
        R_id = P.res("ident")
        cbf = sb("cbf", [128, 512 + 256], BF16); R_cbf = P.res("cbf"); R_cbfi = P.res("cbfi")
        nbgk = sb("nbgk", [128, 2], F32); R_nbgk = P.res("nbgk")
        st1 = sb("st1", [128, 4, 4], F32); R_st1 = [P.res("st1_%d" % i) for i in range(4)]
        st1_i = [0]
        st2 = sb("st2", [128, 8], F32); R_st2 = P.res("st2")
        hTb = [sb("hT%d" % i, [128, KC, 512], BF16) for i in range(2)]
        R_hTb = [[P.res("hT%d_%d" % (i, t)) for t in range(4)] for i in range(2)]
        o_sb = sb("o_sb", [128, 4, 128], F32); R_osb = P.res("o_sb")
        qp = [[[sb("qp%d%d%d" % (b, p, h), [128, 512], BF16) for h in range(2)] for p in range(2)] for b in range(1)]
        R_qp = [[[P.res("qp%d%d%d" % (b, p, h)) for h in range(2)] for p in range(2)] for b in range(1)]
        kT = [sb("kT%d" % b, [128, 2, 512], BF16) for b in range(1)]
        R_kT = [[P.res("kT%d%d" % (b, p)) for p in range(2)] for b in range(1)]
        vt = [sb("vt%d" % b, [128, 4, 512], BF16) for b in range(1)]
        R_vt = [[P.res("vt%d%d" % (b, t)) for t in range(4)] for b in range(1)]
        sg = [sb("sg%d" % b, [128, 4, 512], BF16) for b in range(1)]
        R_sg = [[P.res("sg%d%d" % (b, h)) for h in range(4)] for b in range(1)]
        el_p = [sb("el_p%d" % b, [128, 2, 4], F32) for b in range(1)]
        R_elp = [P.res("elp%d" % b) for b in range(1)]
        mixC = [sb("mixC%d" % b, [128, 4, 512], BF16) for b in range(1)]
        R_mixC = [[P.res("mixC%d%d" % (b, j)) for j in range(4)] for b in range(1)]
        mixG = sb("mixG", [128, 4, 512], BF16)
        R_mixG = [P.res("mixG%d" % t) for t in range(4)]
        kt = sb("kt", [128, 4, 256], BF16); R_kt = P.res("kt")
        el_s = sb("el_s", [128, 32, 2], F32); R_els = P.res("els")
        lr_bf = sb("lr_bf", [128, 512], BF16); R_lr = P.res("lr")
        e1b = [sb("e1_%d" % i, [128, 512], F32) for i in range(2)]; R_e1b = [P.res("e1_0"), P.res("e1_1")]
        Bcb = [sb("Bc%d" % i, [128, 512], F32) for i in range(2)]; R_Bcb = [P.res("Bc0"), P.res("Bc1")]
        ebq = [sb("ebq%d" % p, [128, 512], F32) for p in range(2)]; R_ebq = [P.res("ebq0"), P.res("ebq1")]
        enb = [sb("enb%d" % p, [128, 512], F32) for p in range(2)]; R_enb = [P.res("enb0"), P.res("enb1")]
        u_sb = sb("u_sb", [128, 512], F32); R_u = P.res("u")
        hp_p = sb("hp_p", [128, 4, 514], F32); R_hpp = [P.res("hpp%d" % j) for j in range(4)]
        hp_s = sb("hp_s", [128, 4, 32, 6], F32); R_hps = [P.res("hps%d" % j) for j in range(4)]
        tA = sb("tA", [128, 512], F32); R_tA = P.res("tA")
        tB = sb("tB", [128, 512], F32); R_tB = P.res("tB")
        sgt = sb("sgt", [128, 512], F32); R_sgt = P.res("sgt")
        ATm = [sb("ATm%d" % i, [128, 4, 128], BF16) for i in range(2)]; R_ATm = [P.res("ATm0"), P.res("ATm1")]
        S = sb("S", [128, 2, 128], F32); R_S = P.res("S")
        SUe = [sb("SUe%d" % i, [128, 2, 128], F32) for i in range(2)]; R_SUe = [P.res("SUe0"), P.res("SUe1")]
        S_bf = [sb("S_bf%d" % i, [128, 2, 128], BF16) for i in range(2)]; R_Sbf = [P.res("Sbf0"), P.res("Sbf1")]
        sq = sb("sq", [128, 512], BF16); R_sq = P.res("sq")
        rstd = sb("rstd", [128, 4, 128], F32); R_rstd = P.res("rstd")

        NBIG = 3
        big = [ps("big%d" % i, [128, 512], F32) for i in range(NBIG)]
        R_big = [P.res("big%d" % i, True) for i in range(NBIG)]
        big_i = [0]

        def big_next():
            i = big_i[0] % NBIG
            big_i[0] += 1
            return big[i], R_big[i]

        NSM = 2
        sml = [ps("sml%d" % i, [128, 4, 128], F32) for i in range(NSM)]
        R_sml = [P.res("sml%d" % i, True) for i in range(NSM)]
        sml_i = [0]

        def sml_next():
            i = sml_i[0] % NSM
            sml_i[0] += 1
            return sml[i], R_sml[i]

        TRb = [ps("TR%d" % i, [128, 8, 128], BF16) for i in range(2)]; R_TR = [P.res("TR0", True), P.res("TR1", True)]
        tr_i = [0]
        OTb = [ps("OT0", [128, 4, 128], F32)] * 2; R_OTb = [P.res("OT0", True)] * 2
        ot_i = [0]

        ident_bf = cbf[:, 0:128]
        ones_bf = cbf[:, 128:256]
        maskP_bf = cbf[:, 256:384]
        maskS_bf = cbf[:, 384:512]
        wgk_bf = cbf[:, 512:768]
        ident_f = cst_f[:, CO_ID:CO_ID + 128]

        g_b = prm_f[:, PO_G:PO_G + 8].unsqueeze(2).broadcast_to([128, 8, 128])
        gg = prm_f[:, PO_GG:PO_GG + 1]
        fg = fg_t[:]

        use_order = [C_LR, "v"] + list(range(C_G, C_G + 4)) + [C_Q, C_Q + 1, C_K, C_K + 1] + list(range(C_CONV, NCH)) + ["o0", "o1"]
        w_next = [0]

        def issue_weights(n):
            while n > 0 and w_next[0] < len(use_order):
                c = use_order[w_next[0]]
                w_next[0] += 1
                n -= 1
                if c == "v":
                    P.dma(lambda e: e.dma_start(out=w_v_bf[:], in_=w_v_d.rearrange("p (k n) -> p k n", k=KC)), R_win[C_V],
                          writes=R_win[C_V:C_V + 4], queue="pool")
                elif c in ("o0", "o1"):
                    hf = int(c[1])
                    P.dma(lambda e, hf=hf: e.dma_start(out=w_out_bf[:, hf, :, :], in_=w_out_d[:, hf, :].rearrange("p (k n) -> p k n", k=KC)),
                          R_wout[hf], writes=[R_wout[hf]], queue="pool")
                else:
                    P.dma(lambda e, c=c: e.dma_start(out=w_in_bf[:, wslot(c), :, :].rearrange("p k n -> p (k n)"), in_=w_in_d[:, c, :]),
                          R_win[c], writes=[R_win[c]], queue="pool")

        def need_w(kind, c):
            pass

        xready = {}

        xdma = {}

        def x_dma(tok0, t):
            xb, Rx, xj = poolA.get()
            r0 = tok0 + t * 128
            P.dma(lambda e: e.dma_start(out=xb[:], in_=x_all[r0:r0 + 128, :]), Rx, writes=[Rx])
            xdma[(tok0, t)] = (xb, Rx, xj)

        def x_load(tok0, t):
            if (tok0, t) not in xdma:
                x_dma(tok0, t)
            xb, Rx, xj = xdma.pop((tok0, t))
            xs, Rxs, sj = poolX.get()
            k = st1_i[0] % 4
            st1_i[0] += 1
            sk, Rk = st1[:, k, :], R_st1[k]
            P.op("act", lambda e: e.activation(out=xs[:], in_=xb[:], func=AF.Square, accum_out=sk[:, 0:1]),
                 reads=[Rx], writes=[Rxs, Rk])
            P.op("act", lambda e: e.activation(out=sk[:, 1:2], in_=sk[:, 0:1], func=AF.Ln, scale=1.0 / D, bias=EPS),
                 reads=[Rk], writes=[Rk])
            P.op("act", lambda e: e.activation(out=sk[:, 2:3], in_=sk[:, 1:2], func=AF.Exp, scale=-0.5),
                 reads=[Rk], writes=[Rk])
            P.op("pool", lambda e: e.tensor_scalar(out=xs[:], in0=xb[:], scalar1=sk[:, 2:3], scalar2=0.0,
                                                   op0=ALU.mult, op1=ALU.add),
                 reads=[Rx, Rk], writes=[Rxs])
            poolA.release(xj)
            xready[(tok0, t)] = (xs, Rxs, sj)

        xdone = set()

        def x_tr(tok0, t, hb):
            if (tok0, t) in xdone:
                return
            xdone.add((tok0, t))
            if (tok0, t) not in xready:
                x_load(tok0, t)
            xs, Rxs, sj = xready.pop((tok0, t))
            ti = tr_i[0] % 2
            tr_i[0] += 1
            P.op("pe", lambda e: [e.transpose(out=TRb[ti][:, kc, :], in_=xs[:, kc * 128:(kc + 1) * 128], identity=ident_bf)
                                  for kc in range(KC)],
                 reads=[Rxs, R_cbfi], writes=[R_TR[ti]])
            P.op("dve", lambda e: e.tensor_tensor(out=hTb[hb][:, :, t * 128:(t + 1) * 128], in0=TRb[ti][:], in1=g_b, op=ALU.mult),
                 reads=[R_TR[ti], R_prm], writes=[R_hTb[hb][t]])
            poolX.release(sj)

        def x_prep(tok0, NT, hb):
            for t in range(NT):
                if (tok0, t) not in xdone:
                    x_tr(tok0, t, hb)
                    yield

        cur_hb = [0]

        def inproj_fm(c, T, NT):
            hT, R_hT = hTb[cur_hb[0]], R_hTb[cur_hb[0]]
            bank, Rb = big_next()
            P.op("pe", lambda e: [e.matmul(out=bank[:, 0:T], lhsT=w_in_bf[:, wslot(c), kc, :], rhs=hT[:, kc, 0:T],
                                            start=(kc == 0), stop=(kc == KC - 1)) for kc in range(KC)],
                 reads=[R_win[c]] + R_hT[:NT], writes=[Rb])
            return bank, Rb

        def silu_from_psum(bank, Rb, T, out_ap, R_out):
            P.op("act", lambda e: e.activation(out=sgt[:, 0:T], in_=bank[:, 0:T], func=AF.Exp, scale=-1.0), reads=[Rb], writes=[R_sgt])
            P.op("act", lambda e: e.activation(out=sgt[:, 0:T], in_=sgt[:, 0:T], func=AF.Ln, bias=1.0), reads=[R_sgt], writes=[R_sgt])
            P.op("act", lambda e: e.activation(out=sgt[:, 0:T], in_=sgt[:, 0:T], func=AF.Exp, scale=-1.0), reads=[R_sgt], writes=[R_sgt])
            P.op("dve", lambda e: e.tensor_tensor(out=out_ap, in0=bank[:, 0:T], in1=sgt[:, 0:T], op=ALU.mult),
                 reads=[Rb, R_sgt], writes=[R_out])

        def kt_transposes(b, NT):
            ti = tr_i[0] % 2
            tr_i[0] += 1
            tr_i[0] += 1
            P.op("pe", lambda e: [e.transpose(out=TRb[ti][:, t * 2 + p, :], in_=kT[b][:, p, t * 128:(t + 1) * 128], identity=ident_bf)
                                  for t in range(NT) for p in range(2)],
                 reads=R_kT[b] + [R_cbfi], writes=[R_TR[ti]])
            P.op("act", lambda e: e.copy(out=kt[:, 0:NT, :], in_=TRb[ti][:, 0:2 * NT, :].rearrange("p (t q) n -> p t (q n)", q=2)),
                 reads=[R_TR[ti]], writes=[R_kt])

        def stage1(b, tok0, NT, sample, last_prompt):
            T = NT * 128
            rm = cst_f[:, CO_RMS:CO_RMS + 128] if sample else cst_f[:, CO_RMP:CO_RMP + 512]
            bank, Rb = inproj_fm(C_LR, T, NT)
            P.op("act", lambda e, bank=bank: e.copy(out=lr_bf[:, 0:T], in_=bank[:, 0:T]), reads=[Rb], writes=[R_lr])
            yield
            hT, R_hT = hTb[cur_hb[0]], R_hTb[cur_hb[0]]

            def v_step(t):
                bank, Rb = big_next()
                P.op("pe", lambda e: [e.matmul(out=bank[:, :], lhsT=hT[:, kc, t * 128:(t + 1) * 128],
                                               rhs=w_v_bf[:, kc, :],
                                               start=(kc == 0), stop=(kc == KC - 1)) for kc in range(KC)],
                     reads=R_win[C_V:C_V + 4] + [R_hT[t]], writes=[Rb])
                P.op("act", lambda e: e.copy(out=vt[b][:, t, :], in_=bank[:, :]), reads=[Rb], writes=[R_vt[b][t]])

            for t in range(min(NT, 2)):
                v_step(t)
                yield
            gbanks = []
            for p in range(2):
                bank, Rb = big_next()
                gbanks.append((bank, Rb))
                P.op("pe", lambda e, bank=bank, p=p: e.matmul(out=bank[:, 0:T], lhsT=wgk_bf[:, p * 128:(p + 1) * 128],
                                                               rhs=lr_bf[:, 0:T], start=True, stop=True),
                     reads=[R_cbf, R_lr], writes=[Rb])
            for p in range(2):
                bank, Rb = gbanks[p]
                e1, R_e1 = e1b[p], R_e1b[p]
                P.op("act", lambda e, bank=bank, p=p, e1=e1: e.activation(out=e1[:, 0:T], in_=bank[:, 0:T], func=AF.Exp,
                                                                           scale=-1.0, bias=nbgk[:, p:p + 1]),
                     reads=[Rb, R_nbgk], writes=[R_e1])
                P.op("act", lambda e, e1=e1: e.activation(out=e1[:, 0:T], in_=e1[:, 0:T], func=AF.Ln, bias=1.0),
                     reads=[R_e1], writes=[R_e1])
                P.op("dve", lambda e, p=p, e1=e1: e.tensor_tensor_scan(out=Bcb[p][:, 0:T], data0=rm[:, 0:T], data1=e1[:, 0:T],
                                                                        initial=0.0, op0=ALU.mult, op1=ALU.add),
                     reads=[R_e1, R_cst], writes=[R_Bcb[p]])
            yield
            for p in range(2):
                Bc, R_Bc = Bcb[p], R_Bcb[p]
                P.op("act", lambda e, p=p, Bc=Bc: e.activation(out=ebq[p][:, 0:T], in_=Bc[:, 0:T], func=AF.Exp,
                                                                scale=-1.0 / 16.0, bias=LN_QS),
                     reads=[R_Bc], writes=[R_ebq[p]])
                P.op("act", lambda e, p=p, Bc=Bc: e.activation(out=enb[p][:, 0:T], in_=Bc[:, 0:T], func=AF.Exp, scale=1.0 / 16.0),
                     reads=[R_Bc], writes=[R_enb[p]])
                if sample:
                    P.op("act", lambda e, p=p, Bc=Bc: e.activation(out=el_s[:, :, p], in_=Bc[:, LS - 1:128:LS], func=AF.Exp,
                                                                    scale=-1.0 / 16.0),
                         reads=[R_Bc], writes=[R_els])
                else:
                    P.op("act", lambda e, p=p, Bc=Bc: e.activation(out=el_p[b][:, p, 0:NT], in_=Bc[:, 127:T:128], func=AF.Exp,
                                                                    scale=-1.0 / 16.0),
                         reads=[R_Bc], writes=[R_elp[b]])
            yield
            for h in range(4):
                bank, Rb = inproj_fm(C_G + h, T, NT)
                silu_from_psum(bank, Rb, T, sg[b][:, h, 0:T], R_sg[b][h])
                yield
                if h < 2 and 2 + h < NT:
                    v_step(2 + h)
                    yield
            for p in range(2):
                bank, Rb = inproj_fm(C_Q + p, T, NT)
                for hh in range(2):
                    rows = slice(hh * 64, (hh + 1) * 64)
                    P.op("dve", lambda e, bank=bank, p=p, hh=hh, rows=rows: e.tensor_tensor(
                        out=qp[b][p][hh][rows, 0:T], in0=bank[rows, 0:T], in1=ebq[p][rows, 0:T], op=ALU.mult),
                        reads=[Rb, R_ebq[p]], writes=[R_qp[b][p][hh]])
                yield
            for p in range(2):
                bank, Rb = inproj_fm(C_K + p, T, NT)
                P.op("dve", lambda e, bank=bank, p=p: e.tensor_tensor(out=kT[b][:, p, 0:T], in0=bank[:, 0:T], in1=enb[p][:, 0:T],
                                                                      op=ALU.mult),
                     reads=[Rb, R_enb[p]], writes=[R_kT[b][p]])
                yield
            kt_transposes(b, NT)
            yield "SPLIT"
            for j in range(4):
                if sample:
                    hv = hp_s[:, j, :, :]
                    Rh = R_hps[j]

                    def win(o, hv=hv):
                        return hv[:, :, o:o + LS]

                    def v3(ap):
                        return ap.rearrange("p (s l) -> p s l", l=LS)
                else:
                    hv = hp_p[:, j, :]
                    Rh = R_hpp[j]

                    def win(o, hv=hv):
                        return hv[:, o:o + T]

                    def v3(ap):
                        return ap
                cb = C_CONV + 4 * j
                cwj = prm_f[:, PO_CW + 3 * j:PO_CW + 3 * j + 3]
                bank, Rb = inproj_fm(cb + 0, T, NT)
                P.op("act", lambda e, bank=bank: e.copy(out=u_sb[:, 0:T], in_=bank[:, 0:T]), reads=[Rb], writes=[R_u])
                yield
                bank, Rb = inproj_fm(cb + 1, T, NT)
                P.op("dve", lambda e, bank=bank, win=win, v3=v3: e.tensor_tensor(out=win(2), in0=v3(bank[:, 0:T]), in1=v3(u_sb[:, 0:T]),
                                                                              op=ALU.mult),
                     reads=[Rb, R_u], writes=[Rh])
                P.op("pool", lambda e, win=win, v3=v3, cwj=cwj: e.tensor_scalar(out=v3(tA[:, 0:T]), in0=win(0), scalar1=cwj[:, 0:1],
                                                                             scalar2=0.0, op0=ALU.mult, op1=ALU.add),
                     reads=[Rh, R_prm], writes=[R_tA])
                yield
                bank, Rb = inproj_fm(cb + 2, T, NT)
                P.op("dve", lambda e, win=win, v3=v3, cwj=cwj: e.scalar_tensor_tensor(out=v3(tB[:, 0:T]), in0=win(1), scalar=cwj[:, 1:2],
                                                                                   in1=v3(tA[:, 0:T]), op0=ALU.mult, op1=ALU.add),
                     reads=[Rh, R_tA, R_prm], writes=[R_tB])
                P.op("dve", lambda e, win=win, v3=v3, cwj=cwj: e.scalar_tensor_tensor(out=v3(tA[:, 0:T]), in0=win(2), scalar=cwj[:, 2:3],
                                                                                   in1=v3(tB[:, 0:T]), op0=ALU.mult, op1=ALU.add),
                     reads=[Rh, R_tB, R_prm], writes=[R_tA])
                if not sample and not last_prompt:
                    P.op("pool", lambda e, hv=hv: e.tensor_copy(out=hv[:, 0:2], in_=hv[:, T:T + 2]), reads=[Rh], writes=[Rh])
                silu_from_psum(bank, Rb, T, sgt[:, 0:T], R_sgt)
                P.op("pool", lambda e: e.tensor_tensor(out=tA[:, 0:T], in0=tA[:, 0:T], in1=sgt[:, 0:T], op=ALU.mult),
                     reads=[R_tA, R_sgt], writes=[R_tA])
                yield
                bank, Rb = inproj_fm(cb + 3, T, NT)
                P.op("dve", lambda e, bank=bank, j=j: e.tensor_tensor(out=mixC[b][:, j, 0:T], in0=bank[:, 0:T], in1=tA[:, 0:T],
                                                                      op=ALU.mult),
                     reads=[Rb, R_tA], writes=[R_mixC[b][j]])
                yield
            if sample or last_prompt:
                bank, Rb = big_next()
                co_sb, R_co, cj = poolA.get()
                if sample:
                    cn_t, R_cn, cnj = poolA.get()
                    cn_sb = cn_t[:, 0:256].rearrange("p (j n) -> p j n", j=4)
                    for j in range(4):
                        P.op("act", lambda e, j=j: e.copy(out=cn_sb[:, j, :].rearrange("p (s r) -> p s r", r=2), in_=hp_s[:, j, :, LS:LS + 2]),
                             reads=[R_hps[j]], writes=[R_cn])
                    P.op("pe", lambda e, bank=bank: [e.transpose(out=bank[0:64, j * 128:(j + 1) * 128], in_=cn_sb[:, j, :],
                                                                  identity=ident_f) for j in range(4)],
                         reads=[R_cn, R_id], writes=[Rb])
                    poolA.release(cnj)
                    P.op("act", lambda e, bank=bank: e.copy(out=co_sb[0:64, 0:512], in_=bank[0:64, :]), reads=[Rb], writes=[R_co])
                    P.dma(lambda e: e.dma_start(out=conv_s_d, in_=co_sb[0:32, 0:512]), R_co, reads=[R_co], is_output=True)
                    poolA.release(cj)
                else:
                    P.op("pe", lambda e, bank=bank: [e.transpose(out=bank[0:32, j * 128:(j + 1) * 128], in_=hp_p[:, j, T - 30:T + 2],
                                                                  identity=ident_f) for j in range(4)],
                         reads=R_hpp + [R_id], writes=[Rb])
                    P.op("act", lambda e, bank=bank: e.copy(out=co_sb[0:32, 0:512], in_=bank[0:32, :]), reads=[Rb], writes=[R_co])
                    P.dma(lambda e: e.dma_start(out=conv_p_d, in_=co_sb[30:32, 0:512]), R_co, reads=[R_co], is_output=True)
                    poolA.release(cj)
                yield

        pend_mix = []

        def flush_mix():
            while pend_mix:
                pend_mix.pop(0)()

        def norm_a(OT, R_OT):
            P.op("act", lambda e: e.activation(out=sq[:], in_=OT[:].rearrange("p h n -> p (h n)"), func=AF.Square),
                 reads=[R_OT], writes=[R_sq])
            P.op("act", lambda e: e.copy(out=o_sb[:], in_=OT[:]), reads=[R_OT], writes=[R_osb])

        def norm_gate(b, c, cs, OT, R_OT):
            MS, R_MS = sml_next()
            P.op("pe", lambda e: e.matmul(out=MS[:].rearrange("p h n -> p (h n)"), lhsT=ones_bf, rhs=sq[:], start=True, stop=True),
                 reads=[R_sq, R_cbf], writes=[R_MS])
            P.op("act", lambda e: e.activation(out=rstd[:], in_=MS[:], func=AF.Ln, bias=EPS), reads=[R_MS], writes=[R_rstd])
            P.op("act", lambda e: e.activation(out=rstd[:], in_=rstd[:], func=AF.Exp, scale=-0.5), reads=[R_rstd], writes=[R_rstd])
            P.op("pool", lambda e: e.tensor_tensor(out=rstd[:], in0=rstd[:], in1=sg[b][:, :, cs], op=ALU.mult),
                 reads=[R_rstd] + R_sg[b], writes=[R_rstd])
            pend_mix.append(lambda: P.op("dve", lambda e: e.scalar_tensor_tensor(out=mixG[:, :, cs], in0=o_sb[:], scalar=gg, in1=rstd[:],
                                                                                op0=ALU.mult, op1=ALU.mult),
                                         reads=[R_osb, R_rstd, R_prm], writes=[R_mixG[c]]))

        def out_proj(b, tok0, t, sample):
            flush_mix()
            for c in range(8):
                need_w("out", c)
            xr, Rx, bi = poolB.get()
            r0 = tok0 + t * 128
            P.dma(lambda e: e.dma_start(out=xr[:], in_=x_all[r0:r0 + 128, :]), Rx, writes=[Rx])
            for hf in range(2):
                bank, Rb = big_next()

                def mm(e, bank=bank, hf=hf):
                    ins = []
                    for kc in range(KC):
                        lhs = mixG[:, kc, t * 128:(t + 1) * 128] if kc < 4 else mixC[b][:, kc - 4, t * 128:(t + 1) * 128]
                        ins.append(e.matmul(out=bank[:, :], lhsT=lhs, rhs=w_out_bf[:, hf, kc, :],
                                            start=(kc == 0), stop=(kc == KC - 1)))
                    return ins
                P.op("pe", mm, reads=[R_mixG[t]] + R_mixC[b] + [R_wout[hf]], writes=[Rb])
                P.op("dve", lambda e, bank=bank, hf=hf: e.tensor_tensor(out=xr[:, hf * 512:(hf + 1) * 512], in0=bank[:, :],
                                                                        in1=xr[:, hf * 512:(hf + 1) * 512], op=ALU.add),
                     reads=[Rb, Rx], writes=[Rx])
                yield
            jk, R_jk, jj = poolX.get()
            P.op("act", lambda e: e.activation(out=jk[:], in_=xr[:], func=AF.Square, accum_out=st2[:, 0:1]),
                 reads=[Rx], writes=[R_jk, R_st2])
            poolX.release(jj)
            P.op("act", lambda e: e.activation(out=st2[:, 1:2], in_=st2[:, 0:1], func=AF.Ln, scale=1.0 / D, bias=EPS),
                 reads=[R_st2], writes=[R_st2])
            P.op("act", lambda e: e.activation(out=st2[:, 2:3], in_=st2[:, 1:2], func=AF.Exp, scale=-0.5),
                 reads=[R_st2], writes=[R_st2])
            P.op("act", lambda e: e.activation(out=xr[:], in_=xr[:], func=AF.Copy, scale=st2[:, 2:3]),
                 reads=[Rx, R_st2], writes=[Rx])
            P.op("pool", lambda e: e.tensor_tensor(out=xr[:], in0=xr[:], in1=fg, op=ALU.mult),
                 reads=[Rx, R_fg], writes=[Rx])
            def store():
                if sample:
                    P.dma(lambda e: e.dma_start(out=y_d[SEQ:SEQ + NSEQ_S * LS, :], in_=xr[0:NSEQ_S * LS, :]), Rx, reads=[Rx],
                          is_output=True)
                else:
                    P.dma(lambda e: e.dma_start(out=y_d[r0:r0 + 128, :], in_=xr[:]), Rx, reads=[Rx], is_output=True)
            poolB.flush_all()
            poolB.defer(bi, store)
            yield

        def gla_prompt(b, tok0, NT, last_prompt):
            Rq = [R_qp[b][0][0], R_qp[b][0][1], R_qp[b][1][0], R_qp[b][1][1]]
            ots = {}

            def A(c):
                cs = slice(c * 128, (c + 1) * 128)
                AT, R_AT = sml_next()
                P.op("pe", lambda e: [e.matmul(out=AT[:, h, :], lhsT=kT[b][:, h // 2, cs], rhs=qp[b][h // 2][h % 2][:, cs],
                                               start=True, stop=True) for h in range(4)],
                     reads=R_kT[b] + Rq, writes=[R_AT])
                P.op("dve", lambda e: e.tensor_tensor(out=ATm[c % 2][:], in0=AT[:], in1=maskP_bf.unsqueeze(1).broadcast_to([128, 4, 128]),
                                                      op=ALU.mult),
                     reads=[R_AT, R_cbf], writes=[R_ATm[c % 2]])

            def U(c):
                SU, R_SU = sml_next()
                P.op("pe", lambda e: [e.matmul(out=SU[:, h, :], lhsT=kt[:, c, (h // 2) * 128:(h // 2 + 1) * 128],
                                               rhs=vt[b][:, c, h * 128:(h + 1) * 128], start=True, stop=True) for h in range(4)],
                     reads=[R_kt, R_vt[b][c]], writes=[R_SU])
                for hh in range(2):
                    rows = slice(hh * 64, (hh + 1) * 64)
                    P.op("dve", lambda e, hh=hh, rows=rows: e.tensor_tensor(
                        out=SUe[c % 2][rows, :, :], in0=SU[rows, hh::2, :], in1=el_p[b][rows, :, c:c + 1].broadcast_to([64, 2, 128]), op=ALU.mult),
                        reads=[R_SU, R_elp[b]], writes=[R_SUe[c % 2]])

            def O(c):
                cs = slice(c * 128, (c + 1) * 128)
                oi = ot_i[0] % 2
                ot_i[0] += 1
                OT, R_OT = OTb[oi], R_OTb[oi]
                ots[c] = (OT, R_OT)
                si = sbf_i[0]

                def o_mm(e):
                    ins = []
                    for h in range(4):
                        ins.append(e.matmul(out=OT[:, h, :], lhsT=vt[b][:, c, h * 128:(h + 1) * 128], rhs=ATm[c % 2][:, h, :],
                                            start=True, stop=False))
                        ins.append(e.matmul(out=OT[:, h, :], lhsT=S_bf[si][:, h // 2, :], rhs=qp[b][h // 2][h % 2][:, cs],
                                            start=False, stop=True))
                    return ins
                P.op("pe", o_mm, reads=[R_vt[b][c], R_ATm[c % 2], R_Sbf[si]] + Rq, writes=[R_OT])
                norm_a(OT, R_OT)
                for p in range(2):
                    P.op("dve", lambda e, p=p: e.scalar_tensor_tensor(out=S[:, p, :], in0=S[:, p, :], scalar=el_p[b][:, p, c:c + 1],
                                                                     in1=SUe[c % 2][:, p, :], op0=ALU.mult, op1=ALU.add),
                         reads=[R_S, R_SUe[c % 2], R_elp[b]], writes=[R_S])
                sn = 1 - si
                pend_mix.append(lambda: P.op("act", lambda e: e.copy(out=S_bf[sn][:], in_=S[:]), reads=[R_S], writes=[R_Sbf[sn]]))
                sbf_i[0] = sn
                if last_prompt and c == NT - 1:
                    P.dma(lambda e: [e.dma_start(out=gla_p_d.rearrange("(pr hh) k v -> hh k pr v", hh=2)[hh],
                                                 in_=S[hh * 64:(hh + 1) * 64, :, :]) for hh in range(2)],
                          R_S, reads=[R_S], n=2, is_output=True)

            def N(c):
                OT, R_OT = ots.pop(c)
                norm_gate(b, c, slice(c * 128, (c + 1) * 128), OT, R_OT)

            sched = []
            for c in range(NT):
                sched.append(("A", c))
                if c >= 1:
                    sched.append(("N", c - 1))
                sched.append(("U", c))
                sched.append(("O", c))
            deferred_last["f"] = lambda: N(NT - 1)
            for kind, c in sched:
                flush_mix()
                if kind == "A":
                    A(c)
                    yield
                elif kind == "U":
                    U(c)
                    yield
                elif kind == "O":
                    O(c)
                    yield
                elif kind == "N":
                    N(c)
                    yield

        def gla_sample(b, tok0):
            Rq = [R_qp[b][0][0], R_qp[b][0][1], R_qp[b][1][0], R_qp[b][1][1]]
            cs = slice(0, 128)
            AT, R_AT = sml_next()
            P.op("pe", lambda e: [e.matmul(out=AT[:, h, :], lhsT=kT[b][:, h // 2, cs], rhs=qp[b][h // 2][h % 2][:, cs],
                                           start=True, stop=True) for h in range(4)],
                 reads=R_kT[b] + Rq, writes=[R_AT])
            P.op("dve", lambda e: e.tensor_tensor(out=ATm[0][:], in0=AT[:], in1=maskS_bf.unsqueeze(1).broadcast_to([128, 4, 128]),
                                                  op=ALU.mult),
                 reads=[R_AT, R_cbf], writes=[R_ATm[0]])
            oi = ot_i[0] % 2
            ot_i[0] += 1
            OT, R_OT = OTb[oi], R_OTb[oi]
            yield
            first = [True]
            pre = {}
            prev_store = [None]

            def s0_issue(G):
                S0g_t, R_S0g, gj = poolA.get()
                S0g = S0g_t[:].rearrange("p (s a v) -> p s a v", s=4, a=2)
                P.dma(lambda e, G=G, S0g=S0g: [e.dma_start(out=S0g[hh * 64:(hh + 1) * 64, :, :, :],
                                                           in_=gla_in.rearrange("s (pr hh) k v -> hh k s pr v", hh=2)[hh][:, 4 * G:4 * G + 4])
                                               for hh in range(2)],
                      R_S0g, writes=[R_S0g], n=2)
                pre[G] = (S0g, R_S0g, gj)
            kms = {}

            def km_prep(G):
                kmt, R_kmask, kmj = poolX.get()
                kmask = kmt[:].rearrange("p (s n) -> p s n", s=4)
                sel = cst_f[:, CO_SEL + 4 * G:CO_SEL + 4 * G + 4]
                P.op("pool", lambda e: e.tensor_tensor(out=kmask, in0=kt[:, 0:1, :].broadcast_to([128, 4, 256]),
                                                       in1=sel.unsqueeze(2).broadcast_to([128, 4, 256]), op=ALU.mult),
                     reads=[R_kt, R_cst], writes=[R_kmask])
                kms[G] = (kmask, R_kmask, kmj)

            s0_issue(0)
            km_prep(0)
            for G in range(4):
                if G + 1 < 4:
                    s0_issue(G + 1)
                    km_prep(G + 1)
                S0g, R_S0g, gj = pre.pop(G)
                sbt, R_S0gbf, sbj = poolX.get()
                S0g_bf = sbt[:].rearrange("p (s a v) -> p s a v", s=4, a=2)
                P.op("act", lambda e, S0g=S0g, S0g_bf=S0g_bf: e.copy(out=S0g_bf, in_=S0g), reads=[R_S0g], writes=[R_S0gbf])

                def inter(e, G=G, S0g_bf=S0g_bf):
                    ins = []
                    for h in range(4):
                        for i in range(4):
                            s = 4 * G + i
                            ins.append(e.matmul(out=OT[:, h, s * LS:(s + 1) * LS], lhsT=S0g_bf[:, i, h // 2, :],
                                                rhs=qp[b][h // 2][h % 2][:, s * LS:(s + 1) * LS],
                                                start=first[0], stop=False, skip_group_check=True))
                            first[0] = False
                    return ins
                P.op("pe", inter, reads=[R_S0gbf] + Rq, writes=[R_OT])
                poolX.release(sbj)
                yield
                kmask, R_kmask, kmj = kms.pop(G)
                for i in range(4):
                    SU, R_SU = sml_next()
                    P.op("pe", lambda e, i=i, SU=SU, kmask=kmask: [e.matmul(out=SU[:, h, :], lhsT=kmask[:, i, (h // 2) * 128:(h // 2 + 1) * 128],
                                                                            rhs=vt[b][:, 0, h * 128:(h + 1) * 128], start=True, stop=True)
                                                                   for h in range(4)],
                         reads=[R_kmask, R_vt[b][0]], writes=[R_SU])
                    for hh in range(2):
                        rows = slice(hh * 64, (hh + 1) * 64)
                        P.op("dve", lambda e, i=i, hh=hh, rows=rows, SU=SU, S0g=S0g: e.tensor_tensor(
                            out=S0g[rows, i, :, :], in0=SU[rows, hh::2, :], in1=S0g[rows, i, :, :], op=ALU.add),
                            reads=[R_SU, R_S0g], writes=[R_S0g])
                    yield
                poolX.release(kmj)
                P.op("pool", lambda e, G=G, S0g=S0g: e.tensor_tensor(out=S0g, in0=S0g,
                                                                     in1=el_s[:, 4 * G:4 * G + 4, :].unsqueeze(3).broadcast_to([128, 4, 2, 128]),
                                                                     op=ALU.mult),
                     reads=[R_S0g, R_els], writes=[R_S0g])
                def st(G=G, S0g=S0g, R_S0g=R_S0g):
                    P.dma(lambda e: [e.dma_start(out=gla_s_d.rearrange("s (pr hh) k v -> hh k s pr v", hh=2)[hh][:, 4 * G:4 * G + 4],
                                                 in_=S0g[hh * 64:(hh + 1) * 64, :, :, :]) for hh in range(2)],
                          R_S0g, reads=[R_S0g], n=2, is_output=True)
                poolA.defer(gj, st)
                yield
                poolA.flush_all()
            P.op("pe", lambda e: [e.matmul(out=OT[:, h, :], lhsT=vt[b][:, 0, h * 128:(h + 1) * 128], rhs=ATm[0][:, h, :],
                                           start=False, stop=(h == 3), skip_group_check=True) for h in range(4)],
                 reads=[R_vt[b][0], R_ATm[0]], writes=[R_OT])
            norm_a(OT, R_OT)
            yield
            deferred_last["f"] = lambda: norm_gate(b, 0, cs, OT, R_OT)

        deferred_last = {}

        def outproj_all(b, tok0, NT, sample):
            deferred_last.pop("f")()
            yield
            for t in range(NT):
                yield from out_proj(b, tok0, t, sample)
            poolB.flush_all()

        def conv_state_in():
            cs_t, R_csin, cj = poolA.get()
            cs_in = cs_t[0:32, 0:512]
            P.dma(lambda e: e.dma_start(out=cs_in, in_=conv_in), R_csin, writes=[R_csin])
            bank, Rb = big_next()
            P.op("pe", lambda e: [e.transpose(out=bank[:, j * 32:(j + 1) * 32], in_=cs_t[0:32, j * 128:(j + 1) * 128],
                                              identity=ident_f[0:32, 0:32]) for j in range(4)],
                 reads=[R_csin, R_id], writes=[Rb])
            poolA.release(cj)
            for j in range(4):
                P.op("dve", lambda e, j=j: e.tensor_copy(out=hp_s[:, j, 0:NSEQ_S, 0:2],
                                                         in_=bank[:, j * 32:(j + 1) * 32].rearrange("p (s r) -> p s r", r=2)),
                     reads=[Rb], writes=[R_hps[j]])
            yield


        def chain(*gens):
            for g in gens:
                yield from g

        def interleave(a, bgen, na=1, nb=1, extras=()):
            a_done = a is None
            b_done = bgen is None
            acc = 0
            extras = list(extras)
            every = max(1, (na - 2) // (len(extras) + 1)) if extras else 0
            ka = 0
            while not (a_done and b_done):
                if not a_done:
                    try:
                        next(a)
                    except StopIteration:
                        a_done = True
                    ka += 1
                    if extras and ka >= 1 and (ka - 1) % every == 0:
                        extras.pop(0)()
                acc += nb
                while (acc >= na or a_done) and not b_done:
                    acc -= na
                    try:
                        next(bgen)
                    except StopIteration:
                        b_done = True

        def n_steps1(NT, sample, lastp):
            return NT + (1 if sample else 0) + 1 + NT + 2 + 4 + 2 + 2 + 16 + (1 if (sample or lastp) else 0)

        def n_steps2(NT, sample):
            return (1 + 1 + 4 * 5 + 3 + 3) if sample else (1 + 4 * NT + 3 * NT)

        items = [(0, 4, False, False), (512, 4, False, False), (1024, 4, False, False), (1536, 4, False, True), (SEQ, 1, True, False)]
        P.dma(lambda e: e.dma_start(out=cst_f[:, 0:128], in_=cst_d[:, 0:128]), R_id, writes=[R_id])
        P.op("act", lambda e: e.copy(out=cbf[:, 0:128], in_=cst_f[:, CO_ID:CO_ID + 128]), reads=[R_id], writes=[R_cbfi])
        issue_weights(1)
        for t in range(items[0][1]):
            x_load(items[0][0], t)
        P.dma(lambda e: e.dma_start(out=prm_f[:], in_=prm_d), R_prm, writes=[R_prm])
        P.dma(lambda e: e.dma_start(out=cst_f[:, 128:CW], in_=cst_d[:, 128:CW]), R_cst, writes=[R_cst])
        c2, R_c2, c2j = poolA.get()
        P.dma(lambda e: e.dma_start(out=c2[:, 0:CW2], in_=cst2_d), R_c2, writes=[R_c2])
        P.op("act", lambda e: e.copy(out=cbf[:, 128:768], in_=c2[:, 0:CW2]), reads=[R_c2], writes=[R_cbf])
        poolA.release(c2j)
        issue_weights(100)
        P.dma(lambda e: e.dma_start(out=fg_t[:], in_=fg_d), R_fg, writes=[R_fg])
        P.op("act", lambda e: e.mul(out=nbgk[:], in_=prm_f[:, PO_BGK:PO_BGK + 2], mul=-1.0),
             reads=[R_prm], writes=[R_nbgk])
        for b in range(1):
            for p in range(2):
                for h in range(2):
                    P.op("dve", lambda e, t=qp[b][p][h]: e.memset(t[:], 0.0), writes=[R_qp[b][p][h]])
        P.op("dve", lambda e: e.memset(hp_p[:], 0.0), writes=R_hpp)
        P.op("dve", lambda e: e.memset(hp_s[:], 0.0), writes=R_hps)
        P.op("dve", lambda e: e.memset(S[:], 0.0), writes=[R_S])
        P.op("dve", lambda e: e.memset(S_bf[0][:], 0.0), writes=[R_Sbf[0]])
        sbf_i = [0]

        def run_until_split(g):
            for v in g:
                if v == "SPLIT":
                    return
                yield

        prev_out = None
        prev_nout = 1
        for i, (tok0, NT, sample, lastp) in enumerate(items):
            b = 0
            hb = i % 2
            cur_hb[0] = hb
            gens = [x_prep(tok0, NT, hb)]
            if sample:
                gens.append(conv_state_in())
            g1 = chain(*gens, stage1(b, tok0, NT, sample, lastp))
            nA = (1 if sample else 0) + 1 + NT + 2 + 4 + 2 + 2
            interleave(run_until_split(g1), prev_out, nA, prev_nout)
            extras = []
            if i + 1 < len(items):
                ntok0, nNT = items[i + 1][0], items[i + 1][1]

                def mk(t, ntok0=ntok0, nNT=nNT):
                    def f():
                        if t < nNT and poolA.nfree() >= 2:
                            x_dma(ntok0, t)
                        if t >= 1 and (ntok0, t - 1) in xdma and poolX.nfree() >= 2:
                            x_load(ntok0, t - 1)
                    return f

                def mk2(t, ntok0=ntok0, nhb=1 - hb):
                    def f():
                        if (ntok0, t) in xready:
                            x_tr(ntok0, t, nhb)
                    return f
                extras = [mk(t) for t in range(nNT + 1)] + [mk2(t) for t in range(nNT)]
            gla = gla_sample(b, tok0) if sample else gla_prompt(b, tok0, NT, lastp)
            nB = 16 + (1 if (sample or lastp) else 0)
            nG = (1 + 4 * 6 + 1) if sample else (4 * NT - 1)
            interleave(g1, gla, nB, nG, extras)
            prev_out = outproj_all(b, tok0, NT, sample)
            prev_nout = 3 * NT + 1
        interleave(None, prev_out)

        with nc.Block() as block:
            @block.sync
            def _(e):
                P.replay("sp", e)

            @block.scalar
            def _(e):
                P.replay("act", e)

            @block.vector
            def _(e):
                P.replay("dve", e)

            @block.gpsimd
            def _(e):
                P.replay("pool", e)

            @block.tensor
            def _(e):
                P.replay("pe", e)
    return nc


def _colmap():
    cols = []
    cols += list(range(1536, 1552)) + [-1] * 112
    cols += list(range(0, 256))
    cols += list(range(256, 512))
    cols += list(range(512, 1024))
    cols += list(range(1024, 1536))
    for j in range(4):
        cols += list(range(1552 + 128 * j, 1552 + 128 * (j + 1)))
        cols += list(range(2576 + 128 * j, 2576 + 128 * (j + 1)))
        cols += list(range(3088 + 128 * j, 3088 + 128 * (j + 1)))
        cols += list(range(2064 + 128 * j, 2064 + 128 * (j + 1)))
    assert len(cols) == NCH * 128
    return np.array(cols)


def _constants():
    cst = np.zeros((128, CW), np.float32)
    i = np.arange(128)
    cst[:, CO_ID:CO_ID + 128] = np.eye(128, dtype=np.float32)
    rmp = np.ones(512, np.float32); rmp[0::128] = 0.0
    rms = np.ones(128, np.float32); rms[0::LS] = 0.0
    cst[:, CO_RMP:CO_RMP + 512] = rmp[None, :]
    cst[:, CO_RMS:CO_RMS + 128] = rms[None, :]
    cst[:, CO_SEL:CO_SEL + 32] = (i[:, None] // LS == np.arange(32)[None, :]).astype(np.float32)
    return cst


def _constants2(w_gk_up):
    c2 = np.zeros((128, CW2), np.float32)
    i = np.arange(128)
    c2[:, C2_ONES:C2_ONES + 128] = 1.0 / 128.0
    c2[:, C2_MP:C2_MP + 128] = (i[None, :] >= i[:, None]).astype(np.float32)
    c2[:, C2_MS:C2_MS + 128] = ((i[None, :] >= i[:, None]) & (i[None, :] // LS == i[:, None] // LS)).astype(np.float32)
    c2[0:16, C2_WGK:C2_WGK + 256] = w_gk_up
    return c2


_CACHE = {}


def kernel(x_prompt, x_sample, state_gla, state_conv, norm_gain, w_in, w_gk_up, b_gk,
           gla_norm_gain, conv_w, w_out, final_norm_gain):
    f = lambda a: np.ascontiguousarray(np.asarray(a), dtype=np.float32)
    x_prompt, x_sample, state_gla, state_conv = f(x_prompt), f(x_sample), f(state_gla), f(state_conv)
    norm_gain, w_in, w_gk_up, b_gk = f(norm_gain), f(w_in), f(w_gk_up), f(b_gk)
    gla_norm_gain, conv_w, w_out, final_norm_gain = f(gla_norm_gain), f(conv_w), f(w_out), f(final_norm_gain)

    if "nc" not in _CACHE:
        _CACHE["nc"] = build_program()
    nc = _CACHE["nc"]

    cm = _colmap()
    w_in_p = np.concatenate([w_in[0], np.zeros((D, 1), np.float32)], axis=1)[:, cm]
    w_in_l = np.ascontiguousarray(w_in_p.reshape(KC, 128, NCH, 128).transpose(1, 2, 0, 3)).reshape(128, NCH, KC * 128)
    w_v_l = np.ascontiguousarray(w_in[0][:, 512:1024].reshape(KC, 128, 512).transpose(1, 0, 2)).reshape(128, KC * 512)
    w_out_l = np.ascontiguousarray(w_out[0].reshape(KC, 128, 2, 512).transpose(1, 2, 0, 3)).reshape(128, 2, KC * 512)
    prm = np.zeros((128, PW), np.float32)
    prm[:, PO_G:PO_G + 8] = norm_gain[0].reshape(KC, 128).T
    prm[:, PO_BGK:PO_BGK + 2] = b_gk[0].reshape(2, 128).T
    prm[:, PO_GG] = gla_norm_gain[0]
    prm[:, PO_CW:PO_CW + 12] = conv_w[0].reshape(3, 4, 128).transpose(2, 1, 0).reshape(128, 12)
    fg = np.ascontiguousarray(np.broadcast_to(final_norm_gain[None, :], (128, D)))
    cst = _constants()
    cst2 = _constants2(w_gk_up[0])

    in_maps = []
    for c in range(NCORES):
        xs = x_sample[c * NSEQ_S:(c + 1) * NSEQ_S].reshape(NSEQ_S * LS, D)
        x_all = np.concatenate([x_prompt[c], xs, np.zeros((128 - NSEQ_S * LS, D), np.float32)], axis=0)
        in_maps.append({
            "x_all": np.ascontiguousarray(x_all),
            "gla_in": np.ascontiguousarray(state_gla[0, c * NSEQ_S:(c + 1) * NSEQ_S]),
            "conv_in": np.ascontiguousarray(state_conv[0, c * NSEQ_S:(c + 1) * NSEQ_S].reshape(NSEQ_S * 2, 512)),
            "w_in_l": w_in_l, "w_v_l": w_v_l, "w_out_l": w_out_l, "cst": cst, "cst2": cst2, "prm": prm, "fg": fg,
        })
    res = run_bass_kernel_spmd(nc, in_maps, core_ids=list(range(NCORES)))
    R = res.results
    y_prompt = np.stack([R[c]["y"][:SEQ] for c in range(NCORES)], axis=0)
    y_sample = np.concatenate([R[c]["y"][SEQ:].reshape(NSEQ_S, LS, D) for c in range(NCORES)], axis=0)
    gla_p = np.stack([R[c]["gla_p"] for c in range(NCORES)], axis=0)[None]
    conv_p = np.stack([R[c]["conv_p"] for c in range(NCORES)], axis=0)[None]
    gla_s = np.concatenate([R[c]["gla_s"] for c in range(NCORES)], axis=0)[None]
    conv_s = np.concatenate([R[c]["conv_s"].reshape(NSEQ_S, 2, 512) for c in range(NCORES)], axis=0)[None]
    return (y_prompt.astype(np.float32), y_sample.astype(np.float32), gla_p.astype(np.float32),
            conv_p.astype(np.float32), gla_s.astype(np.float32), conv_s.astype(np.float32))
```

```python
import math
from contextlib import ExitStack

import numpy as np
import concourse.bass as bass
import concourse.mybir as mybir
from concourse.bass_utils import run_bass_kernel_spmd

F32 = mybir.dt.float32
BF16 = mybir.dt.bfloat16
AF = mybir.ActivationFunctionType
ALU = mybir.AluOpType

NCORES = 8
D = 1024
KC = 8
SEQ = 2048
NSEQ_S = 16
LS = 4
NIN = 3600
NCH = 29
EPS = 1e-6
LN_QS = math.log(0.125)

C_LR = 0
C_Q = 1
C_K = 3
C_V = 5
C_G = 9
C_CONV = 13

CO_ID = 0
CO_RMP = 128
CO_RMS = 640
CO_SEL = 768
CW = 800
C2_ONES = 0
C2_MP = 128
C2_MS = 256
C2_WGK = 384
CW2 = 640
PO_G = 0
PO_BGK = 8
PO_GG = 10
PO_CW = 11
PW = 23


class _Res:
    __slots__ = ("name", "w", "r", "dsem", "dcnt", "excl")

    def __init__(self, name, excl=False):
        self.name = name
        self.w = None
        self.r = {}
        self.dsem = None
        self.dcnt = 0
        self.excl = excl


class _Eng:
    def __init__(self, name):
        self.name = name
        self.ops = []
        self.cnt = 0
        self.sem = None
        self.seen = {}


class _Plan:
    def __init__(self, nc, es):
        self.nc = nc
        self.es = es
        self.eng = {n: _Eng(n) for n in ("pe", "act", "dve", "pool", "sp")}
        for n, e in self.eng.items():
            e.sem = es.enter_context(nc.semaphore("s_" + n))
        self.nsem = 0
        self.out_events = []

    def res(self, name, excl=False):
        return _Res(name, excl)

    def _deps(self, eng, reads, writes):
        E = self.eng[eng]
        deps = []
        for r in reads:
            if r.w is not None:
                deps.append(r.w)
            if r.excl:
                deps.extend(ev for ev in r.r.values() if ev[2] != eng)
        for w in writes:
            if w.w is not None:
                deps.append(w.w)
            deps.extend(w.r.values())
        waits = []
        for (sem, val, src) in deps:
            if src == "pe" and eng == "pe":
                continue
            k = id(sem)
            if E.seen.get(k, 0) >= val:
                continue
            E.seen[k] = val
            waits.append((sem, val))
        return waits

    def _commit(self, ev, reads, writes):
        for r in reads:
            k = id(ev[0])
            old = r.r.get(k)
            if old is None or old[1] < ev[1]:
                r.r[k] = ev
        for w in writes:
            w.w = ev
            w.r = {}

    def op(self, eng, fn, reads=(), writes=()):
        E = self.eng[eng]
        waits = self._deps(eng, reads, writes)
        E.cnt += 1
        ev = (E.sem, E.cnt, eng)
        E.ops.append((waits, fn, (E.sem, 1), False))
        self._commit(ev, reads, writes)
        return ev

    def dma(self, fn, buf, reads=(), writes=(), n=1, is_output=False, queue="sp"):
        E = self.eng[queue]
        if buf.dsem is None:
            self.nsem += 1
            buf.dsem = self.es.enter_context(self.nc.semaphore("d%d_%s" % (self.nsem, buf.name)))
        waits = self._deps(queue, reads, writes)
        buf.dcnt += 16 * n
        ev = (buf.dsem, buf.dcnt, "dma")
        E.ops.append((waits, fn, (buf.dsem, 16), True))
        self._commit(ev, reads, writes)
        if is_output:
            self.out_events.append(ev)
        return ev

    def replay(self, eng, e):
        E = self.eng[eng]
        for waits, fn, inc, inc_all in E.ops:
            for sem, val in waits:
                e.wait_ge(sem, val)
            ins = fn(e)
            if not isinstance(ins, (list, tuple)):
                ins = [ins]
            if inc_all:
                for i in ins:
                    i.then_inc(inc[0], inc[1])
            else:
                ins[-1].then_inc(inc[0], inc[1])
        if eng == "sp":
            last = {}
            for (sem, val, _) in self.out_events:
                k = id(sem)
                if k not in last or last[k][1] < val:
                    last[k] = (sem, val)
            for sem, val in last.values():
                e.wait_ge(sem, val)


def build_program():
    nc = bass.Bass("TRN2", target_bir_lowering=False)
    NTOK_IN = SEQ + 128
    NTOK_OUT = SEQ + NSEQ_S * LS
    x_all = nc.dram_tensor("x_all", [NTOK_IN, D], F32, kind="ExternalInput").ap()
    gla_in = nc.dram_tensor("gla_in", [NSEQ_S, 4, 64, 128], F32, kind="ExternalInput").ap()
    conv_in = nc.dram_tensor("conv_in", [NSEQ_S * 2, 512], F32, kind="ExternalInput").ap()
    w_in_d = nc.dram_tensor("w_in_l", [128, NCH, KC * 128], F32, kind="ExternalInput").ap()
    w_v_d = nc.dram_tensor("w_v_l", [128, KC * 512], F32, kind="ExternalInput").ap()
    w_out_d = nc.dram_tensor("w_out_l", [128, 2, KC * 512], F32, kind="ExternalInput").ap()
    cst_d = nc.dram_tensor("cst", [128, CW], F32, kind="ExternalInput").ap()
    cst2_d = nc.dram_tensor("cst2", [128, CW2], F32, kind="ExternalInput").ap()
    prm_d = nc.dram_tensor("prm", [128, PW], F32, kind="ExternalInput").ap()
    fg_d = nc.dram_tensor("fg", [128, D], F32, kind="ExternalInput").ap()
    y_d = nc.dram_tensor("y", [NTOK_OUT, D], F32, kind="ExternalOutput").ap()
    gla_p_d = nc.dram_tensor("gla_p", [4, 64, 128], F32, kind="ExternalOutput").ap()
    conv_p_d = nc.dram_tensor("conv_p", [2, 512], F32, kind="ExternalOutput").ap()
    gla_s_d = nc.dram_tensor("gla_s", [NSEQ_S, 4, 64, 128], F32, kind="ExternalOutput").ap()
    conv_s_d = nc.dram_tensor("conv_s", [NSEQ_S * 2, 512], F32, kind="ExternalOutput").ap()

    with ExitStack() as es:
        es.enter_context(nc.allow_low_precision("bf16 matmul operands, fp32 accumulation"))
        P = _Plan(nc, es)

        def sb(name, shape, dt):
            return es.enter_context(nc.sbuf_tensor(name, shape, dt))

        def ps(name, shape, dt):
            return es.enter_context(nc.psum_tensor(name, shape, dt))

        NSLOT = NCH - 4
        w_in_bf = sb("w_in_bf", [128, NSLOT, KC, 128], BF16)
        w_v_bf = sb("w_v_bf", [128, KC, 512], BF16)
        R_win = [P.res("win%d" % c) for c in range(NCH)]
        w_out_bf = sb("w_out_bf", [128, 2, KC, 512], BF16)
        R_wout = [P.res("wout%d" % c) for c in range(2)]

        def wslot(c):
            return c if c < C_V else c - 4
        class _Pool:
            def __init__(self, name, n, dt=F32):
                self.t = [sb("%s%d" % (name, i), [128, 1024], dt) for i in range(n)]
                self.R = [P.res("%s%d" % (name, i)) for i in range(n)]
                self.busy = [False] * n
                self.i = 0
                self.flush = {}

            def get(self):
                n = len(self.t)
                for k in range(n):
                    j = (self.i + k) % n
                    if not self.busy[j]:
                        break
                else:
                    assert self.flush, "pool exhausted"
                    j = next(iter(self.flush))
                    self.flush.pop(j)()
                self.i = j + 1
                self.busy[j] = True
                return self.t[j], self.R[j], j

            def release(self, j):
                self.busy[j] = False

            def nfree(self):
                return sum(1 for x in self.busy if not x)

            def defer(self, j, fn):
                def run():
                    fn()
                    self.busy[j] = False
                self.flush[j] = run

            def flush_all(self):
                for j in list(self.flush):
                    self.flush.pop(j)()

        poolA = _Pool("pA", 3)
        poolB = _Pool("pB", 3)

        poolX = _Pool("xn", 5, BF16)

        cst_f = sb("cst_f", [128, CW], F32); R_cst = P.res("cst")
        prm_f = sb("prm_f", [128, PW], F32); R_prm = P.res("prm")
        fg_t = sb("fg_t", [128, D], F32); R_fg = P.res("fg")
        R_id = P.res("ident")
        cbf = sb("cbf", [128, 512 + 256], BF16); R_cbf = P.res("cbf"); R_cbfi = P.res("cbfi")
        nbgk = sb("nbgk", [128, 2], F32); R_nbgk = P.res("nbgk")
        st1 = sb("st1", [128, 4, 4], F32); R_st1 = [P.res("st1_%d" % i) for i in range(4)]
        st1_i = [0]
        st2 = sb("st2", [128, 8], F32); R_st2 = P.res("st2")
        hTb = [sb("hT%d" % i, [128, KC, 512], BF16) for i in range(2)]
        R_hTb = [[P.res("hT%d_%d" % (i, t)) for t in range(4)] for i in range(2)]
        o_sb = sb("o_sb", [128, 4, 128], F32); R_osb = P.res("o_sb")
        qp = [[[sb("qp%d%d%d" % (b, p, h), [128, 512], BF16) for h in range(2)] for p in range(2)] for b in range(1)]
        R_qp = [[[P.res("qp%d%d%d" % (b, p, h)) for h in range(2)] for p in range(2)] for b in range(1)]
        kT = [sb("kT%d" % b, [128, 2, 512], BF16) for b in range(1)]
        R_kT = [[P.res("kT%d%d" % (b, p)) for p in range(2)] for b in range(1)]
        vt = [sb("vt%d" % b, [128, 4, 512], BF16) for b in range(1)]
        R_vt = [[P.res("vt%d%d" % (b, t)) for t in range(4)] for b in range(1)]
        sg = [sb("sg%d" % b, [128, 4, 512], BF16) for b in range(1)]
        R_sg = [[P.res("sg%d%d" % (b, h)) for h in range(4)] for b in range(1)]
        el_p = [sb("el_p%d" % b, [128, 2, 4], F32) for b in range(1)]
        R_elp = [P.res("elp%d" % b) for b in range(1)]
        mixC = [sb("mixC%d" % b, [128, 4, 512], BF16) for b in range(1)]
        R_mixC = [[P.res("mixC%d%d" % (b, j)) for j in range(4)] for b in range(1)]
        mixG = sb("mixG", [128, 4, 512], BF16)
        R_mixG = [P.res("mixG%d" % t) for t in range(4)]
        kt = sb("kt", [128, 4, 256], BF16); R_kt = P.res("kt")
        el_s = sb("el_s", [128, 32, 2], F32); R_els = P.res("els")
        lr_bf = sb("lr_bf", [128, 512], BF16); R_lr = P.res("lr")
        e1b = [sb("e1_%d" % i, [128, 512], F32) for i in range(2)]; R_e1b = [P.res("e1_0"), P.res("e1_1")]
        Bcb = [sb("Bc%d" % i, [128, 512], F32) for i in range(2)]; R_Bcb = [P.res("Bc0"), P.res("Bc1")]
        ebq = [sb("ebq%d" % p, [128, 512], F32) for p in range(2)]; R_ebq = [P.res("ebq0"), P.res("ebq1")]
        enb = [sb("enb%d" % p, [128, 512], F32) for p in range(2)]; R_enb = [P.res("enb0"), P.res("enb1")]
        u_sb = sb("u_sb", [128, 512], F32); R_u = P.res("u")
        hp_p = sb("hp_p", [128, 4, 514], F32); R_hpp = [P.res("hpp%d" % j) for j in range(4)]
        hp_s = sb("hp_s", [128, 4, 32, 6], F32); R_hps = [P.res("hps%d" % j) for j in range(4)]
        tA = sb("tA", [128, 512], F32); R_tA = P.res("tA")
        tB = sb("tB", [128, 512], F32); R_tB = P.res("tB")
        sgt = sb("sgt", [128, 512], F32); R_sgt = P.res("sgt")
        ATm = [sb("ATm%d" % i, [128, 4, 128], BF16) for i in range(2)]; R_ATm = [P.res("ATm0"), P.res("ATm1")]
        S = sb("S", [128, 2, 128], F32); R_S = P.res("S")
        SUe = [sb("SUe%d" % i, [128, 2, 128], F32) for i in range(2)]; R_SUe = [P.res("SUe0"), P.res("SUe1")]
        S_bf = [sb("S_bf%d" % i, [128, 2, 128], BF16) for i in range(2)]; R_Sbf = [P.res("Sbf0"), P.res("Sbf1")]
        sq = sb("sq", [128, 512], BF16); R_sq = P.res("sq")
        rstd = sb("rstd", [128, 4, 128], F32); R_rstd = P.res("rstd")

        NBIG = 3
        big = [ps("big%d" % i, [128, 512], F32) for i in range(NBIG)]
        R_big = [P.res("big%d" % i, True) for i in range(NBIG)]
        big_i = [0]

        def big_next():
            i = big_i[0] % NBIG
            big_i[0] += 1
            return big[i], R_big[i]

        NSM = 2
        sml = [ps("sml%d" % i, [128, 4, 128], F32) for i in range(NSM)]
        R_sml = [P.res("sml%d" % i, True) for i in range(NSM)]
        sml_i = [0]

        def sml_next():
            i = sml_i[0] % NSM
            sml_i[0] += 1
            return sml[i], R_sml[i]

        TRb = [ps("TR%d" % i, [128, 8, 128], BF16) for i in range(2)]; R_TR = [P.res("TR0", True), P.res("TR1", True)]
        tr_i = [0]
        OTb = [ps("OT0", [128, 4, 128], F32)] * 2; R_OTb = [P.res("OT0", True)] * 2
        ot_i = [0]

        ident_bf = cbf[:, 0:128]
        ones_bf = cbf[:, 128:256]
        maskP_bf = cbf[:, 256:384]
        maskS_bf = cbf[:, 384:512]
        wgk_bf = cbf[:, 512:768]
        ident_f = cst_f[:, CO_ID:CO_ID + 128]

        g_b = prm_f[:, PO_G:PO_G + 8].unsqueeze(2).broadcast_to([128, 8, 128])
        gg = prm_f[:, PO_GG:PO_GG + 1]
        fg = fg_t[:]

        use_order = [C_LR, "v"] + list(range(C_G, C_G + 4)) + [C_Q, C_Q + 1, C_K, C_K + 1] + list(range(C_CONV, NCH)) + ["o0", "o1"]
        w_next = [0]

        def issue_weights(n):
            while n > 0 and w_next[0] < len(use_order):
                c = use_order[w_next[0]]
                w_next[0] += 1
                n -= 1
                if c == "v":
                    P.dma(lambda e: e.dma_start(out=w_v_bf[:], in_=w_v_d.rearrange("p (k n) -> p k n", k=KC)), R_win[C_V],
                          writes=R_win[C_V:C_V + 4], queue="pool")
                elif c in ("o0", "o1"):
                    hf = int(c[1])
                    P.dma(lambda e, hf=hf: e.dma_start(out=w_out_bf[:, hf, :, :], in_=w_out_d[:, hf, :].rearrange("p (k n) -> p k n", k=KC)),
                          R_wout[hf], writes=[R_wout[hf]], queue="pool")
                else:
                    P.dma(lambda e, c=c: e.dma_start(out=w_in_bf[:, wslot(c), :, :].rearrange("p k n -> p (k n)"), in_=w_in_d[:, c, :]),
                          R_win[c], writes=[R_win[c]], queue="pool")

        def need_w(kind, c):
            pass

        xready = {}

        xdma = {}

        def x_dma(tok0, t):
            xb, Rx, xj = poolA.get()
            r0 = tok0 + t * 128
            P.dma(lambda e: e.dma_start(out=xb[:], in_=x_all[r0:r0 + 128, :]), Rx, writes=[Rx])
            xdma[(tok0, t)] = (xb, Rx, xj)

        def x_load(tok0, t):
            if (tok0, t) not in xdma:
                x_dma(tok0, t)
            xb, Rx, xj = xdma.pop((tok0, t))
            xs, Rxs, sj = poolX.get()
            k = st1_i[0] % 4
            st1_i[0] += 1
            sk, Rk = st1[:, k, :], R_st1[k]
            P.op("act", lambda e: e.activation(out=xs[:], in_=xb[:], func=AF.Square, accum_out=sk[:, 0:1]),
                 reads=[Rx], writes=[Rxs, Rk])
            P.op("act", lambda e: e.activation(out=sk[:, 1:2], in_=sk[:, 0:1], func=AF.Ln, scale=1.0 / D, bias=EPS),
                 reads=[Rk], writes=[Rk])
            P.op("act", lambda e: e.activation(out=sk[:, 2:3], in_=sk[:, 1:2], func=AF.Exp, scale=-0.5),
                 reads=[Rk], writes=[Rk])
            P.op("pool", lambda e: e.tensor_scalar(out=xs[:], in0=xb[:], scalar1=sk[:, 2:3], scalar2=0.0,
                                                   op0=ALU.mult, op1=ALU.add),
                 reads=[Rx, Rk], writes=[Rxs])
            poolA.release(xj)
            xready[(tok0, t)] = (xs, Rxs, sj)

        xdone = set()

        def x_tr(tok0, t, hb):
            if (tok0, t) in xdone:
                return
            xdone.add((tok0, t))
            if (tok0, t) not in xready:
                x_load(tok0, t)
            xs, Rxs, sj = xready.pop((tok0, t))
            ti = tr_i[0] % 2
            tr_i[0] += 1
            P.op("pe", lambda e: [e.transpose(out=TRb[ti][:, kc, :], in_=xs[:, kc * 128:(kc + 1) * 128], identity=ident_bf)
                                  for kc in range(KC)],
                 reads=[Rxs, R_cbfi], writes=[R_TR[ti]])
            P.op("dve", lambda e: e.tensor_tensor(out=hTb[hb][:, :, t * 128:(t + 1) * 128], in0=TRb[ti][:], in1=g_b, op=ALU.mult),
                 reads=[R_TR[ti], R_prm], writes=[R_hTb[hb][t]])
            poolX.release(sj)

        def x_prep(tok0, NT, hb):
            for t in range(NT):
                if (tok0, t) not in xdone:
                    x_tr(tok0, t, hb)
                    yield

        cur_hb = [0]

        def inproj_fm(c, T, NT):
            hT, R_hT = hTb[cur_hb[0]], R_hTb[cur_hb[0]]
            bank, Rb = big_next()
            P.op("pe", lambda e: [e.matmul(out=bank[:, 0:T], lhsT=w_in_bf[:, wslot(c), kc, :], rhs=hT[:, kc, 0:T],
                                            start=(kc == 0), stop=(kc == KC - 1)) for kc in range(KC)],
                 reads=[R_win[c]] + R_hT[:NT], writes=[Rb])
            return bank, Rb

        def silu_from_psum(bank, Rb, T, out_ap, R_out):
            P.op("act", lambda e: e.activation(out=sgt[:, 0:T], in_=bank[:, 0:T], func=AF.Exp, scale=-1.0), reads=[Rb], writes=[R_sgt])
            P.op("act", lambda e: e.activation(out=sgt[:, 0:T], in_=sgt[:, 0:T], func=AF.Ln, bias=1.0), reads=[R_sgt], writes=[R_sgt])
            P.op("act", lambda e: e.activation(out=sgt[:, 0:T], in_=sgt[:, 0:T], func=AF.Exp, scale=-1.0), reads=[R_sgt], writes=[R_sgt])
            P.op("dve", lambda e: e.tensor_tensor(out=out_ap, in0=bank[:, 0:T], in1=sgt[:, 0:T], op=ALU.mult),
                 reads=[Rb, R_sgt], writes=[R_out])

        def kt_transposes(b, NT):
            ti = tr_i[0] % 2
            tr_i[0] += 1
            tr_i[0] += 1
            P.op("pe", lambda e: [e.transpose(out=TRb[ti][:, t * 2 + p, :], in_=kT[b][:, p, t * 128:(t + 1) * 128], identity=ident_bf)
                                  for t in range(NT) for p in range(2)],
                 reads=R_kT[b] + [R_cbfi], writes=[R_TR[ti]])
            P.op("act", lambda e: e.copy(out=kt[:, 0:NT, :], in_=TRb[ti][:, 0:2 * NT, :].rearrange("p (t q) n -> p t (q n)", q=2)),
                 reads=[R_TR[ti]], writes=[R_kt])

        def stage1(b, tok0, NT, sample, last_prompt):
            T = NT * 128
            rm = cst_f[:, CO_RMS:CO_RMS + 128] if sample else cst_f[:, CO_RMP:CO_RMP + 512]
            bank, Rb = inproj_fm(C_LR, T, NT)
            P.op("act", lambda e, bank=bank: e.copy(out=lr_bf[:, 0:T], in_=bank[:, 0:T]), reads=[Rb], writes=[R_lr])
            yield
            hT, R_hT = hTb[cur_hb[0]], R_hTb[cur_hb[0]]

            def v_step(t):
                bank, Rb = big_next()
                P.op("pe", lambda e: [e.matmul(out=bank[:, :], lhsT=hT[:, kc, t * 128:(t + 1) * 128],
                                               rhs=w_v_bf[:, kc, :],
                                               start=(kc == 0), stop=(kc == KC - 1)) for kc in range(KC)],
                     reads=R_win[C_V:C_V + 4] + [R_hT[t]], writes=[Rb])
                P.op("act", lambda e: e.copy(out=vt[b][:, t, :], in_=bank[:, :]), reads=[Rb], writes=[R_vt[b][t]])

            for t in range(min(NT, 2)):
                v_step(t)
                yield
            gbanks = []
            for p in range(2):
                bank, Rb = big_next()
                gbanks.append((bank, Rb))
                P.op("pe", lambda e, bank=bank, p=p: e.matmul(out=bank[:, 0:T], lhsT=wgk_bf[:, p * 128:(p + 1) * 128],
                                                               rhs=lr_bf[:, 0:T], start=True, stop=True),
                     reads=[R_cbf, R_lr], writes=[Rb])
            for p in range(2):
                bank, Rb = gbanks[p]
                e1, R_e1 = e1b[p], R_e1b[p]
                P.op("act", lambda e, bank=bank, p=p, e1=e1: e.activation(out=e1[:, 0:T], in_=bank[:, 0:T], func=AF.Exp,
                                                                           scale=-1.0, bias=nbgk[:, p:p + 1]),
                     reads=[Rb, R_nbgk], writes=[R_e1])
                P.op("act", lambda e, e1=e1: e.activation(out=e1[:, 0:T], in_=e1[:, 0:T], func=AF.Ln, bias=1.0),
                     reads=[R_e1], writes=[R_e1])
                P.op("dve", lambda e, p=p, e1=e1: e.tensor_tensor_scan(out=Bcb[p][:, 0:T], data0=rm[:, 0:T], data1=e1[:, 0:T],
                                                                        initial=0.0, op0=ALU.mult, op1=ALU.add),
                     reads=[R_e1, R_cst], writes=[R_Bcb[p]])
            yield
            for p in range(2):
                Bc, R_Bc = Bcb[p], R_Bcb[p]
                P.op("act", lambda e, p=p, Bc=Bc: e.activation(out=ebq[p][:, 0:T], in_=Bc[:, 0:T], func=AF.Exp,
                                                                scale=-1.0 / 16.0, bias=LN_QS),
                     reads=[R_Bc], writes=[R_ebq[p]])
                P.op("act", lambda e, p=p, Bc=Bc: e.activation(out=enb[p][:, 0:T], in_=Bc[:, 0:T], func=AF.Exp, scale=1.0 / 16.0),
                     reads=[R_Bc], writes=[R_enb[p]])
                if sample:
                    P.op("act", lambda e, p=p, Bc=Bc: e.activation(out=el_s[:, :, p], in_=Bc[:, LS - 1:128:LS], func=AF.Exp,
                                                                    scale=-1.0 / 16.0),
                         reads=[R_Bc], writes=[R_els])
                else:
                    P.op("act", lambda e, p=p, Bc=Bc: e.activation(out=el_p[b][:, p, 0:NT], in_=Bc[:, 127:T:128], func=AF.Exp,
                                                                    scale=-1.0 / 16.0),
                         reads=[R_Bc], writes=[R_elp[b]])
            yield
            for h in range(4):
                bank, Rb = inproj_fm(C_G + h, T, NT)
                silu_from_psum(bank, Rb, T, sg[b][:, h, 0:T], R_sg[b][h])
                yield
                if h < 2 and 2 + h < NT:
                    v_step(2 + h)
                    yield
            for p in range(2):
                bank, Rb = inproj_fm(C_Q + p, T, NT)
                for hh in range(2):
                    rows = slice(hh * 64, (hh + 1) * 64)
                    P.op("dve", lambda e, bank=bank, p=p, hh=hh, rows=rows: e.tensor_tensor(
                        out=qp[b][p][hh][rows, 0:T], in0=bank[rows, 0:T], in1=ebq[p][rows, 0:T], op=ALU.mult),
                        reads=[Rb, R_ebq[p]], writes=[R_qp[b][p][hh]])
                yield
            for p in range(2):
                bank, Rb = inproj_fm(C_K + p, T, NT)
                P.op("dve", lambda e, bank=bank, p=p: e.tensor_tensor(out=kT[b][:, p, 0:T], in0=bank[:, 0:T], in1=enb[p][:, 0:T],
                                                                      op=ALU.mult),
                     reads=[Rb, R_enb[p]], writes=[R_kT[b][p]])
                yield
            kt_transposes(b, NT)
            yield "SPLIT"
            for j in range(4):
                if sample:
                    hv = hp_s[:, j, :, :]
                    Rh = R_hps[j]

                    def win(o, hv=hv):
                        return hv[:, :, o:o + LS]

                    def v3(ap):
                        return ap.rearrange("p (s l) -> p s l", l=LS)
                else:
                    hv = hp_p[:, j, :]
                    Rh = R_hpp[j]

                    def win(o, hv=hv):
                        return hv[:, o:o + T]

                    def v3(ap):
                        return ap
                cb = C_CONV + 4 * j
                cwj = prm_f[:, PO_CW + 3 * j:PO_CW + 3 * j + 3]
                bank, Rb = inproj_fm(cb + 0, T, NT)
                P.op("act", lambda e, bank=bank: e.copy(out=u_sb[:, 0:T], in_=bank[:, 0:T]), reads=[Rb], writes=[R_u])
                yield
                bank, Rb = inproj_fm(cb + 1, T, NT)
                P.op("dve", lambda e, bank=bank, win=win, v3=v3: e.tensor_tensor(out=win(2), in0=v3(bank[:, 0:T]), in1=v3(u_sb[:, 0:T]),
                                                                              op=ALU.mult),
                     reads=[Rb, R_u], writes=[Rh])
                P.op("pool", lambda e, win=win, v3=v3, cwj=cwj: e.tensor_scalar(out=v3(tA[:, 0:T]), in0=win(0), scalar1=cwj[:, 0:1],
                                                                             scalar2=0.0, op0=ALU.mult, op1=ALU.add),
                     reads=[Rh, R_prm], writes=[R_tA])
                yield
                bank, Rb = inproj_fm(cb + 2, T, NT)
                P.op("dve", lambda e, win=win, v3=v3, cwj=cwj: e.scalar_tensor_tensor(out=v3(tB[:, 0:T]), in0=win(1), scalar=cwj[:, 1:2],
                                                                                   in1=v3(tA[:, 0:T]), op0=ALU.mult, op1=ALU.add),
                     reads=[Rh, R_tA, R_prm], writes=[R_tB])
                P.op("dve", lambda e, win=win, v3=v3, cwj=cwj: e.scalar_tensor_tensor(out=v3(tA[:, 0:T]), in0=win(2), scalar=cwj[:, 2:3],
                                                                                   in1=v3(tB[:, 0:T]), op0=ALU.mult, op1=ALU.add),
                     reads=[Rh, R_tB, R_prm], writes=[R_tA])
                if not sample and not last_prompt:
                    P.op("pool", lambda e, hv=hv: e.tensor_copy(out=hv[:, 0:2], in_=hv[:, T:T + 2]), reads=[Rh], writes=[Rh])
                silu_from_psum(bank, Rb, T, sgt[:, 0:T], R_sgt)
                P.op("pool", lambda e: e.tensor_tensor(out=tA[:, 0:T], in0=tA[:, 0:T], in1=sgt[:, 0:T], op=ALU.mult),
                     reads=[R_tA, R_sgt], writes=[R_tA])
                yield
                bank, Rb = inproj_fm(cb + 3, T, NT)
                P.op("dve", lambda e, bank=bank, j=j: e.tensor_tensor(out=mixC[b][:, j, 0:T], in0=bank[:, 0:T], in1=tA[:, 0:T],
                                                                      op=ALU.mult),
                     reads=[Rb, R_tA], writes=[R_mixC[b][j]])
                yield
            if sample or last_prompt:
                bank, Rb = big_next()
                co_sb, R_co, cj = poolA.get()
                if sample:
                    cn_t, R_cn, cnj = poolA.get()
                    cn_sb = cn_t[:, 0:256].rearrange("p (j n) -> p j n", j=4)
                    for j in range(4):
                        P.op("act", lambda e, j=j: e.copy(out=cn_sb[:, j, :].rearrange("p (s r) -> p s r", r=2), in_=hp_s[:, j, :, LS:LS + 2]),
                             reads=[R_hps[j]], writes=[R_cn])
                    P.op("pe", lambda e, bank=bank: [e.transpose(out=bank[0:64, j * 128:(j + 1) * 128], in_=cn_sb[:, j, :],
                                                                  identity=ident_f) for j in range(4)],
                         reads=[R_cn, R_id], writes=[Rb])
                    poolA.release(cnj)
                    P.op("act", lambda e, bank=bank: e.copy(out=co_sb[0:64, 0:512], in_=bank[0:64, :]), reads=[Rb], writes=[R_co])
                    P.dma(lambda e: e.dma_start(out=conv_s_d, in_=co_sb[0:32, 0:512]), R_co, reads=[R_co], is_output=True)
                    poolA.release(cj)
                else:
                    P.op("pe", lambda e, bank=bank: [e.transpose(out=bank[0:32, j * 128:(j + 1) * 128], in_=hp_p[:, j, T - 30:T + 2],
                                                                  identity=ident_f) for j in range(4)],
                         reads=R_hpp + [R_id], writes=[Rb])
                    P.op("act", lambda e, bank=bank: e.copy(out=co_sb[0:32, 0:512], in_=bank[0:32, :]), reads=[Rb], writes=[R_co])
                    P.dma(lambda e: e.dma_start(out=conv_p_d, in_=co_sb[30:32, 0:512]), R_co, reads=[R_co], is_output=True)
                    poolA.release(cj)
                yield

        pend_mix = []

        def flush_mix():
            while pend_mix:
                pend_mix.pop(0)()

        def norm_a(OT, R_OT):
            P.op("act", lambda e: e.activation(out=sq[:], in_=OT[:].rearrange("p h n -> p (h n)"), func=AF.Square),
                 reads=[R_OT], writes=[R_sq])
            P.op("act", lambda e: e.copy(out=o_sb[:], in_=OT[:]), reads=[R_OT], writes=[R_osb])

        def norm_gate(b, c, cs, OT, R_OT):
            MS, R_MS = sml_next()
            P.op("pe", lambda e: e.matmul(out=MS[:].rearrange("p h n -> p (h n)"), lhsT=ones_bf, rhs=sq[:], start=True, stop=True),
                 reads=[R_sq, R_cbf], writes=[R_MS])
            P.op("act", lambda e: e.activation(out=rstd[:], in_=MS[:], func=AF.Ln, bias=EPS), reads=[R_MS], writes=[R_rstd])
            P.op("act", lambda e: e.activation(out=rstd[:], in_=rstd[:], func=AF.Exp, scale=-0.5), reads=[R_rstd], writes=[R_rstd])
            P.op("pool", lambda e: e.tensor_tensor(out=rstd[:], in0=rstd[:], in1=sg[b][:, :, cs], op=ALU.mult),
                 reads=[R_rstd] + R_sg[b], writes=[R_rstd])
            pend_mix.append(lambda: P.op("dve", lambda e: e.scalar_tensor_tensor(out=mixG[:, :, cs], in0=o_sb[:], scalar=gg, in1=rstd[:],
                                                                                op0=ALU.mult, op1=ALU.mult),
                                         reads=[R_osb, R_rstd, R_prm], writes=[R_mixG[c]]))

        def out_proj(b, tok0, t, sample):
            flush_mix()
            for c in range(8):
                need_w("out", c)
            xr, Rx, bi = poolB.get()
            r0 = tok0 + t * 128
            P.dma(lambda e: e.dma_start(out=xr[:], in_=x_all[r0:r0 + 128, :]), Rx, writes=[Rx])
            for hf in range(2):
                bank, Rb = big_next()

                def mm(e, bank=bank, hf=hf):
                    ins = []
                    for kc in range(KC):
                        lhs = mixG[:, kc, t * 128:(t + 1) * 128] if kc < 4 else mixC[b][:, kc - 4, t * 128:(t + 1) * 128]
                        ins.append(e.matmul(out=bank[:, :], lhsT=lhs, rhs=w_out_bf[:, hf, kc, :],
                                            start=(kc == 0), stop=(kc == KC - 1)))
                    return ins
                P.op("pe", mm, reads=[R_mixG[t]] + R_mixC[b] + [R_wout[hf]], writes=[Rb])
                P.op("dve", lambda e, bank=bank, hf=hf: e.tensor_tensor(out=xr[:, hf * 512:(hf + 1) * 512], in0=bank[:, :],
                                                                        in1=xr[:, hf * 512:(hf + 1) * 512], op=ALU.add),
                     reads=[Rb, Rx], writes=[Rx])
                yield
            jk, R_jk, jj = poolX.get()
            P.op("act", lambda e: e.activation(out=jk[:], in_=xr[:], func=AF.Square, accum_out=st2[:, 0:1]),
                 reads=[Rx], writes=[R_jk, R_st2])
            poolX.release(jj)
            P.op("act", lambda e: e.activation(out=st2[:, 1:2], in_=st2[:, 0:1], func=AF.Ln, scale=1.0 / D, bias=EPS),
                 reads=[R_st2], writes=[R_st2])
            P.op("act", lambda e: e.activation(out=st2[:, 2:3], in_=st2[:, 1:2], func=AF.Exp, scale=-0.5),
                 reads=[R_st2], writes=[R_st2])
            P.op("pool", lambda e: e.tensor_scalar(out=xr[:], in0=xr[:], scalar1=st2[:, 2:3], scalar2=0.0, op0=ALU.mult, op1=ALU.add),
                 reads=[Rx, R_st2], writes=[Rx])
            P.op("pool", lambda e: e.tensor_tensor(out=xr[:], in0=xr[:], in1=fg, op=ALU.mult),
                 reads=[Rx, R_fg], writes=[Rx])
            def store():
                if sample:
                    P.dma(lambda e: e.dma_start(out=y_d[SEQ:SEQ + NSEQ_S * LS, :], in_=xr[0:NSEQ_S * LS, :]), Rx, reads=[Rx],
                          is_output=True)
                else:
                    P.dma(lambda e: e.dma_start(out=y_d[r0:r0 + 128, :], in_=xr[:]), Rx, reads=[Rx], is_output=True)
            poolB.flush_all()
            poolB.defer(bi, store)
            yield

        def gla_prompt(b, tok0, NT, last_prompt):
            Rq = [R_qp[b][0][0], R_qp[b][0][1], R_qp[b][1][0], R_qp[b][1][1]]
            ots = {}

            def A(c):
                cs = slice(c * 128, (c + 1) * 128)
                AT, R_AT = sml_next()
                P.op("pe", lambda e: [e.matmul(out=AT[:, h, :], lhsT=kT[b][:, h // 2, cs], rhs=qp[b][h // 2][h % 2][:, cs],
                                               start=True, stop=True) for h in range(4)],
                     reads=R_kT[b] + Rq, writes=[R_AT])
                P.op("dve", lambda e: e.tensor_tensor(out=ATm[c % 2][:], in0=AT[:], in1=maskP_bf.unsqueeze(1).broadcast_to([128, 4, 128]),
                                                      op=ALU.mult),
                     reads=[R_AT, R_cbf], writes=[R_ATm[c % 2]])

            def U(c):
                SU, R_SU = sml_next()
                P.op("pe", lambda e: [e.matmul(out=SU[:, h, :], lhsT=kt[:, c, (h // 2) * 128:(h // 2 + 1) * 128],
                                               rhs=vt[b][:, c, h * 128:(h + 1) * 128], start=True, stop=True) for h in range(4)],
                     reads=[R_kt, R_vt[b][c]], writes=[R_SU])
                for hh in range(2):
                    rows = slice(hh * 64, (hh + 1) * 64)
                    P.op("dve", lambda e, hh=hh, rows=rows: e.tensor_tensor(
                        out=SUe[c % 2][rows, :, :], in0=SU[rows, hh::2, :], in1=el_p[b][rows, :, c:c + 1].broadcast_to([64, 2, 128]), op=ALU.mult),
                        reads=[R_SU, R_elp[b]], writes=[R_SUe[c % 2]])

            def O(c):
                cs = slice(c * 128, (c + 1) * 128)
                oi = ot_i[0] % 2
                ot_i[0] += 1
                OT, R_OT = OTb[oi], R_OTb[oi]
                ots[c] = (OT, R_OT)
                si = sbf_i[0]

                def o_mm(e):
                    ins = []
                    for h in range(4):
                        ins.append(e.matmul(out=OT[:, h, :], lhsT=vt[b][:, c, h * 128:(h + 1) * 128], rhs=ATm[c % 2][:, h, :],
                                            start=True, stop=False))
                        ins.append(e.matmul(out=OT[:, h, :], lhsT=S_bf[si][:, h // 2, :], rhs=qp[b][h // 2][h % 2][:, cs],
                                            start=False, stop=True))
                    return ins
                P.op("pe", o_mm, reads=[R_vt[b][c], R_ATm[c % 2], R_Sbf[si]] + Rq, writes=[R_OT])
                norm_a(OT, R_OT)
                for p in range(2):
                    P.op("dve", lambda e, p=p: e.scalar_tensor_tensor(out=S[:, p, :], in0=S[:, p, :], scalar=el_p[b][:, p, c:c + 1],
                                                                     in1=SUe[c % 2][:, p, :], op0=ALU.mult, op1=ALU.add),
                         reads=[R_S, R_SUe[c % 2], R_elp[b]], writes=[R_S])
                sn = 1 - si
                pend_mix.append(lambda: P.op("act", lambda e: e.copy(out=S_bf[sn][:], in_=S[:]), reads=[R_S], writes=[R_Sbf[sn]]))
                sbf_i[0] = sn
                if last_prompt and c == NT - 1:
                    P.dma(lambda e: [e.dma_start(out=gla_p_d.rearrange("(pr hh) k v -> hh k pr v", hh=2)[hh],
                                                 in_=S[hh * 64:(hh + 1) * 64, :, :]) for hh in range(2)],
                          R_S, reads=[R_S], n=2, is_output=True)

            def N(c):
                OT, R_OT = ots.pop(c)
                norm_gate(b, c, slice(c * 128, (c + 1) * 128), OT, R_OT)

            sched = []
            for c in range(NT):
                sched.append(("A", c))
                if c >= 1:
                    sched.append(("N", c - 1))
                sched.append(("U", c))
                sched.append(("O", c))
            deferred_last["f"] = lambda: N(NT - 1)
            for kind, c in sched:
                flush_mix()
                if kind == "A":
                    A(c)
                    yield
                elif kind == "U":
                    U(c)
                    yield
                elif kind == "O":
                    O(c)
                    yield
                elif kind == "N":
                    N(c)
                    yield

        def gla_sample(b, tok0):
            Rq = [R_qp[b][0][0], R_qp[b][0][1], R_qp[b][1][0], R_qp[b][1][1]]
            cs = slice(0, 128)
            AT, R_AT = sml_next()
            P.op("pe", lambda e: [e.matmul(out=AT[:, h, :], lhsT=kT[b][:, h // 2, cs], rhs=qp[b][h // 2][h % 2][:, cs],
                                           start=True, stop=True) for h in range(4)],
                 reads=R_kT[b] + Rq, writes=[R_AT])
            P.op("dve", lambda e: e.tensor_tensor(out=ATm[0][:], in0=AT[:], in1=maskS_bf.unsqueeze(1).broadcast_to([128, 4, 128]),
                                                  op=ALU.mult),
                 reads=[R_AT, R_cbf], writes=[R_ATm[0]])
            oi = ot_i[0] % 2
            ot_i[0] += 1
            OT, R_OT = OTb[oi], R_OTb[oi]
            yield
            first = [True]
            pre = {}
            prev_store = [None]

            def s0_issue(G):
                S0g_t, R_S0g, gj = poolA.get()
                S0g = S0g_t[:].rearrange("p (s a v) -> p s a v", s=4, a=2)
                P.dma(lambda e, G=G, S0g=S0g: [e.dma_start(out=S0g[hh * 64:(hh + 1) * 64, :, :, :],
                                                           in_=gla_in.rearrange("s (pr hh) k v -> hh k s pr v", hh=2)[hh][:, 4 * G:4 * G + 4])
                                               for hh in range(2)],
                      R_S0g, writes=[R_S0g], n=2)
                pre[G] = (S0g, R_S0g, gj)
            kms = {}

            def km_prep(G):
                kmt, R_kmask, kmj = poolX.get()
                kmask = kmt[:].rearrange("p (s n) -> p s n", s=4)
                sel = cst_f[:, CO_SEL + 4 * G:CO_SEL + 4 * G + 4]
                P.op("pool", lambda e: e.tensor_tensor(out=kmask, in0=kt[:, 0:1, :].broadcast_to([128, 4, 256]),
                                                       in1=sel.unsqueeze(2).broadcast_to([128, 4, 256]), op=ALU.mult),
                     reads=[R_kt, R_cst], writes=[R_kmask])
                kms[G] = (kmask, R_kmask, kmj)

            s0_issue(0)
            km_prep(0)
            for G in range(4):
                if G + 1 < 4:
                    s0_issue(G + 1)
                    km_prep(G + 1)
                S0g, R_S0g, gj = pre.pop(G)
                sbt, R_S0gbf, sbj = poolX.get()
                S0g_bf = sbt[:].rearrange("p (s a v) -> p s a v", s=4, a=2)
                P.op("act", lambda e, S0g=S0g, S0g_bf=S0g_bf: e.copy(out=S0g_bf, in_=S0g), reads=[R_S0g], writes=[R_S0gbf])

                def inter(e, G=G, S0g_bf=S0g_bf):
                    ins = []
                    for h in range(4):
                        for i in range(4):
                            s = 4 * G + i
                            ins.append(e.matmul(out=OT[:, h, s * LS:(s + 1) * LS], lhsT=S0g_bf[:, i, h // 2, :],
                                                rhs=qp[b][h // 2][h % 2][:, s * LS:(s + 1) * LS],
                                                start=first[0], stop=False, skip_group_check=True))
                            first[0] = False
                    return ins
                P.op("pe", inter, reads=[R_S0gbf] + Rq, writes=[R_OT])
                poolX.release(sbj)
                yield
                kmask, R_kmask, kmj = kms.pop(G)
                for i in range(4):
                    SU, R_SU = sml_next()
                    P.op("pe", lambda e, i=i, SU=SU, kmask=kmask: [e.matmul(out=SU[:, h, :], lhsT=kmask[:, i, (h // 2) * 128:(h // 2 + 1) * 128],
                                                                            rhs=vt[b][:, 0, h * 128:(h + 1) * 128], start=True, stop=True)
                                                                   for h in range(4)],
                         reads=[R_kmask, R_vt[b][0]], writes=[R_SU])
                    for hh in range(2):
                        rows = slice(hh * 64, (hh + 1) * 64)
                        P.op("dve", lambda e, i=i, hh=hh, rows=rows, SU=SU, S0g=S0g: e.tensor_tensor(
                            out=S0g[rows, i, :, :], in0=SU[rows, hh::2, :], in1=S0g[rows, i, :, :], op=ALU.add),
                            reads=[R_SU, R_S0g], writes=[R_S0g])
                    yield
                poolX.release(kmj)
                P.op("pool", lambda e, G=G, S0g=S0g: e.tensor_tensor(out=S0g, in0=S0g,
                                                                     in1=el_s[:, 4 * G:4 * G + 4, :].unsqueeze(3).broadcast_to([128, 4, 2, 128]),
                                                                     op=ALU.mult),
                     reads=[R_S0g, R_els], writes=[R_S0g])
                def st(G=G, S0g=S0g, R_S0g=R_S0g):
                    P.dma(lambda e: [e.dma_start(out=gla_s_d.rearrange("s (pr hh) k v -> hh k s pr v", hh=2)[hh][:, 4 * G:4 * G + 4],
                                                 in_=S0g[hh * 64:(hh + 1) * 64, :, :, :]) for hh in range(2)],
                          R_S0g, reads=[R_S0g], n=2, is_output=True)
                poolA.defer(gj, st)
                yield
                poolA.flush_all()
            P.op("pe", lambda e: [e.matmul(out=OT[:, h, :], lhsT=vt[b][:, 0, h * 128:(h + 1) * 128], rhs=ATm[0][:, h, :],
                                           start=False, stop=(h == 3), skip_group_check=True) for h in range(4)],
                 reads=[R_vt[b][0], R_ATm[0]], writes=[R_OT])
            norm_a(OT, R_OT)
            yield
            deferred_last["f"] = lambda: norm_gate(b, 0, cs, OT, R_OT)

        deferred_last = {}

        def outproj_all(b, tok0, NT, sample):
            deferred_last.pop("f")()
            yield
            for t in range(NT):
                yield from out_proj(b, tok0, t, sample)
            poolB.flush_all()

        def conv_state_in():
            cs_t, R_csin, cj = poolA.get()
            cs_in = cs_t[0:32, 0:512]
            P.dma(lambda e: e.dma_start(out=cs_in, in_=conv_in), R_csin, writes=[R_csin])
            bank, Rb = big_next()
            P.op("pe", lambda e: [e.transpose(out=bank[:, j * 32:(j + 1) * 32], in_=cs_t[0:32, j * 128:(j + 1) * 128],
                                              identity=ident_f[0:32, 0:32]) for j in range(4)],
                 reads=[R_csin, R_id], writes=[Rb])
            poolA.release(cj)
            for j in range(4):
                P.op("dve", lambda e, j=j: e.tensor_copy(out=hp_s[:, j, 0:NSEQ_S, 0:2],
                                                         in_=bank[:, j * 32:(j + 1) * 32].rearrange("p (s r) -> p s r", r=2)),
                     reads=[Rb], writes=[R_hps[j]])
            yield


        def chain(*gens):
            for g in gens:
                yield from g

        def interleave(a, bgen, na=1, nb=1, extras=()):
            a_done = a is None
            b_done = bgen is None
            acc = 0
            extras = list(extras)
            every = max(1, (na - 2) // (len(extras) + 1)) if extras else 0
            ka = 0
            while not (a_done and b_done):
                if not a_done:
                    try:
                        next(a)
                    except StopIteration:
                        a_done = True
                    ka += 1
                    if extras and ka >= 1 and (ka - 1) % every == 0:
                        extras.pop(0)()
                acc += nb
                while (acc >= na or a_done) and not b_done:
                    acc -= na
                    try:
                        next(bgen)
                    except StopIteration:
                        b_done = True

        def n_steps1(NT, sample, lastp):
            return NT + (1 if sample else 0) + 1 + NT + 2 + 4 + 2 + 2 + 16 + (1 if (sample or lastp) else 0)

        def n_steps2(NT, sample):
            return (1 + 1 + 4 * 5 + 3 + 3) if sample else (1 + 4 * NT + 3 * NT)

        items = [(0, 4, False, False), (512, 4, False, False), (1024, 4, False, False), (1536, 4, False, True), (SEQ, 1, True, False)]
        P.dma(lambda e: e.dma_start(out=cst_f[:, 0:128], in_=cst_d[:, 0:128]), R_id, writes=[R_id])
        P.op("act", lambda e: e.copy(out=cbf[:, 0:128], in_=cst_f[:, CO_ID:CO_ID + 128]), reads=[R_id], writes=[R_cbfi])
        issue_weights(1)
        for t in range(items[0][1]):
            x_load(items[0][0], t)
        P.dma(lambda e: e.dma_start(out=prm_f[:], in_=prm_d), R_prm, writes=[R_prm])
        P.dma(lambda e: e.dma_start(out=cst_f[:, 128:CW], in_=cst_d[:, 128:CW]), R_cst, writes=[R_cst])
        c2, R_c2, c2j = poolA.get()
        P.dma(lambda e: e.dma_start(out=c2[:, 0:CW2], in_=cst2_d), R_c2, writes=[R_c2])
        P.op("act", lambda e: e.copy(out=cbf[:, 128:768], in_=c2[:, 0:CW2]), reads=[R_c2], writes=[R_cbf])
        poolA.release(c2j)
        issue_weights(100)
        P.dma(lambda e: e.dma_start(out=fg_t[:], in_=fg_d), R_fg, writes=[R_fg])
        P.op("act", lambda e: e.mul(out=nbgk[:], in_=prm_f[:, PO_BGK:PO_BGK + 2], mul=-1.0),
             reads=[R_prm], writes=[R_nbgk])
        for b in range(1):
            for p in range(2):
                for h in range(2):
                    P.op("dve", lambda e, t=qp[b][p][h]: e.memset(t[:], 0.0), writes=[R_qp[b][p][h]])
        P.op("dve", lambda e: e.memset(hp_p[:], 0.0), writes=R_hpp)
        P.op("dve", lambda e: e.memset(hp_s[:], 0.0), writes=R_hps)
        P.op("dve", lambda e: e.memset(S[:], 0.0), writes=[R_S])
        P.op("dve", lambda e: e.memset(S_bf[0][:], 0.0), writes=[R_Sbf[0]])
        sbf_i = [0]

        def run_until_split(g):
            for v in g:
                if v == "SPLIT":
                    return
                yield

        prev_out = None
        prev_nout = 1
        for i, (tok0, NT, sample, lastp) in enumerate(items):
            b = 0
            hb = i % 2
            cur_hb[0] = hb
            gens = [x_prep(tok0, NT, hb)]
            if sample:
                gens.append(conv_state_in())
            g1 = chain(*gens, stage1(b, tok0, NT, sample, lastp))
            nA = (1 if sample else 0) + 1 + NT + 2 + 4 + 2 + 2
            interleave(run_until_split(g1), prev_out, nA, prev_nout)
            extras = []
            if i + 1 < len(items):
                ntok0, nNT = items[i + 1][0], items[i + 1][1]

                def mk(t, ntok0=ntok0, nNT=nNT):
                    def f():
                        if t < nNT and poolA.nfree() >= 2:
                            x_dma(ntok0, t)
                        if t >= 1 and (ntok0, t - 1) in xdma and poolX.nfree() >= 2:
                            x_load(ntok0, t - 1)
                    return f

                def mk2(t, ntok0=ntok0, nhb=1 - hb):
                    def f():
                        if (ntok0, t) in xready:
                            x_tr(ntok0, t, nhb)
                    return f
                extras = [mk(t) for t in range(nNT + 1)] + [mk2(t) for t in range(nNT)]
            gla = gla_sample(b, tok0) if sample else gla_prompt(b, tok0, NT, lastp)
            nB = 16 + (1 if (sample or lastp) else 0)
            nG = (1 + 4 * 6 + 1) if sample else (4 * NT - 1)
            interleave(g1, gla, nB, nG, extras)
            prev_out = outproj_all(b, tok0, NT, sample)
            prev_nout = 3 * NT + 1
        interleave(None, prev_out)

        with nc.Block() as block:
            @block.sync
            def _(e):
                P.replay("sp", e)

            @block.scalar
            def _(e):
                P.replay("act", e)

            @block.vector
            def _(e):
                P.replay("dve", e)

            @block.gpsimd
            def _(e):
                P.replay("pool", e)

            @block.tensor
            def _(e):
                P.replay("pe", e)
    return nc


def _colmap():
    cols = []
    cols += list(range(1536, 1552)) + [-1] * 112
    cols += list(range(0, 256))
    cols += list(range(256, 512))
    cols += list(range(512, 1024))
    cols += list(range(1024, 1536))
    for j in range(4):
        cols += list(range(1552 + 128 * j, 1552 + 128 * (j + 1)))
        cols += list(range(2576 + 128 * j, 2576 + 128 * (j + 1)))
        cols += list(range(3088 + 128 * j, 3088 + 128 * (j + 1)))
        cols += list(range(2064 + 128 * j, 2064 + 128 * (j + 1)))
    assert len(cols) == NCH * 128
    return np.array(cols)


def _constants():
    cst = np.zeros((128, CW), np.float32)
    i = np.arange(128)
    cst[:, CO_ID:CO_ID + 128] = np.eye(128, dtype=np.float32)
    rmp = np.ones(512, np.float32); rmp[0::128] = 0.0
    rms = np.ones(128, np.float32); rms[0::LS] = 0.0
    cst[:, CO_RMP:CO_RMP + 512] = rmp[None, :]
    cst[:, CO_RMS:CO_RMS + 128] = rms[None, :]
    cst[:, CO_SEL:CO_SEL + 32] = (i[:, None] // LS == np.arange(32)[None, :]).astype(np.float32)
    return cst


def _constants2(w_gk_up):
    c2 = np.zeros((128, CW2), np.float32)
    i = np.arange(128)
    c2[:, C2_ONES:C2_ONES + 128] = 1.0 / 128.0
    c2[:, C2_MP:C2_MP + 128] = (i[None, :] >= i[:, None]).astype(np.float32)
    c2[:, C2_MS:C2_MS + 128] = ((i[None, :] >= i[:, None]) & (i[None, :] // LS == i[:, None] // LS)).astype(np.float32)
    c2[0:16, C2_WGK:C2_WGK + 256] = w_gk_up
    return c2


_CACHE = {}


def kernel(x_prompt, x_sample, state_gla, state_conv, norm_gain, w_in, w_gk_up, b_gk,
           gla_norm_gain, conv_w, w_out, final_norm_gain):
    f = lambda a: np.ascontiguousarray(np.asarray(a), dtype=np.float32)
    x_prompt, x_sample, state_gla, state_conv = f(x_prompt), f(x_sample), f(state_gla), f(state_conv)
    norm_gain, w_in, w_gk_up, b_gk = f(norm_gain), f(w_in), f(w_gk_up), f(b_gk)
    gla_norm_gain, conv_w, w_out, final_norm_gain = f(gla_norm_gain), f(conv_w), f(w_out), f(final_norm_gain)

    if "nc" not in _CACHE:
        _CACHE["nc"] = build_program()
    nc = _CACHE["nc"]

    cm = _colmap()
    w_in_p = np.concatenate([w_in[0], np.zeros((D, 1), np.float32)], axis=1)[:, cm]
    w_in_l = np.ascontiguousarray(w_in_p.reshape(KC, 128, NCH, 128).transpose(1, 2, 0, 3)).reshape(128, NCH, KC * 128)
    w_v_l = np.ascontiguousarray(w_in[0][:, 512:1024].reshape(KC, 128, 512).transpose(1, 0, 2)).reshape(128, KC * 512)
    w_out_l = np.ascontiguousarray(w_out[0].reshape(KC, 128, 2, 512).transpose(1, 2, 0, 3)).reshape(128, 2, KC * 512)
    prm = np.zeros((128, PW), np.float32)
    prm[:, PO_G:PO_G + 8] = norm_gain[0].reshape(KC, 128).T
    prm[:, PO_BGK:PO_BGK + 2] = b_gk[0].reshape(2, 128).T
    prm[:, PO_GG] = gla_norm_gain[0]
    prm[:, PO_CW:PO_CW + 12] = conv_w[0].reshape(3, 4, 128).transpose(2, 1, 0).reshape(128, 12)
    fg = np.ascontiguousarray(np.broadcast_to(final_norm_gain[None, :], (128, D)))
    cst = _constants()
    cst2 = _constants2(w_gk_up[0])

    in_maps = []
    for c in range(NCORES):
        xs = x_sample[c * NSEQ_S:(c + 1) * NSEQ_S].reshape(NSEQ_S * LS, D)
        x_all = np.concatenate([x_prompt[c], xs, np.zeros((128 - NSEQ_S * LS, D), np.float32)], axis=0)
        in_maps.append({
            "x_all": np.ascontiguousarray(x_all),
            "gla_in": np.ascontiguousarray(state_gla[0, c * NSEQ_S:(c + 1) * NSEQ_S]),
            "conv_in": np.ascontiguousarray(state_conv[0, c * NSEQ_S:(c + 1) * NSEQ_S].reshape(NSEQ_S * 2, 512)),
            "w_in_l": w_in_l, "w_v_l": w_v_l, "w_out_l": w_out_l, "cst": cst, "cst2": cst2, "prm": prm, "fg": fg,
        })
    res = run_bass_kernel_spmd(nc, in_maps, core_ids=list(range(NCORES)))
    R = res.results
    y_prompt = np.stack([R[c]["y"][:SEQ] for c in range(NCORES)], axis=0)
    y_sample = np.concatenate([R[c]["y"][SEQ:].reshape(NSEQ_S, LS, D) for c in range(NCORES)], axis=0)
    gla_p = np.stack([R[c]["gla_p"] for c in range(NCORES)], axis=0)[None]
    conv_p = np.stack([R[c]["conv_p"] for c in range(NCORES)], axis=0)[None]
    gla_s = np.concatenate([R[c]["gla_s"] for c in range(NCORES)], axis=0)[None]
    conv_s = np.concatenate([R[c]["conv_s"].reshape(NSEQ_S, 2, 512) for c in range(NCORES)], axis=0)[None]
    return (y_prompt.astype(np.float32), y_sample.astype(np.float32), gla_p.astype(np.float32),
            conv_p.astype(np.float32), gla_s.astype(np.float32), conv_s.astype(np.float32))
```
